# Optimizing a Trainium2 kernel written in Bass

```python
import jax, jax.numpy as jnp
from jax import lax
import numpy as np

D_MODEL = 2048
BATCH = 4
SEQ = 4096
DEPTH = 2

N_META = 16
HEAD_DIM = 128
N_HEADS_FOX = D_MODEL // (2 * HEAD_DIM)
N_HEADS_SB = D_MODEL // (2 * HEAD_DIM)
D_FOX = N_HEADS_FOX * HEAD_DIM
D_SB = N_HEADS_SB * HEAD_DIM
D_MIX = D_FOX + D_SB
D_IN = 3 * D_FOX + 3 * D_SB + N_HEADS_FOX
D_FF = 256 * ((8 * D_MODEL // 3 + 255) // 256)
QB = 128
EPS = 1e-6
SPLITS = [D_FOX, 2 * D_FOX, 3 * D_FOX, 3 * D_FOX + D_SB, 3 * D_FOX + 2 * D_SB, 3 * D_FOX + 3 * D_SB]

kernel_name = "hymba_fox_stickbreaking_macaron"


def rms_norm(x, g):
    xf = x.astype(jnp.float32)
    y = xf * lax.rsqrt(jnp.mean(xf * xf, axis=-1, keepdims=True) + EPS)
    return (y * g.astype(jnp.float32)).astype(x.dtype)


def swiglu(x, w_gate, w_up, w_down):
    return (jax.nn.silu(x @ w_gate) * (x @ w_up)) @ w_down


def to_heads(t, n_heads, pad):
    B, L, _ = t.shape
    t = t.reshape(B, L, n_heads, HEAD_DIM).transpose(0, 2, 1, 3)
    return jnp.pad(t, ((0, 0), (0, 0), (pad, 0), (0, 0)))


def to_blocks(t):
    B, H, Lp = t.shape[:3]
    t = t.reshape((B, H, Lp // QB, QB) + t.shape[3:])
    return jnp.moveaxis(t, 2, 0)


def from_blocks(o):
    nb, B, H, qb, dh = o.shape
    return o.transpose(1, 2, 0, 3, 4).reshape(B, H, nb * qb, dh)


def forgetting_attention(q, k, v, log_f, valid):
    Lp, dh = q.shape[2], q.shape[3]
    c = jnp.cumsum(log_f, axis=-1)
    key_pos = jnp.arange(Lp)
    scale = dh ** -0.5
    starts = jnp.arange(Lp // QB) * QB

    def one_block(args):
        qb, cb, start = args
        q_pos = start + jnp.arange(QB)
        s = jnp.einsum('bhqd,bhkd->bhqk', qb, k).astype(jnp.float32) * scale
        s = s + cb[..., None] - c[:, :, None, :]
        diag = key_pos[None, :] == q_pos[:, None]
        allowed = (key_pos[None, :] <= q_pos[:, None]) & (valid[None, :] | diag)
        p = jax.nn.softmax(jnp.where(allowed, s, -jnp.inf), axis=-1)
        return jnp.einsum('bhqk,bhkd->bhqd', p.astype(v.dtype), v)

    return from_blocks(lax.map(one_block, (to_blocks(q), to_blocks(c), starts)))


def stick_breaking_attention(q, k, v, valid):
    Lp, dh = q.shape[2], q.shape[3]
    key_pos = jnp.arange(Lp)
    scale = dh ** -0.5
    starts = jnp.arange(Lp // QB) * QB

    def one_block(args):
        qb, start = args
        q_pos = start + jnp.arange(QB)
        z = jnp.einsum('bhqd,bhkd->bhqk', qb, k).astype(jnp.float32) * scale
        allowed = (key_pos[None, :] < q_pos[:, None]) & valid[None, :]
        log_1m = jnp.where(allowed, jax.nn.log_sigmoid(-z), 0.0)
        later = lax.cumsum(log_1m, axis=3, reverse=True) - log_1m
        a = jnp.where(allowed, jnp.exp(jax.nn.log_sigmoid(z) + later), 0.0)
        return jnp.einsum('bhqk,bhkd->bhqd', a.astype(v.dtype), v)

    return from_blocks(lax.map(one_block, (to_blocks(q), starts)))


def head_group_norm(o, g, pad):
    o = o[:, :, pad:]
    of = o.astype(jnp.float32)
    of = of * lax.rsqrt(jnp.mean(of * of, axis=-1, keepdims=True) + EPS)
    B, H, L, dh = o.shape
    of = of.transpose(0, 2, 1, 3).reshape(B, L, H * dh)
    return (of * g.astype(jnp.float32)).astype(o.dtype)


def hybrid_mixer(xn, w_in, b_forget, g_fox, g_sb, w_out):
    B, L, _ = xn.shape
    proj = xn @ w_in
    q_f, k_f, v_f, q_s, k_s, v_s, f_logit = jnp.split(proj, SPLITS, axis=-1)
    pad = (-L) % QB
    valid = jnp.arange(L + pad) >= pad
    log_f = jax.nn.log_sigmoid((f_logit + b_forget).astype(jnp.float32)).transpose(0, 2, 1)
    log_f = jnp.pad(log_f, ((0, 0), (0, 0), (pad, 0)))
    o_f = forgetting_attention(to_heads(q_f, N_HEADS_FOX, pad), to_heads(k_f, N_HEADS_FOX, pad),
                               to_heads(v_f, N_HEADS_FOX, pad), log_f, valid)
    o_s = stick_breaking_attention(to_heads(q_s, N_HEADS_SB, pad), to_heads(k_s, N_HEADS_SB, pad),
                                   to_heads(v_s, N_HEADS_SB, pad), valid)
    o = jnp.concatenate([head_group_norm(o_f, g_fox, pad), head_group_norm(o_s, g_sb, pad)], axis=-1)
    return o @ w_out


def setup_inputs(seed: int = 0) -> dict:
    key = jax.random.key(seed)
    ks = jax.random.split(key, 20)
    f32 = jnp.float32
    nrm = lambda k, shape, scale: jax.random.normal(k, shape, f32) * scale
    gain = lambda k, shape: 1.0 + 0.02 * jax.random.normal(k, shape, f32)
    return {
        "x": jax.random.normal(ks[0], (BATCH, SEQ, D_MODEL), f32),
        "meta_tokens": nrm(ks[1], (N_META, D_MODEL), 1.0),
        "ffn1_norm": gain(ks[2], (DEPTH, D_MODEL)),
        "ffn1_w_gate": nrm(ks[3], (DEPTH, D_MODEL, D_FF), D_MODEL ** -0.5),
        "ffn1_w_up": nrm(ks[4], (DEPTH, D_MODEL, D_FF), D_MODEL ** -0.5),
        "ffn1_w_down": nrm(ks[5], (DEPTH, D_FF, D_MODEL), D_FF ** -0.5),
        "mix_norm": gain(ks[6], (DEPTH, D_MODEL)),
        "w_in": nrm(ks[7], (DEPTH, D_MODEL, D_IN), D_MODEL ** -0.5),
        "b_forget": jax.random.uniform(ks[8], (DEPTH, N_HEADS_FOX), f32, 1.0, 5.0),
        "g_fox": gain(ks[9], (DEPTH, D_FOX)),
        "g_sb": gain(ks[10], (DEPTH, D_SB)),
        "w_out": nrm(ks[11], (DEPTH, D_MIX, D_MODEL), D_MIX ** -0.5),
        "ffn2_norm": gain(ks[12], (DEPTH, D_MODEL)),
        "ffn2_w_gate": nrm(ks[13], (DEPTH, D_MODEL, D_FF), D_MODEL ** -0.5),
        "ffn2_w_up": nrm(ks[14], (DEPTH, D_MODEL, D_FF), D_MODEL ** -0.5),
        "ffn2_w_down": nrm(ks[15], (DEPTH, D_FF, D_MODEL), D_FF ** -0.5),
        "final_norm": gain(ks[16], (D_MODEL,)),
    }


def reference(x, meta_tokens, ffn1_norm, ffn1_w_gate, ffn1_w_up, ffn1_w_down, mix_norm, w_in,
              b_forget, g_fox, g_sb, w_out, ffn2_norm, ffn2_w_gate, ffn2_w_up, ffn2_w_down,
              final_norm):
    B = x.shape[0]
    meta = jnp.broadcast_to(meta_tokens[None].astype(x.dtype), (B, N_META, x.shape[-1]))
    h = jnp.concatenate([meta, x], axis=1)
    for l in range(DEPTH):
        h = h + 0.5 * swiglu(rms_norm(h, ffn1_norm[l]), ffn1_w_gate[l], ffn1_w_up[l], ffn1_w_down[l])
        h = h + hybrid_mixer(rms_norm(h, mix_norm[l]), w_in[l], b_forget[l], g_fox[l], g_sb[l], w_out[l])
        h = h + 0.5 * swiglu(rms_norm(h, ffn2_norm[l]), ffn2_w_gate[l], ffn2_w_up[l], ffn2_w_down[l])
    return rms_norm(h, final_norm)[:, N_META:]
```

```python
import numpy as np
from contextlib import ExitStack
import concourse.bass as bass
import concourse.mybir as mybir
from concourse.bass_utils import run_bass_kernel_spmd

F32 = mybir.dt.float32
BF16 = mybir.dt.bfloat16
AF = mybir.ActivationFunctionType
ALU = mybir.AluOpType

D = 2048
NM = 16
SEQ = 4096
T = SEQ + NM
DFF = 5632
NF = DFF // 128
KC = D // 128
NH = 8
NB = 33
DEPTH = 2
EPS = 1e-6
NEG = -30000.0

GROUPS = [(0, NM)] + [(NM + 512 * i, NM + 512 * (i + 1)) for i in range(8)]
SGS = [[0, 1, 2], [3, 4], [5, 6], [7, 8]]


def blk(j):
    return (0, NM) if j == 0 else (NM + 128 * (j - 1), NM + 128 * j)


def grp_blocks(g):
    return (0, 0) if g == 0 else (4 * (g - 1) + 1, 4 * g)


class Buf:
    __slots__ = ("name", "w", "r")

    def __init__(self, name):
        self.name = name
        self.w = None
        self.r = {}


class Eng:
    def __init__(self, nc, es, name, e):
        self.name = name
        self.e = e
        self.sem = es.enter_context(nc.semaphore("s_" + name))
        self.key = "E_" + name
        self.cnt = 0
        self.waited = {}

    def wait(self, tok):
        sem, val, key = tok
        if self.waited.get(key, 0) >= val:
            return
        self.e.wait_ge(sem, val)
        self.waited[key] = val


class DSem:
    def __init__(self, nc, es, name):
        self.sem = es.enter_context(nc.semaphore("d_" + name))
        self.key = "D_" + name
        self.cnt = 0


class Ctx:
    def __init__(self, nc, es):
        self.nc = nc
        self.es = es
        self.pe = Eng(nc, es, "pe", nc.tensor)
        self.act = Eng(nc, es, "act", nc.scalar)
        self.dve = Eng(nc, es, "dve", nc.vector)
        self.pool = Eng(nc, es, "pool", nc.gpsimd)
        self.sp = Eng(nc, es, "sp", nc.sync)
        self.engs = [self.pe, self.act, self.dve, self.pool, self.sp]
        self.dsems = []
        self.dcache = {}

    def dsem(self, name):
        if name in self.dcache:
            return self.dcache[name]
        d = DSem(self.nc, self.es, name)
        self.dsems.append(d)
        self.dcache[name] = d
        return d

    def _deps(self, E, reads, writes):
        for b in reads:
            if b.w is not None:
                if b.w[2] == E.key and E is self.pe:
                    continue
                E.wait(b.w)
        for b in writes:
            if b.w is not None and b.w[2] != E.key:
                E.wait(b.w)
            for tok in b.r.values():
                if tok[2] != E.key:
                    E.wait(tok)

    def _record(self, tok, reads, writes):
        for b in reads:
            old = b.r.get(tok[2])
            if old is None or old[1] < tok[1]:
                b.r[tok[2]] = tok
        for b in writes:
            b.w = tok
            b.r = {}

    def op(self, E, fn, reads=(), writes=(), sig=True):
        self._deps(E, reads, writes)
        ins = fn()
        if sig:
            E.cnt += 1
            ins.then_inc(E.sem, 1)
            tok = (E.sem, E.cnt, E.key)
        else:
            tok = (E.sem, E.cnt + 1, E.key)
        self._record(tok, reads, writes)
        return ins

    def dma(self, Q, ds, out, in_, reads=(), writes=(), **kw):
        self._deps(Q, reads, writes)
        ins = Q.e.dma_start(out=out, in_=in_, **kw)
        ds.cnt += 16
        ins.then_inc(ds.sem, 16)
        tok = (ds.sem, ds.cnt, ds.key)
        self._record(tok, reads, writes)
        return ins

    def barrier(self):
        for E in self.engs:
            for O in self.engs:
                if O is not E and O.cnt > 0:
                    E.wait((O.sem, O.cnt, O.key))
            for d in self.dsems:
                if d.cnt > 0:
                    E.wait((d.sem, d.cnt, d.key))


class Pool:
    def __init__(self, ctx, es, name, n, shape, dtype, psum=False, dma=False):
        nc = ctx.nc
        self.t = []
        self.b = []
        self.d = []
        for i in range(n):
            nm = f"{name}{i}"
            if psum:
                t = es.enter_context(nc.psum_tensor(nm, shape, dtype))
            else:
                t = es.enter_context(nc.sbuf_tensor(nm, shape, dtype))
            self.t.append(t)
            self.b.append(Buf(nm))
        self.n = n
        self.i = -1

    def next(self):
        self.i = (self.i + 1) % self.n
        return self.t[self.i], self.b[self.i], self.i


def build_program(dbg=None):
    nc = bass.Bass("TRN2", target_bir_lowering=False)
    dt = nc.dram_tensor
    h_in = dt("h_in", [D, T], F32, kind="ExternalInput").ap()
    wgu_d = dt("wgu", [DEPTH * 2 * 2 * NF, 128, D], F32, kind="ExternalInput").ap()
    wdn_d = dt("wdn", [DEPTH * 2 * KC, 128, DFF], F32, kind="ExternalInput").ap()
    wqk_d = dt("wqk", [DEPTH * 32, 128, D], F32, kind="ExternalInput").ap()
    wv_d = dt("wv", [DEPTH * 4, 128, KC * 512], F32, kind="ExternalInput").ap()
    wf_d = dt("wf", [DEPTH, 128, KC * 8], F32, kind="ExternalInput").ap()
    wo_d = dt("wo", [DEPTH * KC, 128, D], F32, kind="ExternalInput").ap()
    NCV = DEPTH * (4 * KC + 8) + KC
    cvec_d = dt("cvec", [128, NCV], F32, kind="ExternalInput").ap()
    cf32_d = dt("cf32", [128, 3 * 128], F32, kind="ExternalInput").ap()
    cbf_d = dt("cbf", [128, 15 * 128], F32, kind="ExternalInput").ap()
    out_d = dt("outT", [D, SEQ], F32, kind="ExternalOutput").ap()
    H = dt("Hs", [D, T], F32, kind="Internal").ap()
    QKT = dt("QKTs", [32 * 128, T], BF16, kind="Internal").ap()
    V = dt("Vs", [T, 2048], BF16, kind="Internal").ap()
    ONT = dt("ONTs", [D, T], BF16, kind="Internal").ap()

    with ExitStack() as es:
        ctx = Ctx(nc, es)
        PE, ACT, DVE, POOL, SP = ctx.pe, ctx.act, ctx.dve, ctx.pool, ctx.sp
        sb = lambda name, shape, dtp: es.enter_context(nc.sbuf_tensor(name, shape, dtp))
        _uqc = [0]

        def uq(n):
            _uqc[0] += 1
            return f"{n}_{_uqc[0]}_"

        cvec = sb("cvec_t", [128, NCV], F32)
        cf32 = sb("cf32_t", [128, 3 * 128], F32)
        cbf = sb("cbf_t", [128, 15 * 128], BF16)
        ones3 = sb("ones3_t", [3, 128], BF16)
        lfseq = sb("lfseq_t", [128, NB * 8], F32)
        b_const = Buf("const")
        b_lf = Buf("lfseq")
        dconst = ctx.dsem("const")
        ctx.dma(SP, dconst, cvec[:], cvec_d[:, :], writes=[b_const])
        ctx.dma(SP, dconst, cf32[:], cf32_d[:, :], writes=[b_const])
        dconst2 = ctx.dsem("const2")
        ctx.dma(POOL, dconst2, cbf[:], cbf_d[:, :], writes=[b_const])
        ctx.dma(POOL, dconst2, ones3[:], cbf_d[0:3, 0:128], writes=[b_const])
        ctx.barrier()
        ones_f = cf32[:, 0:128]
        triu_f = cf32[:, 128:256]
        ident_f = cf32[:, 256:384]
        ones_b = cbf[:, 0:128]
        negones_b = cbf[:, 128:256]
        negtri_b = cbf[:, 256:384]
        ident_b = cbf[:, 384:512]
        maskF_b = cbf[:, 512:640]
        maskS_b = cbf[:, 640:768]
        zeros_b = cbf[:, 768:896]
        sel24 = cbf[:, 896:1920]

        def cv(l, which, k):
            base = l * (4 * KC + 8) + which * KC + k
            return cvec[:, base:base + 1]

        def cv_bf(l):
            base = l * (4 * KC + 8) + 4 * KC
            return cvec[:, base:base + 8]

        def cv_final(k):
            base = DEPTH * (4 * KC + 8) + k
            return cvec[:, base:base + 1]

        def ffn_phase(l, which, src, dst):
            wrow_gu = ((l * 2 + which) * 2) * NF
            wrow_dn = (l * 2 + which) * KC
            with ExitStack() as scope:
                xn = scope.enter_context(nc.sbuf_tensor(uq("f_xn"), [128, KC, 1040], BF16))
                xn_b = Buf("f_xn")
                A = scope.enter_context(nc.sbuf_tensor(uq("f_A"), [128, NF, 1040], BF16))
                A_b = [Buf(f"A{f}") for f in range(NF)]
                wg = Pool(ctx, scope, uq("f_wg"), 2, [128, D], BF16)
                wu = Pool(ctx, scope, uq("f_wu"), 2, [128, D], BF16)
                wgd = [ctx.dsem(f"wg{i}") for i in range(2)]
                wd = Pool(ctx, scope, uq("f_wd"), 2, [128, DFF], BF16)
                wdd = [ctx.dsem(f"wd{i}") for i in range(2)]
                sg = Pool(ctx, scope, uq("f_sg"), 2, [128, 512], F32)
                hr = Pool(ctx, scope, uq("f_hr"), 2, [128, 512], F32)
                hrd = [ctx.dsem(f"hr{i}") for i in range(2)]
                ho = Pool(ctx, scope, uq("f_ho"), 2, [128, 512], F32)
                hod = [ctx.dsem(f"ho{i}") for i in range(2)]
                pg = Pool(ctx, scope, uq("f_pg"), 2, [128, 512], F32, psum=True)
                pu = Pool(ctx, scope, uq("f_pu"), 2, [128, 512], F32, psum=True)
                py = Pool(ctx, scope, uq("f_py"), 2, [128, 512], F32, psum=True)
                pss = Pool(ctx, scope, uq("f_pss"), 2, [128, 512], F32, psum=True)
                for sgl in SGS:
                    norm_phase(scope, src, sgl, lambda k: cv(l, 0 if which == 0 else 2, k),
                               xn=xn, xn_b=xn_b, ps_ss=pss, NP=128)
                    offs = []
                    o = 0
                    for gi in sgl:
                        offs.append(o)
                        o += GROUPS[gi][1] - GROUPS[gi][0]
                    for f in range(NF):
                        wgt, wgb, wi = wg.next()
                        wut, wub, _ = wu.next()
                        ctx.dma(POOL, wgd[wi], wgt[:], wgu_d[wrow_gu + f], writes=[wgb], max_dma_last_dim=4096)
                        ctx.dma(POOL, wgd[wi], wut[:], wgu_d[wrow_gu + NF + f], writes=[wub], max_dma_last_dim=4096)
                        for gi, off in zip(sgl, offs):
                            N = GROUPS[gi][1] - GROUPS[gi][0]
                            gt, gb, _ = pg.next()
                            ut, ub, _ = pu.next()
                            for k in range(KC):
                                ctx.op(PE, lambda: nc.tensor.matmul(gt[:, :N], wgt[:, k * 128:(k + 1) * 128], xn[:, k, off:off + N],
                                                                    start=(k == 0), stop=(k == KC - 1)),
                                       reads=[wgb, xn_b], writes=[gb], sig=(k == KC - 1))
                            for k in range(KC):
                                ctx.op(PE, lambda: nc.tensor.matmul(ut[:, :N], wut[:, k * 128:(k + 1) * 128], xn[:, k, off:off + N],
                                                                    start=(k == 0), stop=(k == KC - 1)),
                                       reads=[wub, xn_b], writes=[ub], sig=(k == KC - 1))
                            st, sbf, _ = sg.next()
                            ctx.op(ACT, lambda: nc.scalar.activation(out=st[:, :N], in_=gt[:, :N], func=AF.Silu),
                                   reads=[gb], writes=[sbf])
                            ctx.op(DVE, lambda: nc.vector.tensor_tensor(out=A[:, f, off:off + N], in0=st[:, :N], in1=ut[:, :N], op=ALU.mult),
                                   reads=[sbf, ub], writes=[A_b[f]])
                    for dc in range(KC):
                        wdt, wdb, wi = wd.next()
                        ctx.dma(POOL, wdd[wi], wdt[:], wdn_d[wrow_dn + dc], writes=[wdb], max_dma_last_dim=4096)
                        for gi, off in zip(sgl, offs):
                            c0, c1 = GROUPS[gi]
                            N = c1 - c0
                            hrt, hrb, hri = hr.next()
                            ctx.dma(SP, hrd[hri], hrt[:, :N], src[dc * 128:(dc + 1) * 128, c0:c1], writes=[hrb])
                            yt, yb, _ = py.next()
                            for f in range(NF):
                                ctx.op(PE, lambda: nc.tensor.matmul(yt[:, :N], wdt[:, f * 128:(f + 1) * 128], A[:, f, off:off + N],
                                                                    start=(f == 0), stop=(f == NF - 1)),
                                       reads=[wdb, A_b[f]], writes=[yb], sig=(f == NF - 1))
                            hot, hob, hoi = ho.next()
                            ctx.op(DVE, lambda: nc.vector.scalar_tensor_tensor(out=hot[:, :N], in0=yt[:, :N], scalar=0.5, in1=hrt[:, :N],
                                                                               op0=ALU.mult, op1=ALU.add),
                                   reads=[yb, hrb], writes=[hob])
                            ctx.dma(SP, hod[hoi], dst[dc * 128:(dc + 1) * 128, c0:c1], hot[:, :N], reads=[hob])
                ctx.barrier()

        _norm_cache = {}

        def norm_phase(scope, src, groups, gain_fn, xn=None, xn_b=None, xoff0=0, out_dst=None, ps_ss=None, NP=256):
            key = id(scope)
            if key not in _norm_cache:
                c = {}
                c["hst"] = Pool(ctx, scope, uq("hst"), 2, [128, KC, NP], F32)
                c["sq"] = Pool(ctx, scope, uq("nsq"), 2, [128, NP], F32)
                c["lnb"] = Pool(ctx, scope, uq("nln"), 2, [128, NP], F32)
                c["rsb"] = Pool(ctx, scope, uq("nrs"), 2, [128, NP], F32)
                if out_dst is not None:
                    c["ost"] = Pool(ctx, scope, uq("nost"), 3, [128, NP], F32)
                _norm_cache[key] = c
            c = _norm_cache[key]
            hst, sq, lnb, rsb = c["hst"], c["sq"], c["lnb"], c["rsb"]
            ost = c.get("ost")
            xoff = xoff0
            for gi in groups:
                c0, c1 = GROUPS[gi]
                for p0 in range(c0, c1, NP):
                    p1 = min(c1, p0 + NP)
                    N = p1 - p0
                    ht, hb, hi = hst.next()
                    ctx.dma(SP, g_hst_d[hi], ht[:, :, :N],
                            src[:, p0:p1].rearrange("(k p) n -> p k n", p=128), writes=[hb])
                    st, sbf, _ = ps_ss.next()
                    for k in range(KC):
                        qt, qb, _ = sq.next()
                        ctx.op(ACT, lambda: nc.scalar.activation(out=qt[:, :N], in_=ht[:, k, :N], func=AF.Square),
                               reads=[hb], writes=[qb])
                        ctx.op(PE, lambda: nc.tensor.matmul(st[:, :N], ones_f, qt[:, :N], start=(k == 0), stop=(k == KC - 1)),
                               reads=[qb], writes=[sbf], sig=True)
                    lt, lb, _ = lnb.next()
                    ctx.op(ACT, lambda: nc.scalar.activation(out=lt[:, :N], in_=st[:, :N], func=AF.Ln, bias=EPS, scale=1.0 / D),
                           reads=[sbf], writes=[lb])
                    rt, rb, _ = rsb.next()
                    ctx.op(ACT, lambda: nc.scalar.activation(out=rt[:, :N], in_=lt[:, :N], func=AF.Exp, scale=-0.5),
                           reads=[lb], writes=[rb])
                    for k in range(KC):
                        if out_dst is None:
                            ctx.op(DVE, lambda: nc.vector.scalar_tensor_tensor(
                                out=xn[:, k, xoff:xoff + N], in0=ht[:, k, :N], scalar=gain_fn(k), in1=rt[:, :N],
                                op0=ALU.mult, op1=ALU.mult), reads=[hb, rb], writes=[xn_b])
                        else:
                            ot, ob, oi = ost.next()
                            ctx.op(DVE, lambda: nc.vector.scalar_tensor_tensor(
                                out=ot[:, :N], in0=ht[:, k, :N], scalar=gain_fn(k), in1=rt[:, :N],
                                op0=ALU.mult, op1=ALU.mult), reads=[hb, rb], writes=[ob])
                            ctx.dma(SP, g_ost_d[oi], out_dst[k * 128:(k + 1) * 128, p0 - NM:p1 - NM], ot[:, :N], reads=[ob])
                    xoff += N

        g_hst_d = [ctx.dsem(f"ghst{i}") for i in range(2)]
        g_ost_d = [ctx.dsem(f"gost{i}") for i in range(3)]

        def proj_phase(l):
            with ExitStack() as scope:
                xn = scope.enter_context(nc.sbuf_tensor(uq("p_xn"), [128, KC, T], BF16))
                xn_b = Buf("p_xn")
                pss = Pool(ctx, scope, uq("p_pss"), 2, [128, 512], F32, psum=True)
                pq = Pool(ctx, scope, uq("p_pq"), 3, [128, 512], F32, psum=True)
                with ExitStack() as nscope:
                    norm_phase(nscope, H, list(range(9)), lambda k: cv(l, 1, k), xn=xn, xn_b=xn_b, ps_ss=pss)
                    ctx.barrier()
                wq = Pool(ctx, scope, uq("p_wq"), 2, [128, D], BF16)
                wqd = [ctx.dsem(f"pwq{i}") for i in range(2)]
                stg = Pool(ctx, scope, uq("p_stg"), 3, [128, 512], BF16)
                stgd = [ctx.dsem(f"pstg{i}") for i in range(3)]
                for cc in range(32):
                    wt, wb, wi = wq.next()
                    ctx.dma(POOL, wqd[wi], wt[:], wqk_d[l * 32 + cc], writes=[wb], max_dma_last_dim=4096)
                    is_q = (cc // 8) in (0, 2)
                    for gi in range(9):
                        c0, c1 = GROUPS[gi]
                        N = c1 - c0
                        pt, pb, _ = pq.next()
                        for k in range(KC):
                            ctx.op(PE, lambda: nc.tensor.matmul(pt[:, :N], wt[:, k * 128:(k + 1) * 128], xn[:, k, c0:c1],
                                                                start=(k == 0), stop=(k == KC - 1)),
                                   reads=[wb, xn_b], writes=[pb], sig=(k == KC - 1))
                        st, sbf, si = stg.next()
                        scale = (128.0 ** -0.5) if is_q else 1.0
                        if (gi + cc) % 2 == 0:
                            ctx.op(ACT, lambda: nc.scalar.activation(out=st[:, :N], in_=pt[:, :N], func=AF.Copy, scale=scale),
                                   reads=[pb], writes=[sbf])
                        else:
                            ctx.op(DVE, lambda: nc.vector.tensor_scalar(out=st[:, :N], in0=pt[:, :N], scalar1=scale, scalar2=None,
                                                                        op0=ALU.mult), reads=[pb], writes=[sbf])
                        ctx.dma(SP, stgd[si], QKT[cc * 128:(cc + 1) * 128, c0:c1], st[:, :N], reads=[sbf])
                wv = Pool(ctx, scope, uq("p_wv"), 2, [128, KC * 512], BF16)
                wvd = [ctx.dsem(f"pwv{i}") for i in range(2)]
                for vg in range(4):
                    wt, wb, wi = wv.next()
                    ctx.dma(POOL, wvd[wi], wt[:], wv_d[l * 4 + vg], writes=[wb], max_dma_last_dim=4096)
                    for j in range(NB):
                        t0, t1 = blk(j)
                        kk = t1 - t0
                        pt, pb, _ = pq.next()
                        for k in range(KC):
                            ctx.op(PE, lambda: nc.tensor.matmul(pt[:kk, :512], xn[:, k, t0:t1], wt[:, k * 512:(k + 1) * 512],
                                                                start=(k == 0), stop=(k == KC - 1)),
                                   reads=[wb, xn_b], writes=[pb], sig=(k == KC - 1))
                        st, sbf, si = stg.next()
                        if j % 2 == 0:
                            ctx.op(ACT, lambda: nc.scalar.activation(out=st[:kk, :512], in_=pt[:kk, :512], func=AF.Copy),
                                   reads=[pb], writes=[sbf])
                        else:
                            ctx.op(DVE, lambda: nc.vector.tensor_copy(out=st[:kk, :512], in_=pt[:kk, :512]), reads=[pb], writes=[sbf])
                        ctx.dma(SP, stgd[si], V[t0:t1, vg * 512:(vg + 1) * 512], st[:kk, :512], reads=[sbf])
                wf = scope.enter_context(nc.sbuf_tensor(uq("p_wf"), [128, KC * 8], BF16))
                wfb = Buf("p_wf")
                wfd = ctx.dsem("pwf")
                ctx.dma(POOL, wfd, wf[:], wf_d[l], writes=[wfb])
                ft = Pool(ctx, scope, uq("p_ft"), 2, [128, 8], F32)
                ctx.op(DVE, lambda: nc.vector.memset(lfseq[:], 0.0), writes=[b_lf])
                for j in range(NB):
                    t0, t1 = blk(j)
                    kk = t1 - t0
                    pt, pb, _ = pq.next()
                    for k in range(KC):
                        ctx.op(PE, lambda: nc.tensor.matmul(pt[:kk, :8], xn[:, k, t0:t1], wf[:, k * 8:(k + 1) * 8],
                                                            start=(k == 0), stop=(k == KC - 1)),
                               reads=[wfb, xn_b], writes=[pb], sig=(k == KC - 1))
                    f1, f1b, _ = ft.next()
                    ctx.op(DVE, lambda: nc.vector.tensor_tensor(out=f1[:kk, :], in0=pt[:kk, :8], in1=cv_bf(l)[:kk, :], op=ALU.add),
                           reads=[pb], writes=[f1b])
                    f2, f2b, _ = ft.next()
                    ctx.op(ACT, lambda: nc.scalar.activation(out=f2[:kk, :], in_=f1[:kk, :], func=AF.Exp, scale=-1.0),
                           reads=[f1b], writes=[f2b])
                    f3, f3b, _ = ft.next()
                    ctx.op(ACT, lambda: nc.scalar.activation(out=f3[:kk, :], in_=f2[:kk, :], func=AF.Ln, bias=1.0, scale=1.0),
                           reads=[f2b], writes=[f3b])
                    ctx.op(DVE, lambda: nc.vector.tensor_scalar(out=lfseq[:kk, j * 8:(j + 1) * 8], in0=f3[:kk, :], scalar1=-1.0,
                                                                scalar2=None, op0=ALU.mult), reads=[f3b], writes=[b_lf])
                ctx.barrier()

        def attn_phase(l):
            with ExitStack() as scope:
                bias_t = scope.enter_context(nc.sbuf_tensor(uq("a_bias"), [128, 9 * NB * 8], F32))
                ch24 = scope.enter_context(nc.sbuf_tensor(uq("a_ch24"), [24, T], BF16))
                with ExitStack() as pscope:
                    pm = Pool(ctx, pscope, uq("a_pm"), 2, [128, 512], F32, psum=True)
                    ctok = pscope.enter_context(nc.sbuf_tensor(uq("a_ctok"), [128, NB * 8], F32))
                    tot = pscope.enter_context(nc.sbuf_tensor(uq("a_tot"), [128, NB * 8], F32))
                    pex = pscope.enter_context(nc.sbuf_tensor(uq("a_pex"), [128, NB * 8], F32))
                    chm = pscope.enter_context(nc.sbuf_tensor(uq("a_chm"), [8, T], F32))
                    r1 = pscope.enter_context(nc.sbuf_tensor(uq("a_r1"), [8, T], F32))
                    cb3 = [pscope.enter_context(nc.sbuf_tensor(uq(f"a_cb{i}"), [8, T], BF16)) for i in range(3)]
                    rhm = pscope.enter_context(nc.sbuf_tensor(uq("a_rhm"), [8, 18], F32))
                    b_ctok, b_tot, b_pex, b_bias, b_chm, b_r1, b_rhm, b_ch3 = (Buf(n) for n in
                                                                               ("ctok", "tot", "pex", "bias", "chm", "r1", "rhm", "ch3"))
                    b_cb3 = [Buf(f"cb{i}") for i in range(3)]
                    wt_, wb_, _ = pm.next()
                    ctx.op(PE, lambda: nc.tensor.matmul(wt_[:, :NB * 8], triu_f, lfseq[:, :], start=True, stop=True),
                           reads=[b_lf], writes=[wb_])
                    tt_, tb_, _ = pm.next()
                    ctx.op(PE, lambda: nc.tensor.matmul(tt_[:, :NB * 8], ones_f, lfseq[:, :], start=True, stop=True),
                           reads=[b_lf], writes=[tb_])
                    ctx.op(DVE, lambda: nc.vector.tensor_copy(out=tot[:, :], in_=tt_[:, :NB * 8]), reads=[tb_], writes=[b_tot])
                    ctx.op(DVE, lambda: nc.vector.memset(pex[:, 0:8], 0.0), writes=[b_pex])
                    for j in range(1, NB):
                        ctx.op(DVE, lambda: nc.vector.tensor_tensor(out=pex[:, j * 8:(j + 1) * 8], in0=pex[:, (j - 1) * 8:j * 8],
                                                                    in1=tot[:, (j - 1) * 8:j * 8], op=ALU.add),
                               reads=[b_tot, b_pex], writes=[b_pex])
                    ctx.op(DVE, lambda: nc.vector.tensor_tensor(out=ctok[:, :], in0=wt_[:, :NB * 8], in1=pex[:, :], op=ALU.add),
                           reads=[wb_, b_pex], writes=[b_ctok])
                    for g in range(9):
                        jb0, jb1 = grp_blocks(g)
                        for kb in range(jb1 + 1):
                            o = (g * NB + kb) * 8
                            ctx.op(DVE, lambda: nc.vector.tensor_tensor(out=bias_t[:, o:o + 8], in0=pex[:, jb0 * 8:(jb0 + 1) * 8],
                                                                        in1=ctok[:, kb * 8:(kb + 1) * 8], op=ALU.subtract),
                                   reads=[b_pex, b_ctok], writes=[b_bias])
                    rt_, rb_, _ = pm.next()
                    for g in range(9):
                        jb0, _ = grp_blocks(g)
                        ctx.op(PE, lambda: nc.tensor.matmul(rt_[0:8, 2 * g:2 * g + 2], pex[:, jb0 * 8:(jb0 + 1) * 8], ident_f[:, 0:2],
                                                            start=True, stop=True), reads=[b_pex], writes=[rb_])
                    ctx.op(DVE, lambda: nc.vector.tensor_copy(out=rhm[:, 0:18], in_=rt_[0:8, 0:18]), reads=[rb_], writes=[b_rhm])
                    for g in range(9):
                        c0, c1 = GROUPS[g]
                        jb0, jb1 = grp_blocks(g)
                        ct_, cbb_, _ = pm.next()
                        for j in range(jb0, jb1 + 1):
                            t0, t1 = blk(j)
                            kk = t1 - t0
                            ctx.op(PE, lambda: nc.tensor.matmul(ct_[0:8, t0 - c0:t1 - c0], ctok[:kk, j * 8:(j + 1) * 8], ident_f[:kk, :kk],
                                                                start=True, stop=True), reads=[b_ctok], writes=[cbb_])
                        rcol = rhm[:, 2 * g:2 * g + 1]
                        ctx.op(DVE, lambda: nc.vector.tensor_scalar(out=chm[:, c0:c1], in0=ct_[0:8, 0:c1 - c0], scalar1=rcol, scalar2=None,
                                                                    op0=ALU.subtract), reads=[cbb_, b_rhm], writes=[b_chm])
                    ctx.op(DVE, lambda: nc.vector.tensor_copy(out=cb3[0][:, :], in_=chm[:, :]), reads=[b_chm], writes=[b_cb3[0]])
                    ctx.op(DVE, lambda: nc.vector.tensor_tensor(out=r1[:, :], in0=chm[:, :], in1=cb3[0][:, :], op=ALU.subtract),
                           reads=[b_chm, b_cb3[0]], writes=[b_r1])
                    ctx.op(DVE, lambda: nc.vector.tensor_copy(out=cb3[1][:, :], in_=r1[:, :]), reads=[b_r1], writes=[b_cb3[1]])
                    ctx.op(DVE, lambda: nc.vector.tensor_tensor(out=chm[:, :], in0=r1[:, :], in1=cb3[1][:, :], op=ALU.subtract),
                           reads=[b_r1, b_cb3[1]], writes=[b_chm])
                    ctx.op(DVE, lambda: nc.vector.tensor_copy(out=cb3[2][:, :], in_=chm[:, :]), reads=[b_chm], writes=[b_cb3[2]])
                    dch = ctx.dsem(f"ch3_{l}")
                    for i in range(3):
                        for h in range(8):
                            ctx.dma(SP, dch, ch24[3 * h + i:3 * h + i + 1, :], cb3[i][h:h + 1, :], reads=[b_cb3[i]], writes=[b_ch3])

                    ctx.barrier()
                qT = Pool(ctx, scope, uq("a_qT"), 2, [128, T], BF16)
                kT = Pool(ctx, scope, uq("a_kT"), 2, [128, T], BF16)
                Vh = Pool(ctx, scope, uq("a_Vh"), 2, [128, NB, 128], BF16)
                hd = [ctx.dsem(f"ahd{i}") for i in range(2)]
                ps = Pool(ctx, scope, uq("a_ps"), 3, [128, 512], F32, psum=True)
                po = Pool(ctx, scope, uq("a_po"), 2, [128, 512], F32, psum=True)
                pdn = Pool(ctx, scope, uq("a_pd"), 1, [128, 512], F32, psum=True)
                pT = Pool(ctx, scope, uq("a_pT"), 3, [128, 512], BF16)
                eb = Pool(ctx, scope, uq("a_eb"), 2, [128, 512], F32)
                spb = Pool(ctx, scope, uq("a_sp"), 3, [128, 512], BF16)
                rs = Pool(ctx, scope, uq("a_rs"), 2, [128, 512], BF16)
                usq = Pool(ctx, scope, uq("a_usq"), 2, [128, 512], F32)
                t1p = Pool(ctx, scope, uq("a_t1"), 2, [128, 512], F32)
                t2p = Pool(ctx, scope, uq("a_t2"), 2, [128, 512], F32)
                rrp = Pool(ctx, scope, uq("a_rr"), 2, [128, 512], F32)
                ost = Pool(ctx, scope, uq("a_ost"), 2, [128, 512], BF16)
                ostd = [ctx.dsem(f"aost{i}") for i in range(2)]

                def load_head(cq, ck, vcol):
                    qt, qb, hi = qT.next()
                    kt, kb_, _ = kT.next()
                    vt, vb, _ = Vh.next()
                    ctx.dma(SP, hd[hi], qt[:, :], QKT[cq * 128:(cq + 1) * 128, :], writes=[qb])
                    ctx.dma(SP, hd[hi], kt[:, :], QKT[ck * 128:(ck + 1) * 128, :], writes=[kb_])
                    ctx.dma(SP, hd[hi], vt[0:NM, 0, :], V[0:NM, vcol:vcol + 128], writes=[vb])
                    ctx.dma(SP, hd[hi], vt[:, 1:NB, :], V[NM:T, vcol:vcol + 128].rearrange("(j p) c -> p j c", p=128), writes=[vb])
                    return (qt, qb, kt, kb_, vt, vb)

                def epilogue(h_feat, g, ot, ob, dt_, db_, fox):
                    c0, c1 = GROUPS[g]
                    N = c1 - c0
                    ut, ub, _ = usq.next()
                    ctx.op(ACT, lambda: nc.scalar.activation(out=ut[:, :N], in_=ot[:, :N], func=AF.Square), reads=[ob], writes=[ub])
                    st, sbf, _ = ps.next()
                    ctx.op(PE, lambda: nc.tensor.matmul(st[:, :N], ones_f, ut[:, :N], start=True, stop=True), reads=[ub], writes=[sbf])
                    lt, lb, _ = t2p.next()
                    if fox:
                        t1t, t1b, _ = t1p.next()
                        ctx.op(ACT, lambda: nc.scalar.activation(out=t1t[:, :N], in_=dt_[:, :N], func=AF.Square, scale=EPS ** 0.5),
                               reads=[db_], writes=[t1b])
                        ctx.op(DVE, lambda: nc.vector.scalar_tensor_tensor(out=lt[:, :N], in0=st[:, :N], scalar=1.0 / 128, in1=t1t[:, :N],
                                                                           op0=ALU.mult, op1=ALU.add), reads=[sbf, t1b], writes=[lb])
                        l2, l2b, _ = t2p.next()
                        ctx.op(ACT, lambda: nc.scalar.activation(out=l2[:, :N], in_=lt[:, :N], func=AF.Ln), reads=[lb], writes=[l2b])
                    else:
                        l2, l2b = lt, lb
                        ctx.op(ACT, lambda: nc.scalar.activation(out=l2[:, :N], in_=st[:, :N], func=AF.Ln, bias=EPS, scale=1.0 / 128),
                               reads=[sbf], writes=[l2b])
                    rt2, rb2, _ = rrp.next()
                    ctx.op(ACT, lambda: nc.scalar.activation(out=rt2[:, :N], in_=l2[:, :N], func=AF.Exp, scale=-0.5), reads=[l2b], writes=[rb2])
                    o_t, o_b, oi = ost.next()
                    ctx.op(DVE, lambda: nc.vector.scalar_tensor_tensor(out=o_t[:, :N], in0=ot[:, :N], scalar=cv(l, 3, h_feat), in1=rt2[:, :N],
                                                                       op0=ALU.mult, op1=ALU.mult), reads=[ob, rb2], writes=[o_b])
                    ctx.dma(SP, ostd[oi], ONT[h_feat * 128:(h_feat + 1) * 128, c0:c1], o_t[:, :N], reads=[o_b])

                def fox_head(h, hd_):
                    qt, qb, kt, kbb, vt, vb = hd_
                    for g in range(9):
                        c0, c1 = GROUPS[g]
                        N = c1 - c0
                        jb0, jb1 = grp_blocks(g)
                        ot, ob, _ = po.next()
                        dt_, db_, _ = pdn.next()
                        tiles = []
                        pend = None
                        for kb in range(jb1 + 2):
                            cur = None
                            if kb <= jb1:
                                ks0, ks1 = blk(kb)
                                kk = ks1 - ks0
                                diag = kb >= jb0
                                n0 = ks0 if diag else c0
                                NA = c1 - n0
                                off = n0 - c0
                                st, sbf, _ = ps.next()
                                ctx.op(PE, lambda: nc.tensor.matmul(st[:kk, :NA], kt[:, ks0:ks1], qt[:, n0:c1], start=True, stop=False),
                                       reads=[kbb, qb], writes=[sbf], sig=False)
                                ctx.op(PE, lambda: nc.tensor.matmul(st[:kk, :NA], sel24[0:24, h * 128:h * 128 + kk], ch24[0:24, n0:c1],
                                                                    start=False, stop=(not diag)),
                                       reads=[b_ch3], writes=[sbf], sig=(not diag))
                                if diag:
                                    ctx.op(PE, lambda: nc.tensor.matmul(st[:kk, 0:kk], ident_b[:kk, :kk], maskF_b[:kk, :kk],
                                                                        start=False, stop=True), writes=[sbf], sig=True)
                                p_t, p_b, _ = pT.next()
                                bo = (g * NB + kb) * 8 + h
                                ctx.op(ACT, lambda: nc.scalar.activation(out=p_t[:kk, :NA], in_=st[:kk, :NA], func=AF.Exp,
                                                                         bias=bias_t[:kk, bo:bo + 1], scale=1.0),
                                       reads=[sbf, b_bias], writes=[p_b])
                                cur = (kb, kk, NA, off, p_t, p_b)
                            if pend is not None:
                                kb2, kk2, NA2, off2, p2, p2b = pend
                                ctx.op(PE, lambda: nc.tensor.matmul(ot[:, off2:off2 + NA2], vt[:kk2, kb2, :], p2[:kk2, :NA2],
                                                                    start=(kb2 == 0), stop=(kb2 == jb1)),
                                       reads=[vb, p2b], writes=[ob], sig=False)
                                ctx.op(PE, lambda: nc.tensor.matmul(dt_[:, off2:off2 + NA2], ones_b[:kk2, :], p2[:kk2, :NA2],
                                                                    start=(kb2 == 0), stop=(kb2 == jb1)),
                                       reads=[p2b], writes=[db_], sig=True)
                            pend = cur
                        epilogue(h, g, ot, ob, dt_, db_, True)

                def sb_head(h, hd_):
                    qt, qb, kt, kbb, vt, vb = hd_
                    for g in range(9):
                        c0, c1 = GROUPS[g]
                        N = c1 - c0
                        jb0, jb1 = grp_blocks(g)
                        ot, ob, _ = po.next()
                        ctx.op(PE, lambda: nc.tensor.matmul(ot[:, :N], zeros_b, qt[:, c0:c1], start=True, stop=False),
                               reads=[qb], writes=[ob], sig=False)
                        rs0, rs0b, _ = rs.next()
                        rs1, rs1b, _ = rs.next()
                        ctx.op(POOL, lambda: nc.gpsimd.memset(rs0[:, :], 0.0), writes=[rs0b])
                        ctx.op(POOL, lambda: nc.gpsimd.memset(rs1[:, :], 0.0), writes=[rs1b])
                        rcur, rcurb, rnxt, rnxtb = rs0, rs0b, rs1, rs1b
                        order = list(range(jb1, -1, -1))
                        n = len(order)
                        st1 = [None] * n
                        st2 = [None] * n
                        for step in range(n + 2):
                            if step < n:
                                kb = order[step]
                                ks0, ks1 = blk(kb)
                                kk = ks1 - ks0
                                diag = kb >= jb0
                                n0 = ks0 if diag else c0
                                NA = c1 - n0
                                off = n0 - c0
                                zt, zb, _ = ps.next()
                                ctx.op(PE, lambda: nc.tensor.matmul(zt[:kk, :NA], kt[:, ks0:ks1], qt[:, n0:c1], start=True, stop=False),
                                       reads=[kbb, qb], writes=[zb], sig=(not diag))
                                if diag:
                                    ctx.op(PE, lambda: nc.tensor.matmul(zt[:kk, 0:kk], ident_b[:kk, :kk], maskS_b[:kk, :kk],
                                                                        start=False, stop=False), writes=[zb], sig=True)
                                et, ebb, _ = eb.next()
                                ctx.op(ACT, lambda: nc.scalar.activation(out=et[:kk, :NA], in_=zt[:kk, :NA], func=AF.Exp),
                                       reads=[zb], writes=[ebb])
                                s_t, s_b, _ = spb.next()
                                ctx.op(ACT, lambda: nc.scalar.activation(out=s_t[:kk, :NA], in_=et[:kk, :NA], func=AF.Ln, bias=1.0, scale=1.0),
                                       reads=[ebb], writes=[s_b])
                                st1[step] = (kb, kk, NA, off, zt, zb, s_t, s_b)
                            if 1 <= step <= n:
                                i = step - 1
                                kb, kk, NA, off, zt, zb, s_t, s_b = st1[i]
                                first = (i == 0)
                                ctx.op(PE, lambda: nc.tensor.matmul(zt[:kk, :NA], negtri_b[:kk, :kk], s_t[:kk, :NA], start=False, stop=first),
                                       reads=[s_b], writes=[zb], sig=first)
                                if not first:
                                    ctx.op(PE, lambda: nc.tensor.matmul(zt[:kk, :NA], negones_b[:, :kk], rcur[:, off:off + NA],
                                                                        start=False, stop=True), reads=[rcurb], writes=[zb], sig=True)
                                a_t, a_b, _ = pT.next()
                                ctx.op(ACT, lambda: nc.scalar.activation(out=a_t[:kk, :NA], in_=zt[:kk, :NA], func=AF.Exp),
                                       reads=[zb], writes=[a_b])
                                if kb > 0:
                                    ctx.op(POOL, lambda: nc.gpsimd.tensor_tensor(out=rnxt[:, off:off + NA], in0=rcur[:, off:off + NA],
                                                                                 in1=s_t[:, :NA], op=ALU.add),
                                           reads=[rcurb, s_b], writes=[rnxtb])
                                    rcur, rcurb, rnxt, rnxtb = rnxt, rnxtb, rcur, rcurb
                                st2[i] = (kb, kk, NA, off, a_t, a_b)
                            if step >= 2:
                                i = step - 2
                                kb, kk, NA, off, a_t, a_b = st2[i]
                                ctx.op(PE, lambda: nc.tensor.matmul(ot[:, off:off + NA], vt[:kk, kb, :], a_t[:kk, :NA],
                                                                    start=False, stop=(i == n - 1)),
                                       reads=[vb, a_b], writes=[ob], sig=(i == n - 1))
                        epilogue(8 + h, g, ot, ob, None, None, False)

                nxt = load_head(0, 8, 0)
                for h in range(8):
                    cur = nxt
                    nxt = load_head(h + 1, 9 + h, (h + 1) * 128) if h < 7 else load_head(16, 24, 1024)
                    fox_head(h, cur)
                for h in range(8):
                    cur = nxt
                    if h < 7:
                        nxt = load_head(17 + h, 25 + h, 1024 + (h + 1) * 128)
                    sb_head(h, cur)
                ctx.barrier()

        def out_phase(l):
            with ExitStack() as scope:
                on = scope.enter_context(nc.sbuf_tensor(uq("o_on"), [128, KC, T], BF16))
                on_b = Buf("o_on")
                ond = ctx.dsem(f"o_on{l}")
                for k in range(KC):
                    ctx.dma(SP, ond, on[:, k, :], ONT[k * 128:(k + 1) * 128, :], writes=[on_b])
                wo = Pool(ctx, scope, uq("o_wo"), 2, [128, D], BF16)
                wod = [ctx.dsem(f"owo{i}") for i in range(2)]
                hr = Pool(ctx, scope, uq("o_hr"), 3, [128, 512], F32)
                hrd = [ctx.dsem(f"ohr{i}") for i in range(3)]
                ho = Pool(ctx, scope, uq("o_ho"), 3, [128, 512], F32)
                hod = [ctx.dsem(f"oho{i}") for i in range(3)]
                py = Pool(ctx, scope, uq("o_py"), 3, [128, 512], F32, psum=True)
                for dc in range(KC):
                    wt, wb, wi = wo.next()
                    ctx.dma(POOL, wod[wi], wt[:], wo_d[l * KC + dc], writes=[wb], max_dma_last_dim=4096)
                    for gi in range(9):
                        c0, c1 = GROUPS[gi]
                        N = c1 - c0
                        hrt, hrb, hri = hr.next()
                        ctx.dma(SP, hrd[hri], hrt[:, :N], H[dc * 128:(dc + 1) * 128, c0:c1], writes=[hrb])
                        yt, yb, _ = py.next()
                        for k in range(KC):
                            ctx.op(PE, lambda: nc.tensor.matmul(yt[:, :N], wt[:, k * 128:(k + 1) * 128], on[:, k, c0:c1],
                                                                start=(k == 0), stop=(k == KC - 1)),
                                   reads=[wb, on_b], writes=[yb], sig=(k == KC - 1))
                        hot, hob, hoi = ho.next()
                        ctx.op(DVE, lambda: nc.vector.tensor_tensor(out=hot[:, :N], in0=yt[:, :N], in1=hrt[:, :N], op=ALU.add),
                               reads=[yb, hrb], writes=[hob])
                        ctx.dma(SP, hod[hoi], H[dc * 128:(dc + 1) * 128, c0:c1], hot[:, :N], reads=[hob])
                ctx.barrier()

        def final_phase():
            with ExitStack() as scope:
                pss = Pool(ctx, scope, uq("fn_pss"), 2, [128, 512], F32, psum=True)
                norm_phase(scope, H, list(range(1, 9)), cv_final, out_dst=out_d, ps_ss=pss)
                ctx.barrier()

        for l in range(DEPTH):
            ffn_phase(l, 0, h_in if l == 0 else H, H)
            proj_phase(l)
            attn_phase(l)
            out_phase(l)
            ffn_phase(l, 1, H, H)
        final_phase()
    return nc


def _lhsT_tiles(W, ncol_chunks):
    kc = W.shape[0] // 128
    return np.ascontiguousarray(W.reshape(kc, 128, ncol_chunks, 128).transpose(2, 1, 0, 3).reshape(ncol_chunks, 128, kc * 128))


def _col(v):
    return np.ascontiguousarray(v.reshape(-1, 128).T)


_PROG = None


def kernel(x, meta_tokens, ffn1_norm, ffn1_w_gate, ffn1_w_up, ffn1_w_down, mix_norm, w_in, b_forget, g_fox, g_sb,
           w_out, ffn2_norm, ffn2_w_gate, ffn2_w_up, ffn2_w_down, final_norm):
    global _PROG
    f32 = np.float32
    x = np.asarray(x, f32)
    wgu = []
    wdn = []
    wqk = []
    wv = []
    wf = []
    wo = []
    cvec = []
    for l in range(DEPTH):
        for (g_, u_, d_) in ((ffn1_w_gate, ffn1_w_up, ffn1_w_down), (ffn2_w_gate, ffn2_w_up, ffn2_w_down)):
            wgu.append(_lhsT_tiles(np.asarray(g_[l], f32), NF))
            wgu.append(_lhsT_tiles(np.asarray(u_[l], f32), NF))
            wdn.append(_lhsT_tiles(np.asarray(d_[l], f32), KC))
        wi = np.asarray(w_in[l], f32)
        qk_cols = np.concatenate([wi[:, 0:1024], wi[:, 1024:2048], wi[:, 3072:4096], wi[:, 4096:5120]], axis=1)
        wqk.append(_lhsT_tiles(qk_cols, 32))
        v_cols = np.concatenate([wi[:, 2048:3072], wi[:, 5120:6144]], axis=1)
        wv.append(np.ascontiguousarray(v_cols.reshape(KC, 128, 4, 512).transpose(2, 1, 0, 3).reshape(4, 128, KC * 512)))
        wf.append(np.ascontiguousarray(wi[:, 6144:6152].reshape(KC, 128, 8).transpose(1, 0, 2).reshape(128, KC * 8)))
        wo.append(_lhsT_tiles(np.asarray(w_out[l], f32), KC))
        cvec.append(_col(np.asarray(ffn1_norm[l], f32)))
        cvec.append(_col(np.asarray(mix_norm[l], f32)))
        cvec.append(_col(np.asarray(ffn2_norm[l], f32)))
        cvec.append(_col(np.concatenate([np.asarray(g_fox[l], f32), np.asarray(g_sb[l], f32)])))
        cvec.append(np.ascontiguousarray(np.broadcast_to(np.asarray(b_forget[l], f32)[None, :], (128, 8))))
    cvec.append(_col(np.asarray(final_norm, f32)))
    cvec = np.ascontiguousarray(np.concatenate(cvec, axis=1))
    wgu = np.concatenate(wgu, axis=0)
    wdn = np.concatenate(wdn, axis=0)
    wqk = np.concatenate(wqk, axis=0)
    wv = np.concatenate(wv, axis=0)
    wf = np.stack(wf, axis=0)
    wo = np.concatenate(wo, axis=0)
    p = np.arange(128)
    ones = np.ones((128, 128), f32)
    triu = (p[:, None] <= p[None, :]).astype(f32)
    ident = np.eye(128, dtype=f32)
    cf32 = np.ascontiguousarray(np.concatenate([ones, triu, ident], axis=1))
    negtri = -(p[:, None] >= p[None, :]).astype(f32)
    maskF = np.where(p[:, None] <= p[None, :], 0.0, NEG).astype(f32)
    maskS = np.where(p[:, None] < p[None, :], 0.0, NEG).astype(f32)
    sel = np.zeros((128, 8 * 128), f32)
    for hh in range(8):
        sel[3 * hh:3 * hh + 3, hh * 128:(hh + 1) * 128] = 1.0
    cbf = np.ascontiguousarray(np.concatenate([ones, -ones, negtri, ident, maskF, maskS, np.zeros((128, 128), f32), sel], axis=1))
    meta = np.asarray(meta_tokens, f32)
    if _PROG is None:
        _PROG = build_program()
    nc = _PROG
    in_maps = []
    for c in range(8):
        b = c % 4
        hT = np.ascontiguousarray(np.concatenate([meta, x[b]], axis=0).T)
        in_maps.append({"h_in": hT, "wgu": wgu, "wdn": wdn, "wqk": wqk, "wv": wv, "wf": wf, "wo": wo,
                        "cvec": cvec, "cf32": cf32, "cbf": cbf})
    res = run_bass_kernel_spmd(nc, in_maps, core_ids=list(range(8)))
    out = np.stack([np.ascontiguousarray(res.results[b]["outT"].T) for b in range(4)], axis=0)
    return out.astype(f32)
```

```python
import numpy as np
from contextlib import ExitStack
import concourse.bass as bass
import concourse.mybir as mybir
from concourse.bass_utils import run_bass_kernel_spmd

F32 = mybir.dt.float32
BF16 = mybir.dt.bfloat16
AF = mybir.ActivationFunctionType
ALU = mybir.AluOpType

D = 2048
NM = 16
SEQ = 4096
T = SEQ + NM
DFF = 5632
NF = DFF // 128
KC = D // 128
NH = 8
NB = 33
DEPTH = 2
EPS = 1e-6
NEG = -30000.0

GROUPS = [(0, NM)] + [(NM + 512 * i, NM + 512 * (i + 1)) for i in range(8)]
TL = NM + SEQ // 2
NBL = 17
NHL = 4
GROUPS_L = [(0, NM)] + [(NM + 512 * i, NM + 512 * (i + 1)) for i in range(4)]
SGS = [[0, 1, 2], [3, 4]]
PAIRS = [[0, 1], [2, 3], [4, 5], [6, 7]]


def blk(j):
    return (0, NM) if j == 0 else (NM + 128 * (j - 1), NM + 128 * j)


def grp_blocks(g):
    return (0, 0) if g == 0 else (4 * (g - 1) + 1, 4 * g)


class Buf:
    __slots__ = ("name", "w", "r")

    def __init__(self, name):
        self.name = name
        self.w = None
        self.r = {}


class Eng:
    def __init__(self, nc, es, name, e):
        self.name = name
        self.e = e
        self.sem = es.enter_context(nc.semaphore("s_" + name))
        self.key = "E_" + name
        self.cnt = 0
        self.waited = {}

    def wait(self, tok):
        sem, val, key = tok
        if self.waited.get(key, 0) >= val:
            return
        self.e.wait_ge(sem, val)
        self.waited[key] = val


class DSem:
    def __init__(self, nc, es, name):
        self.sem = es.enter_context(nc.semaphore("d_" + name))
        self.key = "D_" + name
        self.cnt = 0


class Ctx:
    def __init__(self, nc, es):
        self.nc = nc
        self.es = es
        self.pe = Eng(nc, es, "pe", nc.tensor)
        self.act = Eng(nc, es, "act", nc.scalar)
        self.dve = Eng(nc, es, "dve", nc.vector)
        self.pool = Eng(nc, es, "pool", nc.gpsimd)
        self.sp = Eng(nc, es, "sp", nc.sync)
        self.engs = [self.pe, self.act, self.dve, self.pool, self.sp]
        self.dsems = []
        self.dcache = {}

    def dsem(self, name):
        if name in self.dcache:
            return self.dcache[name]
        d = DSem(self.nc, self.es, name)
        self.dsems.append(d)
        self.dcache[name] = d
        return d

    def _deps(self, E, reads, writes):
        for b in reads:
            if b.w is not None:
                if b.w[2] == E.key and E is self.pe:
                    continue
                E.wait(b.w)
        for b in writes:
            if b.w is not None and b.w[2] != E.key:
                E.wait(b.w)
            for tok in b.r.values():
                if tok[2] != E.key:
                    E.wait(tok)

    def _record(self, tok, reads, writes):
        for b in reads:
            old = b.r.get(tok[2])
            if old is None or old[1] < tok[1]:
                b.r[tok[2]] = tok
        for b in writes:
            b.w = tok
            b.r = {}

    def op(self, E, fn, reads=(), writes=(), sig=True):
        self._deps(E, reads, writes)
        ins = fn()
        if sig:
            E.cnt += 1
            ins.then_inc(E.sem, 1)
            tok = (E.sem, E.cnt, E.key)
        else:
            tok = (E.sem, E.cnt + 1, E.key)
        self._record(tok, reads, writes)
        return ins

    def dma(self, Q, ds, out, in_, reads=(), writes=(), **kw):
        self._deps(Q, reads, writes)
        ins = Q.e.dma_start(out=out, in_=in_, **kw)
        ds.cnt += 16
        ins.then_inc(ds.sem, 16)
        tok = (ds.sem, ds.cnt, ds.key)
        self._record(tok, reads, writes)
        self.last_tok = tok
        return ins

    def barrier(self):
        for E in self.engs:
            for O in self.engs:
                if O is not E and O.cnt > 0:
                    E.wait((O.sem, O.cnt, O.key))
            for d in self.dsems:
                if d.cnt > 0:
                    E.wait((d.sem, d.cnt, d.key))


class Pool:
    def __init__(self, ctx, es, name, n, shape, dtype, psum=False, dma=False):
        nc = ctx.nc
        self.t = []
        self.b = []
        self.d = []
        for i in range(n):
            nm = f"{name}{i}"
            if psum:
                t = es.enter_context(nc.psum_tensor(nm, shape, dtype))
            else:
                t = es.enter_context(nc.sbuf_tensor(nm, shape, dtype))
            self.t.append(t)
            self.b.append(Buf(nm))
        self.n = n
        self.i = -1

    def next(self):
        self.i = (self.i + 1) % self.n
        return self.t[self.i], self.b[self.i], self.i


def build_program(dbg=None):
    nc = bass.Bass("TRN2", target_bir_lowering=False)
    dt = nc.dram_tensor
    h_in = dt("h_in", [D, TL], F32, kind="ExternalInput").ap()
    wgu_d = dt("wgu", [DEPTH * 2 * 2 * NF, 128, D], F32, kind="ExternalInput").ap()
    wdn_d = dt("wdn", [DEPTH * 2 * KC, 128, DFF], F32, kind="ExternalInput").ap()
    wqk_d = dt("wqk", [DEPTH * 32, 128, D], F32, kind="ExternalInput").ap()
    wv_d = dt("wv", [DEPTH * 4, 128, KC * 512], F32, kind="ExternalInput").ap()
    wf_d = dt("wf", [DEPTH, 128, KC * 8], F32, kind="ExternalInput").ap()
    wo_d = dt("wo", [DEPTH * KC, 128, D], F32, kind="ExternalInput").ap()
    NCV = DEPTH * (4 * KC + 8) + KC
    cvec_d = dt("cvec", [128, NCV], F32, kind="ExternalInput").ap()
    cf32_d = dt("cf32", [128, 3 * 128], F32, kind="ExternalInput").ap()
    cbf_d = dt("cbf", [128, 15 * 128], F32, kind="ExternalInput").ap()
    rsel_d = dt("rsel", [128, 2], F32, kind="ExternalInput").ap()
    out_d = dt("outT", [D, SEQ // 2], F32, kind="ExternalOutput").ap()
    H = dt("Hs", [D, TL], F32, kind="Internal").ap()
    VQ = [(0, 528), (528, 1040), (1040, 1552), (1552, 2064)]
    SQK = [dt(f"SQK{c}", [128, TL], BF16, kind="Internal").ap() for c in range(32)]
    RQK = [dt(f"RQK{c}", [256, TL], BF16, kind="Internal").ap() for c in range(32)]
    SV = [[dt(f"SV{v}_{q}", [r1 - r0, 512], BF16, kind="Internal").ap() for q, (r0, r1) in enumerate(VQ)] for v in range(4)]
    RV = [[dt(f"RV{v}_{q}", [2 * (r1 - r0), 512], BF16, kind="Internal").ap() for q, (r0, r1) in enumerate(VQ)] for v in range(4)]
    SLF = dt("SLF", [128, NBL * 8], F32, kind="Internal").ap()
    RLF = dt("RLF", [2 * 128, NBL * 8], F32, kind="Internal").ap()
    SO = [[dt(f"SO{h}_{i}", [64, T], BF16, kind="Internal").ap() for i in range(2)] for h in range(8)]
    RO = [[dt(f"RO{h}_{i}", [128, T], BF16, kind="Internal").ap() for i in range(2)] for h in range(8)]

    with ExitStack() as es:
        ctx = Ctx(nc, es)
        PE, ACT, DVE, POOL, SP = ctx.pe, ctx.act, ctx.dve, ctx.pool, ctx.sp
        sb = lambda name, shape, dtp: es.enter_context(nc.sbuf_tensor(name, shape, dtp))
        _uqc = [0]

        def uq(n):
            _uqc[0] += 1
            return f"{n}_{_uqc[0]}_"

        cvec = sb("cvec_t", [128, NCV], F32)
        cf32 = sb("cf32_t", [128, 3 * 128], F32)
        cbf = sb("cbf_t", [128, 15 * 128], BF16)
        ones3 = sb("ones3_t", [3, 128], BF16)
        lfseq = sb("lfseq_t", [128, NB * NHL], F32)
        rsel = sb("rsel_t", [128, 2], F32)
        b_const = Buf("const")
        b_lf = Buf("lfseq")
        dconst = ctx.dsem("const")
        ctx.dma(SP, dconst, cvec[:], cvec_d[:, :], writes=[b_const])
        ctx.dma(SP, dconst, cf32[:], cf32_d[:, :], writes=[b_const])
        ctx.dma(SP, dconst, rsel[:], rsel_d[:, :], writes=[b_const])
        dconst2 = ctx.dsem("const2")
        ctx.dma(POOL, dconst2, cbf[:], cbf_d[:, :], writes=[b_const])
        ctx.dma(POOL, dconst2, ones3[:], cbf_d[0:3, 0:128], writes=[b_const])
        ctx.barrier()
        ones_f = cf32[:, 0:128]
        triu_f = cf32[:, 128:256]
        ident_f = cf32[:, 256:384]
        ones_b = cbf[:, 0:128]
        negones_b = cbf[:, 128:256]
        negtri_b = cbf[:, 256:384]
        ident_b = cbf[:, 384:512]
        maskF_b = cbf[:, 512:640]
        maskS_b = cbf[:, 640:768]
        zeros_b = cbf[:, 768:896]
        sel24 = cbf[:, 896:1920]
        m0 = rsel[:, 0:1]
        m1 = rsel[:, 1:2]
        ccsem = ctx.dsem("cc")

        def all_gather(src, dst, toks):
            for t in toks:
                POOL.wait(t)
            ins = nc.gpsimd.collective_compute("AllGather", ALU.bypass, replica_groups=PAIRS, ins=[src], outs=[dst])
            ins.then_inc(ccsem.sem)
            ccsem.cnt += 1

        def blend(out, a, b, tmp, tmp_b, reads, writes, eng=None):
            ctx.op(DVE, lambda: nc.vector.tensor_scalar(out=tmp, in0=b, scalar1=m1, scalar2=None, op0=ALU.mult),
                   reads=reads, writes=[tmp_b])
            ctx.op(DVE, lambda: nc.vector.scalar_tensor_tensor(out=out, in0=a, scalar=m0, in1=tmp, op0=ALU.mult, op1=ALU.add),
                   reads=list(reads) + [tmp_b], writes=writes)

        def cv(l, which, k):
            base = l * (4 * KC + 8) + which * KC + k
            return cvec[:, base:base + 1]

        def cv_bf(l):
            base = l * (4 * KC + 8) + 4 * KC
            return cvec[:, base:base + 8]

        def cv_final(k):
            base = DEPTH * (4 * KC + 8) + k
            return cvec[:, base:base + 1]

        def ffn_phase(l, which, src, dst):
            wrow_gu = ((l * 2 + which) * 2) * NF
            wrow_dn = (l * 2 + which) * KC
            with ExitStack() as scope:
                xn = scope.enter_context(nc.sbuf_tensor(uq("f_xn"), [128, KC, 1040], BF16))
                xn_b = Buf("f_xn")
                A = scope.enter_context(nc.sbuf_tensor(uq("f_A"), [128, NF, 1040], BF16))
                A_b = [Buf(f"A{f}") for f in range(NF)]
                wg = Pool(ctx, scope, uq("f_wg"), 2, [128, D], BF16)
                wu = Pool(ctx, scope, uq("f_wu"), 2, [128, D], BF16)
                wgd = [ctx.dsem(f"wg{i}") for i in range(2)]
                wd = Pool(ctx, scope, uq("f_wd"), 2, [128, DFF], BF16)
                wdd = [ctx.dsem(f"wd{i}") for i in range(2)]
                sg = Pool(ctx, scope, uq("f_sg"), 2, [128, 512], F32)
                hr = Pool(ctx, scope, uq("f_hr"), 2, [128, 512], F32)
                hrd = [ctx.dsem(f"hr{i}") for i in range(2)]
                ho = Pool(ctx, scope, uq("f_ho"), 2, [128, 512], F32)
                hod = [ctx.dsem(f"ho{i}") for i in range(2)]
                pg = Pool(ctx, scope, uq("f_pg"), 2, [128, 512], F32, psum=True)
                pu = Pool(ctx, scope, uq("f_pu"), 2, [128, 512], F32, psum=True)
                py = Pool(ctx, scope, uq("f_py"), 2, [128, 512], F32, psum=True)
                pss = Pool(ctx, scope, uq("f_pss"), 2, [128, 512], F32, psum=True)
                for sgl in SGS:
                    norm_phase(scope, src, sgl, lambda k: cv(l, 0 if which == 0 else 2, k),
                               xn=xn, xn_b=xn_b, ps_ss=pss, NP=128)
                    offs = []
                    o = 0
                    for gi in sgl:
                        offs.append(o)
                        o += GROUPS_L[gi][1] - GROUPS_L[gi][0]
                    for f in range(NF):
                        wgt, wgb, wi = wg.next()
                        wut, wub, _ = wu.next()
                        ctx.dma(POOL, wgd[wi], wgt[:], wgu_d[wrow_gu + f], writes=[wgb], max_dma_last_dim=4096)
                        ctx.dma(POOL, wgd[wi], wut[:], wgu_d[wrow_gu + NF + f], writes=[wub], max_dma_last_dim=4096)
                        for gi, off in zip(sgl, offs):
                            N = GROUPS_L[gi][1] - GROUPS_L[gi][0]
                            gt, gb, _ = pg.next()
                            ut, ub, _ = pu.next()
                            for k in range(KC):
                                ctx.op(PE, lambda: nc.tensor.matmul(gt[:, :N], wgt[:, k * 128:(k + 1) * 128], xn[:, k, off:off + N],
                                                                    start=(k == 0), stop=(k == KC - 1)),
                                       reads=[wgb, xn_b], writes=[gb], sig=(k == KC - 1))
                            for k in range(KC):
                                ctx.op(PE, lambda: nc.tensor.matmul(ut[:, :N], wut[:, k * 128:(k + 1) * 128], xn[:, k, off:off + N],
                                                                    start=(k == 0), stop=(k == KC - 1)),
                                       reads=[wub, xn_b], writes=[ub], sig=(k == KC - 1))
                            st, sbf, _ = sg.next()
                            ctx.op(ACT, lambda: nc.scalar.activation(out=st[:, :N], in_=gt[:, :N], func=AF.Silu),
                                   reads=[gb], writes=[sbf])
                            ctx.op(DVE, lambda: nc.vector.tensor_tensor(out=A[:, f, off:off + N], in0=st[:, :N], in1=ut[:, :N], op=ALU.mult),
                                   reads=[sbf, ub], writes=[A_b[f]])
                    for dc in range(KC):
                        wdt, wdb, wi = wd.next()
                        ctx.dma(POOL, wdd[wi], wdt[:], wdn_d[wrow_dn + dc], writes=[wdb], max_dma_last_dim=4096)
                        for gi, off in zip(sgl, offs):
                            c0, c1 = GROUPS_L[gi]
                            N = c1 - c0
                            hrt, hrb, hri = hr.next()
                            ctx.dma(SP, hrd[hri], hrt[:, :N], src[dc * 128:(dc + 1) * 128, c0:c1], writes=[hrb])
                            yt, yb, _ = py.next()
                            for f in range(NF):
                                ctx.op(PE, lambda: nc.tensor.matmul(yt[:, :N], wdt[:, f * 128:(f + 1) * 128], A[:, f, off:off + N],
                                                                    start=(f == 0), stop=(f == NF - 1)),
                                       reads=[wdb, A_b[f]], writes=[yb], sig=(f == NF - 1))
                            hot, hob, hoi = ho.next()
                            ctx.op(DVE, lambda: nc.vector.scalar_tensor_tensor(out=hot[:, :N], in0=yt[:, :N], scalar=0.5, in1=hrt[:, :N],
                                                                               op0=ALU.mult, op1=ALU.add),
                                   reads=[yb, hrb], writes=[hob])
                            ctx.dma(SP, hod[hoi], dst[dc * 128:(dc + 1) * 128, c0:c1], hot[:, :N], reads=[hob])
                ctx.barrier()

        _norm_cache = {}

        def norm_phase(scope, src, groups, gain_fn, xn=None, xn_b=None, xoff0=0, out_dst=None, ps_ss=None, NP=256):
            c = getattr(scope, "_norm_tiles", None)
            if c is None:
                c = {}
                c["hst"] = Pool(ctx, scope, uq("hst"), 2, [128, KC, NP], F32)
                c["sq"] = Pool(ctx, scope, uq("nsq"), 2, [128, NP], F32)
                c["lnb"] = Pool(ctx, scope, uq("nln"), 2, [128, NP], F32)
                c["rsb"] = Pool(ctx, scope, uq("nrs"), 2, [128, NP], F32)
                if out_dst is not None:
                    c["ost"] = Pool(ctx, scope, uq("nost"), 3, [128, NP], F32)
                scope._norm_tiles = c
            hst, sq, lnb, rsb = c["hst"], c["sq"], c["lnb"], c["rsb"]
            ost = c.get("ost")
            xoff = xoff0
            for gi in groups:
                c0, c1 = GROUPS_L[gi]
                for p0 in range(c0, c1, NP):
                    p1 = min(c1, p0 + NP)
                    N = p1 - p0
                    ht, hb, hi = hst.next()
                    ctx.dma(SP, g_hst_d[hi], ht[:, :, :N],
                            src[:, p0:p1].rearrange("(k p) n -> p k n", p=128), writes=[hb])
                    st, sbf, _ = ps_ss.next()
                    for k in range(KC):
                        qt, qb, _ = sq.next()
                        ctx.op(ACT, lambda: nc.scalar.activation(out=qt[:, :N], in_=ht[:, k, :N], func=AF.Square),
                               reads=[hb], writes=[qb])
                        ctx.op(PE, lambda: nc.tensor.matmul(st[:, :N], ones_f, qt[:, :N], start=(k == 0), stop=(k == KC - 1)),
                               reads=[qb], writes=[sbf], sig=True)
                    lt, lb, _ = lnb.next()
                    ctx.op(ACT, lambda: nc.scalar.activation(out=lt[:, :N], in_=st[:, :N], func=AF.Ln, bias=EPS, scale=1.0 / D),
                           reads=[sbf], writes=[lb])
                    rt, rb, _ = rsb.next()
                    ctx.op(ACT, lambda: nc.scalar.activation(out=rt[:, :N], in_=lt[:, :N], func=AF.Exp, scale=-0.5),
                           reads=[lb], writes=[rb])
                    for k in range(KC):
                        if out_dst is None:
                            ctx.op(DVE, lambda: nc.vector.scalar_tensor_tensor(
                                out=xn[:, k, xoff:xoff + N], in0=ht[:, k, :N], scalar=gain_fn(k), in1=rt[:, :N],
                                op0=ALU.mult, op1=ALU.mult), reads=[hb, rb], writes=[xn_b])
                        else:
                            ot, ob, oi = ost.next()
                            ctx.op(DVE, lambda: nc.vector.scalar_tensor_tensor(
                                out=ot[:, :N], in0=ht[:, k, :N], scalar=gain_fn(k), in1=rt[:, :N],
                                op0=ALU.mult, op1=ALU.mult), reads=[hb, rb], writes=[ob])
                            ctx.dma(SP, g_ost_d[oi], out_dst[k * 128:(k + 1) * 128, p0 - NM:p1 - NM], ot[:, :N], reads=[ob])
                    xoff += N

        g_hst_d = [ctx.dsem(f"ghst{i}") for i in range(2)]
        g_ost_d = [ctx.dsem(f"gost{i}") for i in range(3)]

        def proj_phase(l):
            with ExitStack() as scope:
                xn = scope.enter_context(nc.sbuf_tensor(uq("p_xn"), [128, KC, TL], BF16))
                xn_b = Buf("p_xn")
                pss = Pool(ctx, scope, uq("p_pss"), 2, [128, 512], F32, psum=True)
                pq = Pool(ctx, scope, uq("p_pq"), 3, [128, 512], F32, psum=True)
                with ExitStack() as nscope:
                    norm_phase(nscope, H, list(range(5)), lambda k: cv(l, 1, k), xn=xn, xn_b=xn_b, ps_ss=pss)
                    ctx.barrier()
                wq = Pool(ctx, scope, uq("p_wq"), 2, [128, D], BF16)
                wqd = [ctx.dsem(f"pwq{i}") for i in range(2)]
                stg = Pool(ctx, scope, uq("p_stg"), 3, [128, 512], BF16)
                stgd = [ctx.dsem(f"pstg{i}") for i in range(3)]
                def load_wq(cc_):
                    wt_, wb_, wi_ = wq.next()
                    ctx.dma(POOL, wqd[wi_], wt_[:], wqk_d[l * 32 + cc_], writes=[wb_], max_dma_last_dim=4096)
                    return wt_, wb_

                wnext = load_wq(0)
                for cc in range(32):
                    wt, wb = wnext
                    is_q = (cc // 8) in (0, 2)
                    qtoks = []
                    for gi in range(5):
                        c0, c1 = GROUPS_L[gi]
                        N = c1 - c0
                        pt, pb, _ = pq.next()
                        for k in range(KC):
                            ctx.op(PE, lambda: nc.tensor.matmul(pt[:, :N], wt[:, k * 128:(k + 1) * 128], xn[:, k, c0:c1],
                                                                start=(k == 0), stop=(k == KC - 1)),
                                   reads=[wb, xn_b], writes=[pb], sig=(k == KC - 1))
                        st, sbf, si = stg.next()
                        scale = (128.0 ** -0.5) if is_q else 1.0
                        if (gi + cc) % 2 == 0:
                            ctx.op(ACT, lambda: nc.scalar.activation(out=st[:, :N], in_=pt[:, :N], func=AF.Copy, scale=scale),
                                   reads=[pb], writes=[sbf])
                        else:
                            ctx.op(DVE, lambda: nc.vector.tensor_scalar(out=st[:, :N], in0=pt[:, :N], scalar1=scale, scalar2=None,
                                                                        op0=ALU.mult), reads=[pb], writes=[sbf])
                        ctx.dma(SP, stgd[si], SQK[cc][:, c0:c1], st[:, :N], reads=[sbf])
                        qtoks.append(ctx.last_tok)
                    if cc + 1 < 32:
                        wnext = load_wq(cc + 1)
                    all_gather(SQK[cc], RQK[cc], qtoks)
                wv = Pool(ctx, scope, uq("p_wv"), 2, [128, KC * 512], BF16)
                wvd = [ctx.dsem(f"pwv{i}") for i in range(2)]
                def load_wv(vg_):
                    wt_, wb_, wi_ = wv.next()
                    ctx.dma(POOL, wvd[wi_], wt_[:], wv_d[l * 4 + vg_], writes=[wb_], max_dma_last_dim=4096)
                    return wt_, wb_

                wvnext = load_wv(0)
                for vg in range(4):
                    wt, wb = wvnext
                    vtoks = [[], [], [], []]
                    for j in range(NBL):
                        t0, t1 = blk(j)
                        kk = t1 - t0
                        pt, pb, _ = pq.next()
                        for k in range(KC):
                            ctx.op(PE, lambda: nc.tensor.matmul(pt[:kk, :512], xn[:, k, t0:t1], wt[:, k * 512:(k + 1) * 512],
                                                                start=(k == 0), stop=(k == KC - 1)),
                                   reads=[wb, xn_b], writes=[pb], sig=(k == KC - 1))
                        st, sbf, si = stg.next()
                        if j % 2 == 0:
                            ctx.op(ACT, lambda: nc.scalar.activation(out=st[:kk, :512], in_=pt[:kk, :512], func=AF.Copy),
                                   reads=[pb], writes=[sbf])
                        else:
                            ctx.op(DVE, lambda: nc.vector.tensor_copy(out=st[:kk, :512], in_=pt[:kk, :512]), reads=[pb], writes=[sbf])
                        q_ = 0 if j <= 4 else (j - 1) // 4
                        ctx.dma(SP, stgd[si], SV[vg][q_][t0 - VQ[q_][0]:t1 - VQ[q_][0], :], st[:kk, :512], reads=[sbf])
                        vtoks[q_].append(ctx.last_tok)
                    if vg + 1 < 4:
                        wvnext = load_wv(vg + 1)
                    for q_ in range(4):
                        all_gather(SV[vg][q_], RV[vg][q_], vtoks[q_])
                wf = scope.enter_context(nc.sbuf_tensor(uq("p_wf"), [128, KC * 8], BF16))
                wfb = Buf("p_wf")
                wfd = ctx.dsem("pwf")
                ctx.dma(POOL, wfd, wf[:], wf_d[l], writes=[wfb])
                ft = Pool(ctx, scope, uq("p_ft"), 2, [128, 8], F32)
                lfl = scope.enter_context(nc.sbuf_tensor(uq("p_lfl"), [128, NBL * 8], F32))
                b_lfl = Buf("lfl")
                ctx.op(DVE, lambda: nc.vector.memset(lfl[:], 0.0), writes=[b_lfl])
                for j in range(NBL):
                    t0, t1 = blk(j)
                    kk = t1 - t0
                    pt, pb, _ = pq.next()
                    for k in range(KC):
                        ctx.op(PE, lambda: nc.tensor.matmul(pt[:kk, :8], xn[:, k, t0:t1], wf[:, k * 8:(k + 1) * 8],
                                                            start=(k == 0), stop=(k == KC - 1)),
                               reads=[wfb, xn_b], writes=[pb], sig=(k == KC - 1))
                    f1, f1b, _ = ft.next()
                    ctx.op(DVE, lambda: nc.vector.tensor_tensor(out=f1[:kk, :], in0=pt[:kk, :8], in1=cv_bf(l)[:kk, :], op=ALU.add),
                           reads=[pb], writes=[f1b])
                    f2, f2b, _ = ft.next()
                    ctx.op(ACT, lambda: nc.scalar.activation(out=f2[:kk, :], in_=f1[:kk, :], func=AF.Exp, scale=-1.0),
                           reads=[f1b], writes=[f2b])
                    f3, f3b, _ = ft.next()
                    ctx.op(ACT, lambda: nc.scalar.activation(out=f3[:kk, :], in_=f2[:kk, :], func=AF.Ln, bias=1.0, scale=1.0),
                           reads=[f2b], writes=[f3b])
                    ctx.op(DVE, lambda: nc.vector.tensor_scalar(out=lfl[:kk, j * 8:(j + 1) * 8], in0=f3[:kk, :], scalar1=-1.0,
                                                                scalar2=None, op0=ALU.mult), reads=[f3b], writes=[b_lfl])
                ctx.dma(SP, ctx.dsem("slf"), SLF[:, :], lfl[:, :], reads=[b_lfl])
                all_gather(SLF, RLF, [ctx.last_tok])
                ctx.barrier()

        def attn_phase(l):
            with ExitStack() as scope:
                bias_t = scope.enter_context(nc.sbuf_tensor(uq("a_bias"), [128, 9 * NB * NHL], F32))
                ch24 = scope.enter_context(nc.sbuf_tensor(uq("a_ch24"), [3 * NHL, T], BF16))
                with ExitStack() as pscope:
                    pm = Pool(ctx, pscope, uq("a_pm"), 2, [128, 512], F32, psum=True)
                    L0 = pscope.enter_context(nc.sbuf_tensor(uq("a_L0"), [128, NBL * 8], F32))
                    L1 = pscope.enter_context(nc.sbuf_tensor(uq("a_L1"), [128, NBL * 8], F32))
                    ltmp = pscope.enter_context(nc.sbuf_tensor(uq("a_ltmp"), [128, NBL * NHL], F32))
                    b_L, b_ltmp = Buf("L01"), Buf("ltmp")
                    dlf = ctx.dsem("rlf")
                    ctx.dma(SP, dlf, L0[:, :], RLF[0:128, :], writes=[b_L])
                    ctx.dma(SP, dlf, L1[:, :], RLF[128:256, :], writes=[b_L])
                    L0v = L0[:, :].rearrange("p (b h) -> p b h", h=8)
                    L1v = L1[:, :].rearrange("p (b h) -> p b h", h=8)
                    lfv = lfseq[:, :].rearrange("p (b h) -> p b h", h=NHL)
                    ltv = ltmp[:, :].rearrange("p (b h) -> p b h", h=NHL)
                    blend(lfv[:, 0:NBL, :], L0v[:, :, 0:NHL], L0v[:, :, NHL:8], ltv[:, 0:NBL, :], b_ltmp, [b_L], [b_lf])
                    blend(lfv[:, NBL:NB, :], L1v[:, 1:NBL, 0:NHL], L1v[:, 1:NBL, NHL:8], ltv[:, 0:NBL - 1, :], b_ltmp, [b_L], [b_lf])
                    ctok = pscope.enter_context(nc.sbuf_tensor(uq("a_ctok"), [128, NB * NHL], F32))
                    tot = pscope.enter_context(nc.sbuf_tensor(uq("a_tot"), [128, NB * NHL], F32))
                    pex = pscope.enter_context(nc.sbuf_tensor(uq("a_pex"), [128, NB * NHL], F32))
                    chm = pscope.enter_context(nc.sbuf_tensor(uq("a_chm"), [NHL, T], F32))
                    r1 = pscope.enter_context(nc.sbuf_tensor(uq("a_r1"), [NHL, T], F32))
                    cb3 = [pscope.enter_context(nc.sbuf_tensor(uq(f"a_cb{i}"), [NHL, T], BF16)) for i in range(3)]
                    rhm = pscope.enter_context(nc.sbuf_tensor(uq("a_rhm"), [NHL, 18], F32))
                    b_ctok, b_tot, b_pex, b_bias, b_chm, b_r1, b_rhm, b_ch3 = (Buf(n) for n in
                                                                               ("ctok", "tot", "pex", "bias", "chm", "r1", "rhm", "ch3"))
                    b_cb3 = [Buf(f"cb{i}") for i in range(3)]
                    wt_, wb_, _ = pm.next()
                    ctx.op(PE, lambda: nc.tensor.matmul(wt_[:, :NB * NHL], triu_f, lfseq[:, :], start=True, stop=True),
                           reads=[b_lf], writes=[wb_])
                    tt_, tb_, _ = pm.next()
                    ctx.op(PE, lambda: nc.tensor.matmul(tt_[:, :NB * NHL], ones_f, lfseq[:, :], start=True, stop=True),
                           reads=[b_lf], writes=[tb_])
                    ctx.op(DVE, lambda: nc.vector.tensor_copy(out=tot[:, :], in_=tt_[:, :NB * NHL]), reads=[tb_], writes=[b_tot])
                    ctx.op(DVE, lambda: nc.vector.memset(pex[:, 0:NHL], 0.0), writes=[b_pex])
                    for j in range(1, NB):
                        ctx.op(DVE, lambda: nc.vector.tensor_tensor(out=pex[:, j * NHL:(j + 1) * NHL], in0=pex[:, (j - 1) * NHL:j * NHL],
                                                                    in1=tot[:, (j - 1) * NHL:j * NHL], op=ALU.add),
                               reads=[b_tot, b_pex], writes=[b_pex])
                    ctx.op(DVE, lambda: nc.vector.tensor_tensor(out=ctok[:, :], in0=wt_[:, :NB * NHL], in1=pex[:, :], op=ALU.add),
                           reads=[wb_, b_pex], writes=[b_ctok])
                    for g in range(9):
                        jb0, jb1 = grp_blocks(g)
                        for kb in range(jb1 + 1):
                            o = (g * NB + kb) * NHL
                            ctx.op(DVE, lambda: nc.vector.tensor_tensor(out=bias_t[:, o:o + NHL], in0=pex[:, jb0 * NHL:(jb0 + 1) * NHL],
                                                                        in1=ctok[:, kb * NHL:(kb + 1) * NHL], op=ALU.subtract),
                                   reads=[b_pex, b_ctok], writes=[b_bias])
                    rt_, rb_, _ = pm.next()
                    for g in range(9):
                        jb0, _ = grp_blocks(g)
                        ctx.op(PE, lambda: nc.tensor.matmul(rt_[0:NHL, 2 * g:2 * g + 2], pex[:, jb0 * NHL:(jb0 + 1) * NHL], ident_f[:, 0:2],
                                                            start=True, stop=True), reads=[b_pex], writes=[rb_])
                    ctx.op(DVE, lambda: nc.vector.tensor_copy(out=rhm[:, 0:18], in_=rt_[0:NHL, 0:18]), reads=[rb_], writes=[b_rhm])
                    for g in range(9):
                        c0, c1 = GROUPS[g]
                        jb0, jb1 = grp_blocks(g)
                        ct_, cbb_, _ = pm.next()
                        for j in range(jb0, jb1 + 1):
                            t0, t1 = blk(j)
                            kk = t1 - t0
                            ctx.op(PE, lambda: nc.tensor.matmul(ct_[0:NHL, t0 - c0:t1 - c0], ctok[:kk, j * NHL:(j + 1) * NHL], ident_f[:kk, :kk],
                                                                start=True, stop=True), reads=[b_ctok], writes=[cbb_])
                        rcol = rhm[:, 2 * g:2 * g + 1]
                        ctx.op(DVE, lambda: nc.vector.tensor_scalar(out=chm[:, c0:c1], in0=ct_[0:NHL, 0:c1 - c0], scalar1=rcol, scalar2=None,
                                                                    op0=ALU.subtract), reads=[cbb_, b_rhm], writes=[b_chm])
                    ctx.op(DVE, lambda: nc.vector.tensor_copy(out=cb3[0][:, :], in_=chm[:, :]), reads=[b_chm], writes=[b_cb3[0]])
                    ctx.op(DVE, lambda: nc.vector.tensor_tensor(out=r1[:, :], in0=chm[:, :], in1=cb3[0][:, :], op=ALU.subtract),
                           reads=[b_chm, b_cb3[0]], writes=[b_r1])
                    ctx.op(DVE, lambda: nc.vector.tensor_copy(out=cb3[1][:, :], in_=r1[:, :]), reads=[b_r1], writes=[b_cb3[1]])
                    ctx.op(DVE, lambda: nc.vector.tensor_tensor(out=chm[:, :], in0=r1[:, :], in1=cb3[1][:, :], op=ALU.subtract),
                           reads=[b_r1, b_cb3[1]], writes=[b_chm])
                    ctx.op(DVE, lambda: nc.vector.tensor_copy(out=cb3[2][:, :], in_=chm[:, :]), reads=[b_chm], writes=[b_cb3[2]])
                    dch = ctx.dsem(f"ch3_{l}")
                    for i in range(3):
                        for h in range(NHL):
                            ctx.dma(SP, dch, ch24[3 * h + i:3 * h + i + 1, :], cb3[i][h:h + 1, :], reads=[b_cb3[i]], writes=[b_ch3])

                    ctx.barrier()
                qT = Pool(ctx, scope, uq("a_qT"), 2, [128, T], BF16)
                kT = Pool(ctx, scope, uq("a_kT"), 2, [128, T], BF16)
                Vh = Pool(ctx, scope, uq("a_Vh"), 2, [128, NB * 128], BF16)
                hd = [ctx.dsem(f"ahd{i}") for i in range(2)]
                ps = Pool(ctx, scope, uq("a_ps"), 3, [128, 512], F32, psum=True)
                po = Pool(ctx, scope, uq("a_po"), 2, [128, 512], F32, psum=True)
                pdn = Pool(ctx, scope, uq("a_pd"), 1, [128, 512], F32, psum=True)
                pT = Pool(ctx, scope, uq("a_pT"), 3, [128, 512], BF16)
                eb = Pool(ctx, scope, uq("a_eb"), 2, [128, 512], F32)
                spb = Pool(ctx, scope, uq("a_sp"), 3, [128, 512], BF16)
                rs = Pool(ctx, scope, uq("a_rs"), 2, [128, 512], BF16)
                usq = Pool(ctx, scope, uq("a_usq"), 2, [128, 512], F32)
                t1p = Pool(ctx, scope, uq("a_t1"), 2, [128, 512], F32)
                t2p = Pool(ctx, scope, uq("a_t2"), 2, [128, 512], F32)
                rrp = Pool(ctx, scope, uq("a_rr"), 2, [128, 512], F32)
                ost = Pool(ctx, scope, uq("a_ost"), 2, [128, 512], BF16)
                ostd = [ctx.dsem(f"aost{i}") for i in range(2)]

                stgp = Pool(ctx, scope, uq("a_stg"), 4, [128, NB * 128], BF16)
                stgd = [ctx.dsem(f"astg{i}") for i in range(4)]
                btmp = Pool(ctx, scope, uq("a_btmp"), 2, [128, NB * 128], BF16)
                for i in range(4):
                    ctx.op(POOL, lambda: nc.gpsimd.memset(stgp.t[i][:, :], 0.0), writes=[stgp.b[i]])

                def load_qk(dst_pool, typ, hl):
                    t, tb, _ = dst_pool.next()
                    cands = []
                    for j in range(2):
                        cc = typ * 8 + 4 * j + hl
                        st, stb, si = stgp.next()
                        ctx.dma(SP, stgd[si], st[:, 0:TL], RQK[cc][0:128, :], writes=[stb])
                        ctx.dma(SP, stgd[si], st[:, TL:T], RQK[cc][128:256, NM:TL], writes=[stb])
                        cands.append((st, stb))
                    tm, tmb, _ = btmp.next()
                    blend(t[:, 0:T], cands[0][0][:, 0:T], cands[1][0][:, 0:T], tm[:, 0:T], tmb, [cands[0][1], cands[1][1]], [tb])
                    return t, tb

                def load_v(vbase, hl):
                    t, tb, _ = Vh.next()
                    cands = []
                    for j in range(2):
                        col = vbase + (4 * j + hl) * 128
                        st, stb, si = stgp.next()
                        stv = st[:, :].rearrange("p (j c) -> p j c", c=128)
                        vg_, cin = divmod(col, 512)
                        ctx.dma(SP, stgd[si], st[0:NM, 0:128], RV[vg_][0][0:NM, cin:cin + 128], writes=[stb])
                        for s_ in range(2):
                            for q_, (r0, r1) in enumerate(VQ):
                                nr = r1 - r0
                                skip = NM if q_ == 0 else 0
                                gb0 = s_ * 16 + 4 * q_ + 1
                                ctx.dma(SP, stgd[si], stv[:, gb0:gb0 + 4, :],
                                        RV[vg_][q_][s_ * nr + skip:(s_ + 1) * nr, cin:cin + 128].rearrange("(j p) c -> p j c", p=128),
                                        writes=[stb])
                        cands.append((st, stb))
                    tm, tmb, _ = btmp.next()
                    blend(t[:, :], cands[0][0][:, :], cands[1][0][:, :], tm[:, :], tmb, [cands[0][1], cands[1][1]], [tb])
                    return t, tb

                def load_head(kind, hl):
                    tq = 0 if kind == 0 else 2
                    qt, qb = load_qk(qT, tq, hl)
                    kt, kb_ = load_qk(kT, tq + 1, hl)
                    vt, vb = load_v(0 if kind == 0 else 1024, hl)
                    return (qt, qb, kt, kb_, vt, vb)

                otoks = [[], []]

                def epilogue(h_feat, g, ot, ob, dt_, db_, fox):
                    c0, c1 = GROUPS[g]
                    N = c1 - c0
                    ut, ub, _ = usq.next()
                    ctx.op(ACT, lambda: nc.scalar.activation(out=ut[:, :N], in_=ot[:, :N], func=AF.Square), reads=[ob], writes=[ub])
                    st, sbf, _ = ps.next()
                    ctx.op(PE, lambda: nc.tensor.matmul(st[:, :N], ones_f, ut[:, :N], start=True, stop=True), reads=[ub], writes=[sbf])
                    lt, lb, _ = t2p.next()
                    if fox:
                        t1t, t1b, _ = t1p.next()
                        ctx.op(ACT, lambda: nc.scalar.activation(out=t1t[:, :N], in_=dt_[:, :N], func=AF.Square, scale=EPS ** 0.5),
                               reads=[db_], writes=[t1b])
                        ctx.op(DVE, lambda: nc.vector.scalar_tensor_tensor(out=lt[:, :N], in0=st[:, :N], scalar=1.0 / 128, in1=t1t[:, :N],
                                                                           op0=ALU.mult, op1=ALU.add), reads=[sbf, t1b], writes=[lb])
                        l2, l2b, _ = t2p.next()
                        ctx.op(ACT, lambda: nc.scalar.activation(out=l2[:, :N], in_=lt[:, :N], func=AF.Ln), reads=[lb], writes=[l2b])
                    else:
                        l2, l2b = lt, lb
                        ctx.op(ACT, lambda: nc.scalar.activation(out=l2[:, :N], in_=st[:, :N], func=AF.Ln, bias=EPS, scale=1.0 / 128),
                               reads=[sbf], writes=[l2b])
                    rt2, rb2, _ = rrp.next()
                    ctx.op(ACT, lambda: nc.scalar.activation(out=rt2[:, :N], in_=l2[:, :N], func=AF.Exp, scale=-0.5), reads=[l2b], writes=[rb2])
                    o_t, o_b, oi = ost.next()
                    ctx.op(DVE, lambda: nc.vector.scalar_tensor_tensor(out=o_t[:, :N], in0=ot[:, :N], scalar=cv(l, 3, h_feat), in1=rt2[:, :N],
                                                                       op0=ALU.mult, op1=ALU.mult), reads=[ob, rb2], writes=[o_b])
                    for i2 in range(2):
                        ctx.dma(SP, ostd[oi], SO[h_feat][i2][:, c0:c1], o_t[i2 * 64:(i2 + 1) * 64, :N], reads=[o_b])
                        otoks[i2].append(ctx.last_tok)
                    if g == 8:
                        for i2 in range(2):
                            all_gather(SO[h_feat][i2], RO[h_feat][i2], otoks[0] + otoks[1])
                        otoks[0].clear()
                        otoks[1].clear()

                def fox_head(h, hd_):
                    qt, qb, kt, kbb, vt, vb = hd_
                    for g in range(9):
                        c0, c1 = GROUPS[g]
                        N = c1 - c0
                        jb0, jb1 = grp_blocks(g)
                        ot, ob, _ = po.next()
                        dt_, db_, _ = pdn.next()
                        tiles = []
                        pend = None
                        for kb in range(jb1 + 2):
                            cur = None
                            if kb <= jb1:
                                ks0, ks1 = blk(kb)
                                kk = ks1 - ks0
                                diag = kb >= jb0
                                n0 = ks0 if diag else c0
                                NA = c1 - n0
                                off = n0 - c0
                                st, sbf, _ = ps.next()
                                ctx.op(PE, lambda: nc.tensor.matmul(st[:kk, :NA], kt[:, ks0:ks1], qt[:, n0:c1], start=True, stop=False),
                                       reads=[kbb, qb], writes=[sbf], sig=False)
                                ctx.op(PE, lambda: nc.tensor.matmul(st[:kk, :NA], sel24[0:3 * NHL, h * 128:h * 128 + kk], ch24[0:3 * NHL, n0:c1],
                                                                    start=False, stop=(not diag)),
                                       reads=[b_ch3], writes=[sbf], sig=(not diag))
                                if diag:
                                    ctx.op(PE, lambda: nc.tensor.matmul(st[:kk, 0:kk], ident_b[:kk, :kk], maskF_b[:kk, :kk],
                                                                        start=False, stop=True), writes=[sbf], sig=True)
                                p_t, p_b, _ = pT.next()
                                bo = (g * NB + kb) * NHL + h
                                ctx.op(ACT, lambda: nc.scalar.activation(out=p_t[:kk, :NA], in_=st[:kk, :NA], func=AF.Exp,
                                                                         bias=bias_t[:kk, bo:bo + 1], scale=1.0),
                                       reads=[sbf, b_bias], writes=[p_b])
                                cur = (kb, kk, NA, off, p_t, p_b)
                            if pend is not None:
                                kb2, kk2, NA2, off2, p2, p2b = pend
                                ctx.op(PE, lambda: nc.tensor.matmul(ot[:, off2:off2 + NA2], vt[:kk2, kb2 * 128:(kb2 + 1) * 128], p2[:kk2, :NA2],
                                                                    start=(kb2 == 0), stop=(kb2 == jb1)),
                                       reads=[vb, p2b], writes=[ob], sig=False)
                                ctx.op(PE, lambda: nc.tensor.matmul(dt_[:, off2:off2 + NA2], ones_b[:kk2, :], p2[:kk2, :NA2],
                                                                    start=(kb2 == 0), stop=(kb2 == jb1)),
                                       reads=[p2b], writes=[db_], sig=True)
                            pend = cur
                        epilogue(h, g, ot, ob, dt_, db_, True)

                def sb_head(h, hd_):
                    qt, qb, kt, kbb, vt, vb = hd_
                    for g in range(9):
                        c0, c1 = GROUPS[g]
                        N = c1 - c0
                        jb0, jb1 = grp_blocks(g)
                        ot, ob, _ = po.next()
                        ctx.op(PE, lambda: nc.tensor.matmul(ot[:, :N], zeros_b, qt[:, c0:c1], start=True, stop=False),
                               reads=[qb], writes=[ob], sig=False)
                        rs0, rs0b, _ = rs.next()
                        rs1, rs1b, _ = rs.next()
                        ctx.op(POOL, lambda: nc.gpsimd.memset(rs0[:, :], 0.0), writes=[rs0b])
                        ctx.op(POOL, lambda: nc.gpsimd.memset(rs1[:, :], 0.0), writes=[rs1b])
                        rcur, rcurb, rnxt, rnxtb = rs0, rs0b, rs1, rs1b
                        order = list(range(jb1, -1, -1))
                        n = len(order)
                        st1 = [None] * n
                        st2 = [None] * n
                        for step in range(n + 2):
                            if step < n:
                                kb = order[step]
                                ks0, ks1 = blk(kb)
                                kk = ks1 - ks0
                                diag = kb >= jb0
                                n0 = ks0 if diag else c0
                                NA = c1 - n0
                                off = n0 - c0
                                zt, zb, _ = ps.next()
                                ctx.op(PE, lambda: nc.tensor.matmul(zt[:kk, :NA], kt[:, ks0:ks1], qt[:, n0:c1], start=True, stop=False),
                                       reads=[kbb, qb], writes=[zb], sig=(not diag))
                                if diag:
                                    ctx.op(PE, lambda: nc.tensor.matmul(zt[:kk, 0:kk], ident_b[:kk, :kk], maskS_b[:kk, :kk],
                                                                        start=False, stop=False), writes=[zb], sig=True)
                                et, ebb, _ = eb.next()
                                ctx.op(ACT, lambda: nc.scalar.activation(out=et[:kk, :NA], in_=zt[:kk, :NA], func=AF.Exp),
                                       reads=[zb], writes=[ebb])
                                s_t, s_b, _ = spb.next()
                                ctx.op(ACT, lambda: nc.scalar.activation(out=s_t[:kk, :NA], in_=et[:kk, :NA], func=AF.Ln, bias=1.0, scale=1.0),
                                       reads=[ebb], writes=[s_b])
                                st1[step] = (kb, kk, NA, off, zt, zb, s_t, s_b)
                            if 1 <= step <= n:
                                i = step - 1
                                kb, kk, NA, off, zt, zb, s_t, s_b = st1[i]
                                first = (i == 0)
                                ctx.op(PE, lambda: nc.tensor.matmul(zt[:kk, :NA], negtri_b[:kk, :kk], s_t[:kk, :NA], start=False, stop=first),
                                       reads=[s_b], writes=[zb], sig=first)
                                if not first:
                                    ctx.op(PE, lambda: nc.tensor.matmul(zt[:kk, :NA], negones_b[:, :kk], rcur[:, off:off + NA],
                                                                        start=False, stop=True), reads=[rcurb], writes=[zb], sig=True)
                                a_t, a_b, _ = pT.next()
                                ctx.op(ACT, lambda: nc.scalar.activation(out=a_t[:kk, :NA], in_=zt[:kk, :NA], func=AF.Exp),
                                       reads=[zb], writes=[a_b])
                                if kb > 0:
                                    ctx.op(POOL, lambda: nc.gpsimd.tensor_tensor(out=rnxt[:, off:off + NA], in0=rcur[:, off:off + NA],
                                                                                 in1=s_t[:, :NA], op=ALU.add),
                                           reads=[rcurb, s_b], writes=[rnxtb])
                                    rcur, rcurb, rnxt, rnxtb = rnxt, rnxtb, rcur, rcurb
                                st2[i] = (kb, kk, NA, off, a_t, a_b)
                            if step >= 2:
                                i = step - 2
                                kb, kk, NA, off, a_t, a_b = st2[i]
                                ctx.op(PE, lambda: nc.tensor.matmul(ot[:, off:off + NA], vt[:kk, kb * 128:(kb + 1) * 128], a_t[:kk, :NA],
                                                                    start=False, stop=(i == n - 1)),
                                       reads=[vb, a_b], writes=[ob], sig=(i == n - 1))
                        epilogue(NHL + h, g, ot, ob, None, None, False)

                heads = [(0, hl) for hl in range(NHL)] + [(1, hl) for hl in range(NHL)]
                nxt = load_head(*heads[0])
                for i, (kind, hl) in enumerate(heads):
                    cur = nxt
                    if i + 1 < len(heads):
                        nxt = load_head(*heads[i + 1])
                    if kind == 0:
                        fox_head(hl, cur)
                    else:
                        sb_head(hl, cur)
                ctx.barrier()

        def out_phase(l):
            with ExitStack() as scope:
                on = scope.enter_context(nc.sbuf_tensor(uq("o_on"), [128, KC, TL], BF16))
                on_b = Buf("o_on")
                ostg = Pool(ctx, scope, uq("o_stg"), 2, [128, T], BF16)
                ostgd = [ctx.dsem(f"ostg{i}") for i in range(2)]
                otmp = Pool(ctx, scope, uq("o_tmp"), 2, [128, SEQ // 2], BF16)
                for k in range(KC):
                    s_, hc = divmod(k, 8)
                    st, stb, si = ostg.next()
                    for i2 in range(2):
                        ctx.dma(SP, ostgd[si], st[i2 * 64:(i2 + 1) * 64, :], RO[hc][i2][s_ * 64:(s_ + 1) * 64, :], writes=[stb])
                    ctx.op(DVE, lambda: nc.vector.tensor_copy(out=on[:, k, 0:NM], in_=st[:, 0:NM]), reads=[stb], writes=[on_b])
                    tm, tmb, _ = otmp.next()
                    blend(on[:, k, NM:TL], st[:, NM:TL], st[:, TL:T], tm[:, :], tmb, [stb], [on_b])
                wo = Pool(ctx, scope, uq("o_wo"), 2, [128, D], BF16)
                wod = [ctx.dsem(f"owo{i}") for i in range(2)]
                hr = Pool(ctx, scope, uq("o_hr"), 3, [128, 512], F32)
                hrd = [ctx.dsem(f"ohr{i}") for i in range(3)]
                ho = Pool(ctx, scope, uq("o_ho"), 3, [128, 512], F32)
                hod = [ctx.dsem(f"oho{i}") for i in range(3)]
                py = Pool(ctx, scope, uq("o_py"), 3, [128, 512], F32, psum=True)
                for dc in range(KC):
                    wt, wb, wi = wo.next()
                    ctx.dma(POOL, wod[wi], wt[:], wo_d[l * KC + dc], writes=[wb], max_dma_last_dim=4096)
                    for gi in range(5):
                        c0, c1 = GROUPS_L[gi]
                        N = c1 - c0
                        hrt, hrb, hri = hr.next()
                        ctx.dma(SP, hrd[hri], hrt[:, :N], H[dc * 128:(dc + 1) * 128, c0:c1], writes=[hrb])
                        yt, yb, _ = py.next()
                        for k in range(KC):
                            ctx.op(PE, lambda: nc.tensor.matmul(yt[:, :N], wt[:, k * 128:(k + 1) * 128], on[:, k, c0:c1],
                                                                start=(k == 0), stop=(k == KC - 1)),
                                   reads=[wb, on_b], writes=[yb], sig=(k == KC - 1))
                        hot, hob, hoi = ho.next()
                        ctx.op(DVE, lambda: nc.vector.tensor_tensor(out=hot[:, :N], in0=yt[:, :N], in1=hrt[:, :N], op=ALU.add),
                               reads=[yb, hrb], writes=[hob])
                        ctx.dma(SP, hod[hoi], H[dc * 128:(dc + 1) * 128, c0:c1], hot[:, :N], reads=[hob])
                ctx.barrier()

        def final_phase():
            with ExitStack() as scope:
                pss = Pool(ctx, scope, uq("fn_pss"), 2, [128, 512], F32, psum=True)
                norm_phase(scope, H, list(range(1, 5)), cv_final, out_dst=out_d, ps_ss=pss)
                ctx.barrier()

        for l in range(DEPTH):
            ffn_phase(l, 0, h_in if l == 0 else H, H)
            proj_phase(l)
            attn_phase(l)
            out_phase(l)
            ffn_phase(l, 1, H, H)
        final_phase()
    return nc


def _lhsT_tiles(W, ncol_chunks):
    kc = W.shape[0] // 128
    return np.ascontiguousarray(W.reshape(kc, 128, ncol_chunks, 128).transpose(2, 1, 0, 3).reshape(ncol_chunks, 128, kc * 128))


def _col(v):
    return np.ascontiguousarray(v.reshape(-1, 128).T)


_PROG = None


def kernel(x, meta_tokens, ffn1_norm, ffn1_w_gate, ffn1_w_up, ffn1_w_down, mix_norm, w_in, b_forget, g_fox, g_sb,
           w_out, ffn2_norm, ffn2_w_gate, ffn2_w_up, ffn2_w_down, final_norm):
    global _PROG
    f32 = np.float32
    x = np.asarray(x, f32)
    wgu = []
    wdn = []
    wqk = []
    wv = []
    wf = []
    wo = []
    cvecs = [[], []]
    for l in range(DEPTH):
        for (g_, u_, d_) in ((ffn1_w_gate, ffn1_w_up, ffn1_w_down), (ffn2_w_gate, ffn2_w_up, ffn2_w_down)):
            wgu.append(_lhsT_tiles(np.asarray(g_[l], f32), NF))
            wgu.append(_lhsT_tiles(np.asarray(u_[l], f32), NF))
            wdn.append(_lhsT_tiles(np.asarray(d_[l], f32), KC))
        wi = np.asarray(w_in[l], f32)
        qk_cols = np.concatenate([wi[:, 0:1024], wi[:, 1024:2048], wi[:, 3072:4096], wi[:, 4096:5120]], axis=1)
        wqk.append(_lhsT_tiles(qk_cols, 32))
        v_cols = np.concatenate([wi[:, 2048:3072], wi[:, 5120:6144]], axis=1)
        wv.append(np.ascontiguousarray(v_cols.reshape(KC, 128, 4, 512).transpose(2, 1, 0, 3).reshape(4, 128, KC * 512)))
        wf.append(np.ascontiguousarray(wi[:, 6144:6152].reshape(KC, 128, 8).transpose(1, 0, 2).reshape(128, KC * 8)))
        wperm = [0, 1, 2, 3, 8, 9, 10, 11, 4, 5, 6, 7, 12, 13, 14, 15]
        wo.append(_lhsT_tiles(np.ascontiguousarray(np.asarray(w_out[l], f32).reshape(KC, 128, D)[wperm].reshape(D, D)), KC))
        gf = _col(np.asarray(g_fox[l], f32))
        gs = _col(np.asarray(g_sb[l], f32))
        for r in range(2):
            cv_ = cvecs[r]
            cv_.append(_col(np.asarray(ffn1_norm[l], f32)))
            cv_.append(_col(np.asarray(mix_norm[l], f32)))
            cv_.append(_col(np.asarray(ffn2_norm[l], f32)))
            cv_.append(np.concatenate([gf[:, 4 * r:4 * r + 4], gs[:, 4 * r:4 * r + 4], np.zeros((128, 8), f32)], axis=1))
            cv_.append(np.ascontiguousarray(np.broadcast_to(np.asarray(b_forget[l], f32)[None, :], (128, 8))))
    for r in range(2):
        cvecs[r].append(_col(np.asarray(final_norm, f32)))
        cvecs[r] = np.ascontiguousarray(np.concatenate(cvecs[r], axis=1))
    wgu = np.concatenate(wgu, axis=0)
    wdn = np.concatenate(wdn, axis=0)
    wqk = np.concatenate(wqk, axis=0)
    wv = np.concatenate(wv, axis=0)
    wf = np.stack(wf, axis=0)
    wo = np.concatenate(wo, axis=0)
    p = np.arange(128)
    ones = np.ones((128, 128), f32)
    triu = (p[:, None] <= p[None, :]).astype(f32)
    ident = np.eye(128, dtype=f32)
    cf32 = np.ascontiguousarray(np.concatenate([ones, triu, ident], axis=1))
    negtri = -(p[:, None] >= p[None, :]).astype(f32)
    maskF = np.where(p[:, None] <= p[None, :], 0.0, NEG).astype(f32)
    maskS = np.where(p[:, None] < p[None, :], 0.0, NEG).astype(f32)
    sel = np.zeros((128, 8 * 128), f32)
    for hh in range(8):
        sel[3 * hh:3 * hh + 3, hh * 128:(hh + 1) * 128] = 1.0
    cbf = np.ascontiguousarray(np.concatenate([ones, -ones, negtri, ident, maskF, maskS, np.zeros((128, 128), f32), sel], axis=1))
    meta = np.asarray(meta_tokens, f32)
    if _PROG is None:
        _PROG = build_program()
    nc = _PROG
    in_maps = []
    for c in range(8):
        b, r = divmod(c, 2)
        hT = np.ascontiguousarray(np.concatenate([meta, x[b, r * (SEQ // 2):(r + 1) * (SEQ // 2)]], axis=0).T)
        rsel = np.ascontiguousarray(np.broadcast_to(np.array([1.0 - r, float(r)], f32)[None, :], (128, 2)))
        in_maps.append({"h_in": hT, "wgu": wgu, "wdn": wdn, "wqk": wqk, "wv": wv, "wf": wf, "wo": wo,
                        "cvec": cvecs[r], "cf32": cf32, "cbf": cbf, "rsel": rsel})
    res = run_bass_kernel_spmd(nc, in_maps, core_ids=list(range(8)))
    out = np.empty((4, SEQ, D), f32)
    for c in range(8):
        b, r = divmod(c, 2)
        out[b, r * (SEQ // 2):(r + 1) * (SEQ // 2), :] = res.results[c]["outT"].T
    return out
```

```python
import numpy as np
from contextlib import ExitStack
import concourse.bass as bass
import concourse.mybir as mybir
from concourse.bass_utils import run_bass_kernel_spmd

F32 = mybir.dt.float32
BF16 = mybir.dt.bfloat16
AF = mybir.ActivationFunctionType
ALU = mybir.AluOpType

D = 2048
NM = 16
SEQ = 4096
T = SEQ + NM
DFF = 5632
NF = DFF // 128
KC = D // 128
NH = 8
NB = 33
DEPTH = 2
EPS = 1e-6
NEG = -30000.0

GROUPS = [(0, NM)] + [(NM + 512 * i, NM + 512 * (i + 1)) for i in range(8)]
TL = NM + SEQ // 2
NBL = 17
NHL = 4
GROUPS_L = [(0, NM)] + [(NM + 512 * i, NM + 512 * (i + 1)) for i in range(4)]
SGS = [[0, 1, 2], [3, 4]]
PAIRS = [[0, 1], [2, 3], [4, 5], [6, 7]]


def blk(j):
    return (0, NM) if j == 0 else (NM + 128 * (j - 1), NM + 128 * j)


def grp_blocks(g):
    return (0, 0) if g == 0 else (4 * (g - 1) + 1, 4 * g)


class Buf:
    __slots__ = ("name", "w", "r")

    def __init__(self, name):
        self.name = name
        self.w = None
        self.r = {}


class Eng:
    def __init__(self, nc, es, name, e):
        self.name = name
        self.e = e
        self.sem = es.enter_context(nc.semaphore("s_" + name))
        self.key = "E_" + name
        self.cnt = 0
        self.waited = {}

    def wait(self, tok):
        sem, val, key = tok
        if self.waited.get(key, 0) >= val:
            return
        self.e.wait_ge(sem, val)
        self.waited[key] = val


class DSem:
    def __init__(self, nc, es, name):
        self.sem = es.enter_context(nc.semaphore("d_" + name))
        self.key = "D_" + name
        self.cnt = 0


class Ctx:
    def __init__(self, nc, es):
        self.nc = nc
        self.es = es
        self.pe = Eng(nc, es, "pe", nc.tensor)
        self.act = Eng(nc, es, "act", nc.scalar)
        self.dve = Eng(nc, es, "dve", nc.vector)
        self.pool = Eng(nc, es, "pool", nc.gpsimd)
        self.sp = Eng(nc, es, "sp", nc.sync)
        self.engs = [self.pe, self.act, self.dve, self.pool, self.sp]
        self.dsems = []
        self.dcache = {}

    def dsem(self, name):
        if name in self.dcache:
            return self.dcache[name]
        d = DSem(self.nc, self.es, name)
        self.dsems.append(d)
        self.dcache[name] = d
        return d

    def _deps(self, E, reads, writes):
        for b in reads:
            if b.w is not None:
                if b.w[2] == E.key and E is self.pe:
                    continue
                E.wait(b.w)
        for b in writes:
            if b.w is not None and b.w[2] != E.key:
                E.wait(b.w)
            for tok in b.r.values():
                if tok[2] != E.key:
                    E.wait(tok)

    def _record(self, tok, reads, writes):
        for b in reads:
            old = b.r.get(tok[2])
            if old is None or old[1] < tok[1]:
                b.r[tok[2]] = tok
        for b in writes:
            b.w = tok
            b.r = {}

    def op(self, E, fn, reads=(), writes=(), sig=True):
        self._deps(E, reads, writes)
        ins = fn()
        if sig:
            E.cnt += 1
            ins.then_inc(E.sem, 1)
            tok = (E.sem, E.cnt, E.key)
        else:
            tok = (E.sem, E.cnt + 1, E.key)
        self._record(tok, reads, writes)
        return ins

    def dma(self, Q, ds, out, in_, reads=(), writes=(), **kw):
        self._deps(Q, reads, writes)
        ins = Q.e.dma_start(out=out, in_=in_, **kw)
        ds.cnt += 16
        ins.then_inc(ds.sem, 16)
        tok = (ds.sem, ds.cnt, ds.key)
        self._record(tok, reads, writes)
        self.last_tok = tok
        return ins

    def barrier(self):
        for E in self.engs:
            for O in self.engs:
                if O is not E and O.cnt > 0:
                    E.wait((O.sem, O.cnt, O.key))
            for d in self.dsems:
                if d.cnt > 0:
                    E.wait((d.sem, d.cnt, d.key))


class Pool:
    def __init__(self, ctx, es, name, n, shape, dtype, psum=False, dma=False):
        nc = ctx.nc
        self.t = []
        self.b = []
        self.d = []
        for i in range(n):
            nm = f"{name}{i}"
            if psum:
                t = es.enter_context(nc.psum_tensor(nm, shape, dtype))
            else:
                t = es.enter_context(nc.sbuf_tensor(nm, shape, dtype))
            self.t.append(t)
            self.b.append(Buf(nm))
        self.n = n
        self.i = -1

    def next(self):
        self.i = (self.i + 1) % self.n
        return self.t[self.i], self.b[self.i], self.i


def build_program(dbg=None):
    nc = bass.Bass("TRN2", target_bir_lowering=False)
    dt = nc.dram_tensor
    h_in = dt("h_in", [D, TL], F32, kind="ExternalInput").ap()
    wgu_d = dt("wgu", [DEPTH * 2 * 2 * NF, 128, D], F32, kind="ExternalInput").ap()
    wdn_d = dt("wdn", [DEPTH * 2 * KC, 128, DFF], F32, kind="ExternalInput").ap()
    wqk_d = dt("wqk", [DEPTH * 32, 128, D], F32, kind="ExternalInput").ap()
    wv_d = dt("wv", [DEPTH * 4, 128, KC * 512], F32, kind="ExternalInput").ap()
    wf_d = dt("wf", [DEPTH, 128, KC * 8], F32, kind="ExternalInput").ap()
    wo_d = dt("wo", [DEPTH * KC, 128, D], F32, kind="ExternalInput").ap()
    NCV = DEPTH * (4 * KC + 8) + KC
    cvec_d = dt("cvec", [128, NCV], F32, kind="ExternalInput").ap()
    cf32_d = dt("cf32", [128, 3 * 128], F32, kind="ExternalInput").ap()
    cbf_d = dt("cbf", [128, 15 * 128], F32, kind="ExternalInput").ap()
    rsel_d = dt("rsel", [128, 2], F32, kind="ExternalInput").ap()
    out_d = dt("outT", [D, SEQ // 2], F32, kind="ExternalOutput").ap()
    H = dt("Hs", [D, TL], F32, kind="Internal").ap()
    VQ = [(0, 528), (528, 1040), (1040, 1552), (1552, 2064)]
    SQK = [dt(f"SQK{c}", [128, TL], BF16, kind="Internal").ap() for c in range(32)]
    RQK = [dt(f"RQK{c}", [256, TL], BF16, kind="Internal").ap() for c in range(32)]
    SV = [[dt(f"SV{v}_{q}", [r1 - r0, 512], BF16, kind="Internal").ap() for q, (r0, r1) in enumerate(VQ)] for v in range(4)]
    RV = [[dt(f"RV{v}_{q}", [2 * (r1 - r0), 512], BF16, kind="Internal").ap() for q, (r0, r1) in enumerate(VQ)] for v in range(4)]
    SLF = dt("SLF", [128, NBL * 8], F32, kind="Internal").ap()
    RLF = dt("RLF", [2 * 128, NBL * 8], F32, kind="Internal").ap()
    SO = [[dt(f"SO{h}_{i}", [64, T], BF16, kind="Internal").ap() for i in range(2)] for h in range(8)]
    RO = [[dt(f"RO{h}_{i}", [128, T], BF16, kind="Internal").ap() for i in range(2)] for h in range(8)]

    with ExitStack() as es:
        ctx = Ctx(nc, es)
        PE, ACT, DVE, POOL, SP = ctx.pe, ctx.act, ctx.dve, ctx.pool, ctx.sp
        sb = lambda name, shape, dtp: es.enter_context(nc.sbuf_tensor(name, shape, dtp))
        _uqc = [0]

        def uq(n):
            _uqc[0] += 1
            return f"{n}_{_uqc[0]}_"

        cvec = sb("cvec_t", [128, NCV], F32)
        cf32 = sb("cf32_t", [128, 3 * 128], F32)
        cbf = sb("cbf_t", [128, 15 * 128], BF16)
        ones3 = sb("ones3_t", [3, 128], BF16)
        lfseq = sb("lfseq_t", [128, NB * NHL], F32)
        rsel = sb("rsel_t", [128, 2], F32)
        b_const = Buf("const")
        b_lf = Buf("lfseq")
        dconst = ctx.dsem("const")
        ctx.dma(SP, dconst, cvec[:], cvec_d[:, :], writes=[b_const])
        ctx.dma(SP, dconst, cf32[:], cf32_d[:, :], writes=[b_const])
        ctx.dma(SP, dconst, rsel[:], rsel_d[:, :], writes=[b_const])
        dconst2 = ctx.dsem("const2")
        ctx.dma(POOL, dconst2, cbf[:], cbf_d[:, :], writes=[b_const])
        ctx.dma(POOL, dconst2, ones3[:], cbf_d[0:3, 0:128], writes=[b_const])
        ctx.barrier()
        ones_f = cf32[:, 0:128]
        triu_f = cf32[:, 128:256]
        ident_f = cf32[:, 256:384]
        ones_b = cbf[:, 0:128]
        negones_b = cbf[:, 128:256]
        negtri_b = cbf[:, 256:384]
        ident_b = cbf[:, 384:512]
        maskF_b = cbf[:, 512:640]
        maskS_b = cbf[:, 640:768]
        zeros_b = cbf[:, 768:896]
        sel24 = cbf[:, 896:1920]
        m0 = rsel[:, 0:1]
        m1 = rsel[:, 1:2]
        ccsem = ctx.dsem("cc")

        def all_gather(src, dst, toks):
            for t in toks:
                POOL.wait(t)
            ins = nc.gpsimd.collective_compute("AllGather", ALU.bypass, replica_groups=PAIRS, ins=[src], outs=[dst])
            ins.then_inc(ccsem.sem)
            ccsem.cnt += 1

        def blend(out, a, b, tmp, tmp_b, reads, writes, eng=None):
            ctx.op(DVE, lambda: nc.vector.tensor_scalar(out=tmp, in0=b, scalar1=m1, scalar2=None, op0=ALU.mult),
                   reads=reads, writes=[tmp_b])
            ctx.op(DVE, lambda: nc.vector.scalar_tensor_tensor(out=out, in0=a, scalar=m0, in1=tmp, op0=ALU.mult, op1=ALU.add),
                   reads=list(reads) + [tmp_b], writes=writes)

        def cv(l, which, k):
            base = l * (4 * KC + 8) + which * KC + k
            return cvec[:, base:base + 1]

        def cv_bf(l):
            base = l * (4 * KC + 8) + 4 * KC
            return cvec[:, base:base + 8]

        def cv_final(k):
            base = DEPTH * (4 * KC + 8) + k
            return cvec[:, base:base + 1]

        def ffn_phase(l, which, src, dst):
            wrow_gu = ((l * 2 + which) * 2) * NF
            wrow_dn = (l * 2 + which) * KC
            with ExitStack() as scope:
                xn = scope.enter_context(nc.sbuf_tensor(uq("f_xn"), [128, KC, 1040], BF16))
                xn_b = Buf("f_xn")
                A = scope.enter_context(nc.sbuf_tensor(uq("f_A"), [128, NF, 1040], BF16))
                A_b = [Buf(f"A{f}") for f in range(NF)]
                wg = Pool(ctx, scope, uq("f_wg"), 2, [128, D], BF16)
                wu = Pool(ctx, scope, uq("f_wu"), 2, [128, D], BF16)
                wgd = [ctx.dsem(f"wg{i}") for i in range(2)]
                wd = Pool(ctx, scope, uq("f_wd"), 2, [128, DFF], BF16)
                wdd = [ctx.dsem(f"wd{i}") for i in range(2)]
                sg = Pool(ctx, scope, uq("f_sg"), 2, [128, 512], F32)
                hr = Pool(ctx, scope, uq("f_hr"), 2, [128, 512], F32)
                hrd = [ctx.dsem(f"hr{i}") for i in range(2)]
                ho = Pool(ctx, scope, uq("f_ho"), 2, [128, 512], F32)
                hod = [ctx.dsem(f"ho{i}") for i in range(2)]
                pg = Pool(ctx, scope, uq("f_pg"), 2, [128, 512], F32, psum=True)
                pu = Pool(ctx, scope, uq("f_pu"), 2, [128, 512], F32, psum=True)
                py = Pool(ctx, scope, uq("f_py"), 2, [128, 512], F32, psum=True)
                pss = Pool(ctx, scope, uq("f_pss"), 2, [128, 512], F32, psum=True)
                for sgl in SGS:
                    norm_phase(scope, src, sgl, lambda k: cv(l, 0 if which == 0 else 2, k),
                               xn=xn, xn_b=xn_b, ps_ss=pss, NP=128)
                    offs = []
                    o = 0
                    for gi in sgl:
                        offs.append(o)
                        o += GROUPS_L[gi][1] - GROUPS_L[gi][0]
                    for f in range(NF):
                        wgt, wgb, wi = wg.next()
                        wut, wub, _ = wu.next()
                        ctx.dma(POOL, wgd[wi], wgt[:], wgu_d[wrow_gu + f], writes=[wgb], max_dma_last_dim=4096)
                        ctx.dma(POOL, wgd[wi], wut[:], wgu_d[wrow_gu + NF + f], writes=[wub], max_dma_last_dim=4096)
                        for gi, off in zip(sgl, offs):
                            N = GROUPS_L[gi][1] - GROUPS_L[gi][0]
                            gt, gb, _ = pg.next()
                            ut, ub, _ = pu.next()
                            for k in range(KC):
                                ctx.op(PE, lambda: nc.tensor.matmul(gt[:, :N], wgt[:, k * 128:(k + 1) * 128], xn[:, k, off:off + N],
                                                                    start=(k == 0), stop=(k == KC - 1)),
                                       reads=[wgb, xn_b], writes=[gb], sig=(k == KC - 1))
                            for k in range(KC):
                                ctx.op(PE, lambda: nc.tensor.matmul(ut[:, :N], wut[:, k * 128:(k + 1) * 128], xn[:, k, off:off + N],
                                                                    start=(k == 0), stop=(k == KC - 1)),
                                       reads=[wub, xn_b], writes=[ub], sig=(k == KC - 1))
                            st, sbf, _ = sg.next()
                            ctx.op(ACT, lambda: nc.scalar.activation(out=st[:, :N], in_=gt[:, :N], func=AF.Silu),
                                   reads=[gb], writes=[sbf])
                            ctx.op(DVE, lambda: nc.vector.tensor_tensor(out=A[:, f, off:off + N], in0=st[:, :N], in1=ut[:, :N], op=ALU.mult),
                                   reads=[sbf, ub], writes=[A_b[f]])
                    for dc in range(KC):
                        wdt, wdb, wi = wd.next()
                        ctx.dma(POOL, wdd[wi], wdt[:], wdn_d[wrow_dn + dc], writes=[wdb], max_dma_last_dim=4096)
                        for gi, off in zip(sgl, offs):
                            c0, c1 = GROUPS_L[gi]
                            N = c1 - c0
                            hrt, hrb, hri = hr.next()
                            ctx.dma(SP, hrd[hri], hrt[:, :N], src[dc * 128:(dc + 1) * 128, c0:c1], writes=[hrb])
                            yt, yb, _ = py.next()
                            for f in range(NF):
                                ctx.op(PE, lambda: nc.tensor.matmul(yt[:, :N], wdt[:, f * 128:(f + 1) * 128], A[:, f, off:off + N],
                                                                    start=(f == 0), stop=(f == NF - 1)),
                                       reads=[wdb, A_b[f]], writes=[yb], sig=(f == NF - 1))
                            hot, hob, hoi = ho.next()
                            ctx.op(DVE, lambda: nc.vector.scalar_tensor_tensor(out=hot[:, :N], in0=yt[:, :N], scalar=0.5, in1=hrt[:, :N],
                                                                               op0=ALU.mult, op1=ALU.add),
                                   reads=[yb, hrb], writes=[hob])
                            ctx.dma(SP, hod[hoi], dst[dc * 128:(dc + 1) * 128, c0:c1], hot[:, :N], reads=[hob])
                ctx.barrier()

        _norm_cache = {}

        def norm_phase(scope, src, groups, gain_fn, xn=None, xn_b=None, xoff0=0, out_dst=None, ps_ss=None, NP=256):
            c = getattr(scope, "_norm_tiles", None)
            if c is None:
                c = {}
                c["hst"] = Pool(ctx, scope, uq("hst"), 2, [128, KC, NP], F32)
                c["sq"] = Pool(ctx, scope, uq("nsq"), 2, [128, NP], F32)
                c["lnb"] = Pool(ctx, scope, uq("nln"), 2, [128, NP], F32)
                c["rsb"] = Pool(ctx, scope, uq("nrs"), 2, [128, NP], F32)
                if out_dst is not None:
                    c["ost"] = Pool(ctx, scope, uq("nost"), 3, [128, NP], F32)
                scope._norm_tiles = c
            hst, sq, lnb, rsb = c["hst"], c["sq"], c["lnb"], c["rsb"]
            ost = c.get("ost")
            xoff = xoff0
            for gi in groups:
                c0, c1 = GROUPS_L[gi]
                for p0 in range(c0, c1, NP):
                    p1 = min(c1, p0 + NP)
                    N = p1 - p0
                    ht, hb, hi = hst.next()
                    ctx.dma(SP, g_hst_d[hi], ht[:, :, :N],
                            src[:, p0:p1].rearrange("(k p) n -> p k n", p=128), writes=[hb])
                    st, sbf, _ = ps_ss.next()
                    for k in range(KC):
                        qt, qb, _ = sq.next()
                        ctx.op(ACT, lambda: nc.scalar.activation(out=qt[:, :N], in_=ht[:, k, :N], func=AF.Square),
                               reads=[hb], writes=[qb])
                        ctx.op(PE, lambda: nc.tensor.matmul(st[:, :N], ones_f, qt[:, :N], start=(k == 0), stop=(k == KC - 1)),
                               reads=[qb], writes=[sbf], sig=True)
                    lt, lb, _ = lnb.next()
                    ctx.op(ACT, lambda: nc.scalar.activation(out=lt[:, :N], in_=st[:, :N], func=AF.Ln, bias=EPS, scale=1.0 / D),
                           reads=[sbf], writes=[lb])
                    rt, rb, _ = rsb.next()
                    ctx.op(ACT, lambda: nc.scalar.activation(out=rt[:, :N], in_=lt[:, :N], func=AF.Exp, scale=-0.5),
                           reads=[lb], writes=[rb])
                    for k in range(KC):
                        if out_dst is None:
                            ctx.op(DVE, lambda: nc.vector.scalar_tensor_tensor(
                                out=xn[:, k, xoff:xoff + N], in0=ht[:, k, :N], scalar=gain_fn(k), in1=rt[:, :N],
                                op0=ALU.mult, op1=ALU.mult), reads=[hb, rb], writes=[xn_b])
                        else:
                            ot, ob, oi = ost.next()
                            ctx.op(DVE, lambda: nc.vector.scalar_tensor_tensor(
                                out=ot[:, :N], in0=ht[:, k, :N], scalar=gain_fn(k), in1=rt[:, :N],
                                op0=ALU.mult, op1=ALU.mult), reads=[hb, rb], writes=[ob])
                            ctx.dma(SP, g_ost_d[oi], out_dst[k * 128:(k + 1) * 128, p0 - NM:p1 - NM], ot[:, :N], reads=[ob])
                    xoff += N

        g_hst_d = [ctx.dsem(f"ghst{i}") for i in range(2)]
        g_ost_d = [ctx.dsem(f"gost{i}") for i in range(3)]

        def proj_phase(l):
            with ExitStack() as scope:
                xn = scope.enter_context(nc.sbuf_tensor(uq("p_xn"), [128, KC, TL], BF16))
                xn_b = Buf("p_xn")
                pss = Pool(ctx, scope, uq("p_pss"), 2, [128, 512], F32, psum=True)
                pq = Pool(ctx, scope, uq("p_pq"), 3, [128, 512], F32, psum=True)
                with ExitStack() as nscope:
                    norm_phase(nscope, H, list(range(5)), lambda k: cv(l, 1, k), xn=xn, xn_b=xn_b, ps_ss=pss)
                    ctx.barrier()
                wq = Pool(ctx, scope, uq("p_wq"), 2, [128, D], BF16)
                wqd = [ctx.dsem(f"pwq{i}") for i in range(2)]
                stg = Pool(ctx, scope, uq("p_stg"), 3, [128, 512], BF16)
                stgd = [ctx.dsem(f"pstg{i}") for i in range(3)]
                def load_wq(cc_):
                    wt_, wb_, wi_ = wq.next()
                    ctx.dma(POOL, wqd[wi_], wt_[:], wqk_d[l * 32 + cc_], writes=[wb_], max_dma_last_dim=4096)
                    return wt_, wb_

                wnext = load_wq(0)
                for cc in range(32):
                    wt, wb = wnext
                    is_q = (cc // 8) in (0, 2)
                    qtoks = []
                    for gi in range(5):
                        c0, c1 = GROUPS_L[gi]
                        N = c1 - c0
                        pt, pb, _ = pq.next()
                        for k in range(KC):
                            ctx.op(PE, lambda: nc.tensor.matmul(pt[:, :N], wt[:, k * 128:(k + 1) * 128], xn[:, k, c0:c1],
                                                                start=(k == 0), stop=(k == KC - 1)),
                                   reads=[wb, xn_b], writes=[pb], sig=(k == KC - 1))
                        st, sbf, si = stg.next()
                        scale = (128.0 ** -0.5) if is_q else 1.0
                        if (gi + cc) % 2 == 0:
                            ctx.op(ACT, lambda: nc.scalar.activation(out=st[:, :N], in_=pt[:, :N], func=AF.Copy, scale=scale),
                                   reads=[pb], writes=[sbf])
                        else:
                            ctx.op(DVE, lambda: nc.vector.tensor_scalar(out=st[:, :N], in0=pt[:, :N], scalar1=scale, scalar2=None,
                                                                        op0=ALU.mult), reads=[pb], writes=[sbf])
                        ctx.dma(SP, stgd[si], SQK[cc][:, c0:c1], st[:, :N], reads=[sbf])
                        qtoks.append(ctx.last_tok)
                    if cc + 1 < 32:
                        wnext = load_wq(cc + 1)
                    all_gather(SQK[cc], RQK[cc], qtoks)
                wv = Pool(ctx, scope, uq("p_wv"), 2, [128, KC * 512], BF16)
                wvd = [ctx.dsem(f"pwv{i}") for i in range(2)]
                def load_wv(vg_):
                    wt_, wb_, wi_ = wv.next()
                    ctx.dma(POOL, wvd[wi_], wt_[:], wv_d[l * 4 + vg_], writes=[wb_], max_dma_last_dim=4096)
                    return wt_, wb_

                wvnext = load_wv(0)
                for vg in range(4):
                    wt, wb = wvnext
                    vtoks = [[], [], [], []]
                    for j in range(NBL):
                        t0, t1 = blk(j)
                        kk = t1 - t0
                        pt, pb, _ = pq.next()
                        for k in range(KC):
                            ctx.op(PE, lambda: nc.tensor.matmul(pt[:kk, :512], xn[:, k, t0:t1], wt[:, k * 512:(k + 1) * 512],
                                                                start=(k == 0), stop=(k == KC - 1)),
                                   reads=[wb, xn_b], writes=[pb], sig=(k == KC - 1))
                        st, sbf, si = stg.next()
                        if j % 2 == 0:
                            ctx.op(ACT, lambda: nc.scalar.activation(out=st[:kk, :512], in_=pt[:kk, :512], func=AF.Copy),
                                   reads=[pb], writes=[sbf])
                        else:
                            ctx.op(DVE, lambda: nc.vector.tensor_copy(out=st[:kk, :512], in_=pt[:kk, :512]), reads=[pb], writes=[sbf])
                        q_ = 0 if j <= 4 else (j - 1) // 4
                        ctx.dma(SP, stgd[si], SV[vg][q_][t0 - VQ[q_][0]:t1 - VQ[q_][0], :], st[:kk, :512], reads=[sbf])
                        vtoks[q_].append(ctx.last_tok)
                    if vg + 1 < 4:
                        wvnext = load_wv(vg + 1)
                    for q_ in range(4):
                        all_gather(SV[vg][q_], RV[vg][q_], vtoks[q_])
                wf = scope.enter_context(nc.sbuf_tensor(uq("p_wf"), [128, KC * 8], BF16))
                wfb = Buf("p_wf")
                wfd = ctx.dsem("pwf")
                ctx.dma(POOL, wfd, wf[:], wf_d[l], writes=[wfb])
                ft = Pool(ctx, scope, uq("p_ft"), 2, [128, 8], F32)
                lfl = scope.enter_context(nc.sbuf_tensor(uq("p_lfl"), [128, NBL * 8], F32))
                b_lfl = Buf("lfl")
                ctx.op(DVE, lambda: nc.vector.memset(lfl[:], 0.0), writes=[b_lfl])
                for j in range(NBL):
                    t0, t1 = blk(j)
                    kk = t1 - t0
                    pt, pb, _ = pq.next()
                    for k in range(KC):
                        ctx.op(PE, lambda: nc.tensor.matmul(pt[:kk, :8], xn[:, k, t0:t1], wf[:, k * 8:(k + 1) * 8],
                                                            start=(k == 0), stop=(k == KC - 1)),
                               reads=[wfb, xn_b], writes=[pb], sig=(k == KC - 1))
                    f1, f1b, _ = ft.next()
                    ctx.op(DVE, lambda: nc.vector.tensor_tensor(out=f1[:kk, :], in0=pt[:kk, :8], in1=cv_bf(l)[:kk, :], op=ALU.add),
                           reads=[pb], writes=[f1b])
                    f2, f2b, _ = ft.next()
                    ctx.op(ACT, lambda: nc.scalar.activation(out=f2[:kk, :], in_=f1[:kk, :], func=AF.Exp, scale=-1.0),
                           reads=[f1b], writes=[f2b])
                    f3, f3b, _ = ft.next()
                    ctx.op(ACT, lambda: nc.scalar.activation(out=f3[:kk, :], in_=f2[:kk, :], func=AF.Ln, bias=1.0, scale=1.0),
                           reads=[f2b], writes=[f3b])
                    ctx.op(DVE, lambda: nc.vector.tensor_scalar(out=lfl[:kk, j * 8:(j + 1) * 8], in0=f3[:kk, :], scalar1=-1.0,
                                                                scalar2=None, op0=ALU.mult), reads=[f3b], writes=[b_lfl])
                ctx.dma(SP, ctx.dsem("slf"), SLF[:, :], lfl[:, :], reads=[b_lfl])
                all_gather(SLF, RLF, [ctx.last_tok])
                ctx.barrier()

        def attn_phase(l):
            with ExitStack() as scope:
                bias_t = scope.enter_context(nc.sbuf_tensor(uq("a_bias"), [128, 9 * NB * NHL], F32))
                ch24 = scope.enter_context(nc.sbuf_tensor(uq("a_ch24"), [3 * NHL, T], BF16))
                qT = Pool(ctx, scope, uq("a_qT"), 2, [128, T], BF16)
                kT = Pool(ctx, scope, uq("a_kT"), 2, [128, T], BF16)
                Vh = Pool(ctx, scope, uq("a_Vh"), 2, [128, NB * 128], BF16)
                stgp = Pool(ctx, scope, uq("a_stg"), 6, [128, NB * 128], BF16)
                stgd = [ctx.dsem(f"astg{i}") for i in range(6)]
                for i in range(6):
                    ctx.op(POOL, lambda: nc.gpsimd.memset(stgp.t[i][:, :], 0.0), writes=[stgp.b[i]])

                def load_qk(dst_pool, typ, hl):
                    t, tb, _ = dst_pool.next()
                    cands = []
                    for j in range(2):
                        cc = typ * 8 + 4 * j + hl
                        st, stb, si = stgp.next()
                        ctx.dma(SP, stgd[si], st[:, 0:TL], RQK[cc][0:128, :], writes=[stb])
                        ctx.dma(SP, stgd[si], st[:, TL:T], RQK[cc][128:256, NM:TL], writes=[stb])
                        cands.append((st, stb))

                    def fin():
                        tm, tmb, _ = btmp.next()
                        blend(t[:, 0:T], cands[0][0][:, 0:T], cands[1][0][:, 0:T], tm[:, 0:T], tmb, [cands[0][1], cands[1][1]], [tb])
                    return t, tb, fin

                def load_v(vbase, hl):
                    t, tb, _ = Vh.next()
                    cands = []
                    for j in range(2):
                        col = vbase + (4 * j + hl) * 128
                        st, stb, si = stgp.next()
                        stv = st[:, :].rearrange("p (j c) -> p j c", c=128)
                        vg_, cin = divmod(col, 512)
                        ctx.dma(SP, stgd[si], st[0:NM, 0:128], RV[vg_][0][0:NM, cin:cin + 128], writes=[stb])
                        for s_ in range(2):
                            for q_, (r0, r1) in enumerate(VQ):
                                nr = r1 - r0
                                skip = NM if q_ == 0 else 0
                                gb0 = s_ * 16 + 4 * q_ + 1
                                ctx.dma(SP, stgd[si], stv[:, gb0:gb0 + 4, :],
                                        RV[vg_][q_][s_ * nr + skip:(s_ + 1) * nr, cin:cin + 128].rearrange("(j p) c -> p j c", p=128),
                                        writes=[stb])
                        cands.append((st, stb))

                    def fin():
                        tm, tmb, _ = btmp.next()
                        blend(t[:, :], cands[0][0][:, :], cands[1][0][:, :], tm[:, :], tmb, [cands[0][1], cands[1][1]], [tb])
                    return t, tb, fin

                def load_head(kind, hl):
                    tq = 0 if kind == 0 else 2
                    qt, qb, f1_ = load_qk(qT, tq, hl)
                    kt, kb_, f2_ = load_qk(kT, tq + 1, hl)
                    vt, vb, f3_ = load_v(0 if kind == 0 else 1024, hl)

                    def fin():
                        f1_()
                        f2_()
                        f3_()
                    return (qt, qb, kt, kb_, vt, vb), fin

                heads = [(0, hl) for hl in range(NHL)] + [(1, hl) for hl in range(NHL)]
                nxt, nfin = load_head(*heads[0])
                with ExitStack() as pscope:
                    pm = Pool(ctx, pscope, uq("a_pm"), 2, [128, 512], F32, psum=True)
                    L0 = pscope.enter_context(nc.sbuf_tensor(uq("a_L0"), [128, NBL * 8], F32))
                    L1 = pscope.enter_context(nc.sbuf_tensor(uq("a_L1"), [128, NBL * 8], F32))
                    ltmp = pscope.enter_context(nc.sbuf_tensor(uq("a_ltmp"), [128, NBL * NHL], F32))
                    b_L, b_ltmp = Buf("L01"), Buf("ltmp")
                    dlf = ctx.dsem("rlf")
                    ctx.dma(SP, dlf, L0[:, :], RLF[0:128, :], writes=[b_L])
                    ctx.dma(SP, dlf, L1[:, :], RLF[128:256, :], writes=[b_L])
                    L0v = L0[:, :].rearrange("p (b h) -> p b h", h=8)
                    L1v = L1[:, :].rearrange("p (b h) -> p b h", h=8)
                    lfv = lfseq[:, :].rearrange("p (b h) -> p b h", h=NHL)
                    ltv = ltmp[:, :].rearrange("p (b h) -> p b h", h=NHL)
                    blend(lfv[:, 0:NBL, :], L0v[:, :, 0:NHL], L0v[:, :, NHL:8], ltv[:, 0:NBL, :], b_ltmp, [b_L], [b_lf])
                    blend(lfv[:, NBL:NB, :], L1v[:, 1:NBL, 0:NHL], L1v[:, 1:NBL, NHL:8], ltv[:, 0:NBL - 1, :], b_ltmp, [b_L], [b_lf])
                    ctok = pscope.enter_context(nc.sbuf_tensor(uq("a_ctok"), [128, NB * NHL], F32))
                    tot = pscope.enter_context(nc.sbuf_tensor(uq("a_tot"), [128, NB * NHL], F32))
                    pex = pscope.enter_context(nc.sbuf_tensor(uq("a_pex"), [128, NB * NHL], F32))
                    chm = pscope.enter_context(nc.sbuf_tensor(uq("a_chm"), [NHL, T], F32))
                    r1 = pscope.enter_context(nc.sbuf_tensor(uq("a_r1"), [NHL, T], F32))
                    cb3 = [pscope.enter_context(nc.sbuf_tensor(uq(f"a_cb{i}"), [NHL, T], BF16)) for i in range(3)]
                    rhm = pscope.enter_context(nc.sbuf_tensor(uq("a_rhm"), [NHL, 18], F32))
                    b_ctok, b_tot, b_pex, b_bias, b_chm, b_r1, b_rhm, b_ch3 = (Buf(n) for n in
                                                                               ("ctok", "tot", "pex", "bias", "chm", "r1", "rhm", "ch3"))
                    b_cb3 = [Buf(f"cb{i}") for i in range(3)]
                    wt_, wb_, _ = pm.next()
                    ctx.op(PE, lambda: nc.tensor.matmul(wt_[:, :NB * NHL], triu_f, lfseq[:, :], start=True, stop=True),
                           reads=[b_lf], writes=[wb_])
                    tt_, tb_, _ = pm.next()
                    ctx.op(PE, lambda: nc.tensor.matmul(tt_[:, :NB * NHL], ones_f, lfseq[:, :], start=True, stop=True),
                           reads=[b_lf], writes=[tb_])
                    ctx.op(DVE, lambda: nc.vector.tensor_copy(out=tot[:, :], in_=tt_[:, :NB * NHL]), reads=[tb_], writes=[b_tot])
                    ctx.op(DVE, lambda: nc.vector.memset(pex[:, 0:NHL], 0.0), writes=[b_pex])
                    for j in range(1, NB):
                        ctx.op(DVE, lambda: nc.vector.tensor_tensor(out=pex[:, j * NHL:(j + 1) * NHL], in0=pex[:, (j - 1) * NHL:j * NHL],
                                                                    in1=tot[:, (j - 1) * NHL:j * NHL], op=ALU.add),
                               reads=[b_tot, b_pex], writes=[b_pex])
                    ctx.op(DVE, lambda: nc.vector.tensor_tensor(out=ctok[:, :], in0=wt_[:, :NB * NHL], in1=pex[:, :], op=ALU.add),
                           reads=[wb_, b_pex], writes=[b_ctok])
                    for g in range(9):
                        jb0, jb1 = grp_blocks(g)
                        for kb in range(jb1 + 1):
                            o = (g * NB + kb) * NHL
                            ctx.op(DVE, lambda: nc.vector.tensor_tensor(out=bias_t[:, o:o + NHL], in0=pex[:, jb0 * NHL:(jb0 + 1) * NHL],
                                                                        in1=ctok[:, kb * NHL:(kb + 1) * NHL], op=ALU.subtract),
                                   reads=[b_pex, b_ctok], writes=[b_bias])
                    rt_, rb_, _ = pm.next()
                    for g in range(9):
                        jb0, _ = grp_blocks(g)
                        ctx.op(PE, lambda: nc.tensor.matmul(rt_[0:NHL, 2 * g:2 * g + 2], pex[:, jb0 * NHL:(jb0 + 1) * NHL], ident_f[:, 0:2],
                                                            start=True, stop=True), reads=[b_pex], writes=[rb_])
                    ctx.op(DVE, lambda: nc.vector.tensor_copy(out=rhm[:, 0:18], in_=rt_[0:NHL, 0:18]), reads=[rb_], writes=[b_rhm])
                    for g in range(9):
                        c0, c1 = GROUPS[g]
                        jb0, jb1 = grp_blocks(g)
                        ct_, cbb_, _ = pm.next()
                        for j in range(jb0, jb1 + 1):
                            t0, t1 = blk(j)
                            kk = t1 - t0
                            ctx.op(PE, lambda: nc.tensor.matmul(ct_[0:NHL, t0 - c0:t1 - c0], ctok[:kk, j * NHL:(j + 1) * NHL], ident_f[:kk, :kk],
                                                                start=True, stop=True), reads=[b_ctok], writes=[cbb_])
                        rcol = rhm[:, 2 * g:2 * g + 1]
                        ctx.op(DVE, lambda: nc.vector.tensor_scalar(out=chm[:, c0:c1], in0=ct_[0:NHL, 0:c1 - c0], scalar1=rcol, scalar2=None,
                                                                    op0=ALU.subtract), reads=[cbb_, b_rhm], writes=[b_chm])
                    ctx.op(DVE, lambda: nc.vector.tensor_copy(out=cb3[0][:, :], in_=chm[:, :]), reads=[b_chm], writes=[b_cb3[0]])
                    ctx.op(DVE, lambda: nc.vector.tensor_tensor(out=r1[:, :], in0=chm[:, :], in1=cb3[0][:, :], op=ALU.subtract),
                           reads=[b_chm, b_cb3[0]], writes=[b_r1])
                    ctx.op(DVE, lambda: nc.vector.tensor_copy(out=cb3[1][:, :], in_=r1[:, :]), reads=[b_r1], writes=[b_cb3[1]])
                    ctx.op(DVE, lambda: nc.vector.tensor_tensor(out=chm[:, :], in0=r1[:, :], in1=cb3[1][:, :], op=ALU.subtract),
                           reads=[b_r1, b_cb3[1]], writes=[b_chm])
                    ctx.op(DVE, lambda: nc.vector.tensor_copy(out=cb3[2][:, :], in_=chm[:, :]), reads=[b_chm], writes=[b_cb3[2]])
                    dch = ctx.dsem(f"ch3_{l}")
                    for i in range(3):
                        for h in range(NHL):
                            ctx.dma(SP, dch, ch24[3 * h + i:3 * h + i + 1, :], cb3[i][h:h + 1, :], reads=[b_cb3[i]], writes=[b_ch3])

                    ctx.barrier()
                hd = [ctx.dsem(f"ahd{i}") for i in range(2)]
                ps = Pool(ctx, scope, uq("a_ps"), 3, [128, 512], F32, psum=True)
                po = Pool(ctx, scope, uq("a_po"), 2, [128, 512], F32, psum=True)
                pdn = Pool(ctx, scope, uq("a_pd"), 1, [128, 512], F32, psum=True)
                pT = Pool(ctx, scope, uq("a_pT"), 3, [128, 512], BF16)
                eb = Pool(ctx, scope, uq("a_eb"), 2, [128, 512], F32)
                spb = Pool(ctx, scope, uq("a_sp"), 3, [128, 512], BF16)
                rs = Pool(ctx, scope, uq("a_rs"), 2, [128, 512], BF16)
                usq = Pool(ctx, scope, uq("a_usq"), 2, [128, 512], F32)
                t1p = Pool(ctx, scope, uq("a_t1"), 2, [128, 512], F32)
                t2p = Pool(ctx, scope, uq("a_t2"), 2, [128, 512], F32)
                rrp = Pool(ctx, scope, uq("a_rr"), 2, [128, 512], F32)
                ost = Pool(ctx, scope, uq("a_ost"), 2, [128, 512], BF16)
                ostd = [ctx.dsem(f"aost{i}") for i in range(2)]

                btmp = Pool(ctx, scope, uq("a_btmp"), 1, [128, NB * 128], BF16)
                otoks = [[], []]

                def epilogue(h_feat, g, ot, ob, dt_, db_, fox):
                    c0, c1 = GROUPS[g]
                    N = c1 - c0
                    ut, ub, _ = usq.next()
                    ctx.op(ACT, lambda: nc.scalar.activation(out=ut[:, :N], in_=ot[:, :N], func=AF.Square), reads=[ob], writes=[ub])
                    st, sbf, _ = ps.next()
                    ctx.op(PE, lambda: nc.tensor.matmul(st[:, :N], ones_f, ut[:, :N], start=True, stop=True), reads=[ub], writes=[sbf])
                    lt, lb, _ = t2p.next()
                    if fox:
                        t1t, t1b, _ = t1p.next()
                        ctx.op(ACT, lambda: nc.scalar.activation(out=t1t[:, :N], in_=dt_[:, :N], func=AF.Square, scale=EPS ** 0.5),
                               reads=[db_], writes=[t1b])
                        ctx.op(DVE, lambda: nc.vector.scalar_tensor_tensor(out=lt[:, :N], in0=st[:, :N], scalar=1.0 / 128, in1=t1t[:, :N],
                                                                           op0=ALU.mult, op1=ALU.add), reads=[sbf, t1b], writes=[lb])
                        l2, l2b, _ = t2p.next()
                        ctx.op(ACT, lambda: nc.scalar.activation(out=l2[:, :N], in_=lt[:, :N], func=AF.Ln), reads=[lb], writes=[l2b])
                    else:
                        l2, l2b = lt, lb
                        ctx.op(ACT, lambda: nc.scalar.activation(out=l2[:, :N], in_=st[:, :N], func=AF.Ln, bias=EPS, scale=1.0 / 128),
                               reads=[sbf], writes=[l2b])
                    rt2, rb2, _ = rrp.next()
                    ctx.op(ACT, lambda: nc.scalar.activation(out=rt2[:, :N], in_=l2[:, :N], func=AF.Exp, scale=-0.5), reads=[l2b], writes=[rb2])
                    o_t, o_b, oi = ost.next()
                    ctx.op(DVE, lambda: nc.vector.scalar_tensor_tensor(out=o_t[:, :N], in0=ot[:, :N], scalar=cv(l, 3, h_feat), in1=rt2[:, :N],
                                                                       op0=ALU.mult, op1=ALU.mult), reads=[ob, rb2], writes=[o_b])
                    for i2 in range(2):
                        ctx.dma(SP, ostd[oi], SO[h_feat][i2][:, c0:c1], o_t[i2 * 64:(i2 + 1) * 64, :N], reads=[o_b])
                        otoks[i2].append(ctx.last_tok)
                    if g == 8:
                        for i2 in range(2):
                            all_gather(SO[h_feat][i2], RO[h_feat][i2], otoks[0] + otoks[1])
                        otoks[0].clear()
                        otoks[1].clear()

                def fox_head(h, hd_, mid_cb):
                    qt, qb, kt, kbb, vt, vb = hd_
                    for g in range(9):
                        if g == 5:
                            mid_cb()
                        c0, c1 = GROUPS[g]
                        N = c1 - c0
                        jb0, jb1 = grp_blocks(g)
                        ot, ob, _ = po.next()
                        dt_, db_, _ = pdn.next()
                        tiles = []
                        pend = None
                        for kb in range(jb1 + 2):
                            cur = None
                            if kb <= jb1:
                                ks0, ks1 = blk(kb)
                                kk = ks1 - ks0
                                diag = kb >= jb0
                                n0 = ks0 if diag else c0
                                NA = c1 - n0
                                off = n0 - c0
                                st, sbf, _ = ps.next()
                                ctx.op(PE, lambda: nc.tensor.matmul(st[:kk, :NA], kt[:, ks0:ks1], qt[:, n0:c1], start=True, stop=False),
                                       reads=[kbb, qb], writes=[sbf], sig=False)
                                ctx.op(PE, lambda: nc.tensor.matmul(st[:kk, :NA], sel24[0:3 * NHL, h * 128:h * 128 + kk], ch24[0:3 * NHL, n0:c1],
                                                                    start=False, stop=(not diag)),
                                       reads=[b_ch3], writes=[sbf], sig=(not diag))
                                if diag:
                                    ctx.op(PE, lambda: nc.tensor.matmul(st[:kk, 0:kk], ident_b[:kk, :kk], maskF_b[:kk, :kk],
                                                                        start=False, stop=True), writes=[sbf], sig=True)
                                p_t, p_b, _ = pT.next()
                                bo = (g * NB + kb) * NHL + h
                                ctx.op(ACT, lambda: nc.scalar.activation(out=p_t[:kk, :NA], in_=st[:kk, :NA], func=AF.Exp,
                                                                         bias=bias_t[:kk, bo:bo + 1], scale=1.0),
                                       reads=[sbf, b_bias], writes=[p_b])
                                cur = (kb, kk, NA, off, p_t, p_b)
                            if pend is not None:
                                kb2, kk2, NA2, off2, p2, p2b = pend
                                ctx.op(PE, lambda: nc.tensor.matmul(ot[:, off2:off2 + NA2], vt[:kk2, kb2 * 128:(kb2 + 1) * 128], p2[:kk2, :NA2],
                                                                    start=(kb2 == 0), stop=(kb2 == jb1)),
                                       reads=[vb, p2b], writes=[ob], sig=False)
                                ctx.op(PE, lambda: nc.tensor.matmul(dt_[:, off2:off2 + NA2], ones_b[:kk2, :], p2[:kk2, :NA2],
                                                                    start=(kb2 == 0), stop=(kb2 == jb1)),
                                       reads=[p2b], writes=[db_], sig=True)
                            pend = cur
                        epilogue(h, g, ot, ob, dt_, db_, True)

                def sb_head(h, hd_, mid_cb):
                    qt, qb, kt, kbb, vt, vb = hd_
                    for g in range(9):
                        if g == 5:
                            mid_cb()
                        c0, c1 = GROUPS[g]
                        N = c1 - c0
                        jb0, jb1 = grp_blocks(g)
                        ot, ob, _ = po.next()
                        ctx.op(PE, lambda: nc.tensor.matmul(ot[:, :N], zeros_b, qt[:, c0:c1], start=True, stop=False),
                               reads=[qb], writes=[ob], sig=False)
                        rs0, rs0b, _ = rs.next()
                        rs1, rs1b, _ = rs.next()
                        ctx.op(POOL, lambda: nc.gpsimd.memset(rs0[:, :], 0.0), writes=[rs0b])
                        ctx.op(POOL, lambda: nc.gpsimd.memset(rs1[:, :], 0.0), writes=[rs1b])
                        rcur, rcurb, rnxt, rnxtb = rs0, rs0b, rs1, rs1b
                        order = list(range(jb1, -1, -1))
                        n = len(order)
                        st1 = [None] * n
                        st2 = [None] * n
                        for step in range(n + 2):
                            if step < n:
                                kb = order[step]
                                ks0, ks1 = blk(kb)
                                kk = ks1 - ks0
                                diag = kb >= jb0
                                n0 = ks0 if diag else c0
                                NA = c1 - n0
                                off = n0 - c0
                                zt, zb, _ = ps.next()
                                ctx.op(PE, lambda: nc.tensor.matmul(zt[:kk, :NA], kt[:, ks0:ks1], qt[:, n0:c1], start=True, stop=False),
                                       reads=[kbb, qb], writes=[zb], sig=(not diag))
                                if diag:
                                    ctx.op(PE, lambda: nc.tensor.matmul(zt[:kk, 0:kk], ident_b[:kk, :kk], maskS_b[:kk, :kk],
                                                                        start=False, stop=False), writes=[zb], sig=True)
                                et, ebb, _ = eb.next()
                                ctx.op(ACT, lambda: nc.scalar.activation(out=et[:kk, :NA], in_=zt[:kk, :NA], func=AF.Exp),
                                       reads=[zb], writes=[ebb])
                                s_t, s_b, _ = spb.next()
                                ctx.op(ACT, lambda: nc.scalar.activation(out=s_t[:kk, :NA], in_=et[:kk, :NA], func=AF.Ln, bias=1.0, scale=1.0),
                                       reads=[ebb], writes=[s_b])
                                st1[step] = (kb, kk, NA, off, zt, zb, s_t, s_b)
                            if 1 <= step <= n:
                                i = step - 1
                                kb, kk, NA, off, zt, zb, s_t, s_b = st1[i]
                                first = (i == 0)
                                ctx.op(PE, lambda: nc.tensor.matmul(zt[:kk, :NA], negtri_b[:kk, :kk], s_t[:kk, :NA], start=False, stop=first),
                                       reads=[s_b], writes=[zb], sig=first)
                                if not first:
                                    ctx.op(PE, lambda: nc.tensor.matmul(zt[:kk, :NA], negones_b[:, :kk], rcur[:, off:off + NA],
                                                                        start=False, stop=True), reads=[rcurb], writes=[zb], sig=True)
                                a_t, a_b, _ = pT.next()
                                ctx.op(ACT, lambda: nc.scalar.activation(out=a_t[:kk, :NA], in_=zt[:kk, :NA], func=AF.Exp),
                                       reads=[zb], writes=[a_b])
                                if kb > 0:
                                    ctx.op(POOL, lambda: nc.gpsimd.tensor_tensor(out=rnxt[:, off:off + NA], in0=rcur[:, off:off + NA],
                                                                                 in1=s_t[:, :NA], op=ALU.add),
                                           reads=[rcurb, s_b], writes=[rnxtb])
                                    rcur, rcurb, rnxt, rnxtb = rnxt, rnxtb, rcur, rcurb
                                st2[i] = (kb, kk, NA, off, a_t, a_b)
                            if step >= 2:
                                i = step - 2
                                kb, kk, NA, off, a_t, a_b = st2[i]
                                ctx.op(PE, lambda: nc.tensor.matmul(ot[:, off:off + NA], vt[:kk, kb * 128:(kb + 1) * 128], a_t[:kk, :NA],
                                                                    start=False, stop=(i == n - 1)),
                                       reads=[vb, a_b], writes=[ob], sig=(i == n - 1))
                        epilogue(NHL + h, g, ot, ob, None, None, False)

                nfin()
                for i, (kind, hl) in enumerate(heads):
                    cur = nxt
                    if i + 1 < len(heads):
                        nxt, nfin = load_head(*heads[i + 1])
                    else:
                        nfin = lambda: None
                    if kind == 0:
                        fox_head(hl, cur, nfin)
                    else:
                        sb_head(hl, cur, nfin)
                ctx.barrier()

        def out_phase(l):
            with ExitStack() as scope:
                on = scope.enter_context(nc.sbuf_tensor(uq("o_on"), [128, KC, TL], BF16))
                on_b = Buf("o_on")
                ostg = Pool(ctx, scope, uq("o_stg"), 2, [128, T], BF16)
                ostgd = [ctx.dsem(f"ostg{i}") for i in range(2)]
                otmp = Pool(ctx, scope, uq("o_tmp"), 2, [128, SEQ // 2], BF16)
                for k in range(KC):
                    s_, hc = divmod(k, 8)
                    st, stb, si = ostg.next()
                    for i2 in range(2):
                        ctx.dma(SP, ostgd[si], st[i2 * 64:(i2 + 1) * 64, :], RO[hc][i2][s_ * 64:(s_ + 1) * 64, :], writes=[stb])
                    ctx.op(DVE, lambda: nc.vector.tensor_copy(out=on[:, k, 0:NM], in_=st[:, 0:NM]), reads=[stb], writes=[on_b])
                    tm, tmb, _ = otmp.next()
                    blend(on[:, k, NM:TL], st[:, NM:TL], st[:, TL:T], tm[:, :], tmb, [stb], [on_b])
                wo = Pool(ctx, scope, uq("o_wo"), 2, [128, D], BF16)
                wod = [ctx.dsem(f"owo{i}") for i in range(2)]
                hr = Pool(ctx, scope, uq("o_hr"), 3, [128, 512], F32)
                hrd = [ctx.dsem(f"ohr{i}") for i in range(3)]
                ho = Pool(ctx, scope, uq("o_ho"), 3, [128, 512], F32)
                hod = [ctx.dsem(f"oho{i}") for i in range(3)]
                py = Pool(ctx, scope, uq("o_py"), 3, [128, 512], F32, psum=True)
                for dc in range(KC):
                    wt, wb, wi = wo.next()
                    ctx.dma(POOL, wod[wi], wt[:], wo_d[l * KC + dc], writes=[wb], max_dma_last_dim=4096)
                    for gi in range(5):
                        c0, c1 = GROUPS_L[gi]
                        N = c1 - c0
                        hrt, hrb, hri = hr.next()
                        ctx.dma(SP, hrd[hri], hrt[:, :N], H[dc * 128:(dc + 1) * 128, c0:c1], writes=[hrb])
                        yt, yb, _ = py.next()
                        for k in range(KC):
                            ctx.op(PE, lambda: nc.tensor.matmul(yt[:, :N], wt[:, k * 128:(k + 1) * 128], on[:, k, c0:c1],
                                                                start=(k == 0), stop=(k == KC - 1)),
                                   reads=[wb, on_b], writes=[yb], sig=(k == KC - 1))
                        hot, hob, hoi = ho.next()
                        ctx.op(DVE, lambda: nc.vector.tensor_tensor(out=hot[:, :N], in0=yt[:, :N], in1=hrt[:, :N], op=ALU.add),
                               reads=[yb, hrb], writes=[hob])
                        ctx.dma(SP, hod[hoi], H[dc * 128:(dc + 1) * 128, c0:c1], hot[:, :N], reads=[hob])
                ctx.barrier()

        def final_phase():
            with ExitStack() as scope:
                pss = Pool(ctx, scope, uq("fn_pss"), 2, [128, 512], F32, psum=True)
                norm_phase(scope, H, list(range(1, 5)), cv_final, out_dst=out_d, ps_ss=pss)
                ctx.barrier()

        for l in range(DEPTH):
            ffn_phase(l, 0, h_in if l == 0 else H, H)
            proj_phase(l)
            attn_phase(l)
            out_phase(l)
            ffn_phase(l, 1, H, H)
        final_phase()
    return nc


def _lhsT_tiles(W, ncol_chunks):
    kc = W.shape[0] // 128
    return np.ascontiguousarray(W.reshape(kc, 128, ncol_chunks, 128).transpose(2, 1, 0, 3).reshape(ncol_chunks, 128, kc * 128))


def _col(v):
    return np.ascontiguousarray(v.reshape(-1, 128).T)


_PROG = None


def kernel(x, meta_tokens, ffn1_norm, ffn1_w_gate, ffn1_w_up, ffn1_w_down, mix_norm, w_in, b_forget, g_fox, g_sb,
           w_out, ffn2_norm, ffn2_w_gate, ffn2_w_up, ffn2_w_down, final_norm):
    global _PROG
    f32 = np.float32
    x = np.asarray(x, f32)
    wgu = []
    wdn = []
    wqk = []
    wv = []
    wf = []
    wo = []
    cvecs = [[], []]
    for l in range(DEPTH):
        for (g_, u_, d_) in ((ffn1_w_gate, ffn1_w_up, ffn1_w_down), (ffn2_w_gate, ffn2_w_up, ffn2_w_down)):
            wgu.append(_lhsT_tiles(np.asarray(g_[l], f32), NF))
            wgu.append(_lhsT_tiles(np.asarray(u_[l], f32), NF))
            wdn.append(_lhsT_tiles(np.asarray(d_[l], f32), KC))
        wi = np.asarray(w_in[l], f32)
        qk_cols = np.concatenate([wi[:, 0:1024], wi[:, 1024:2048], wi[:, 3072:4096], wi[:, 4096:5120]], axis=1)
        wqk.append(_lhsT_tiles(qk_cols, 32))
        v_cols = np.concatenate([wi[:, 2048:3072], wi[:, 5120:6144]], axis=1)
        wv.append(np.ascontiguousarray(v_cols.reshape(KC, 128, 4, 512).transpose(2, 1, 0, 3).reshape(4, 128, KC * 512)))
        wf.append(np.ascontiguousarray(wi[:, 6144:6152].reshape(KC, 128, 8).transpose(1, 0, 2).reshape(128, KC * 8)))
        wperm = [0, 1, 2, 3, 8, 9, 10, 11, 4, 5, 6, 7, 12, 13, 14, 15]
        wo.append(_lhsT_tiles(np.ascontiguousarray(np.asarray(w_out[l], f32).reshape(KC, 128, D)[wperm].reshape(D, D)), KC))
        gf = _col(np.asarray(g_fox[l], f32))
        gs = _col(np.asarray(g_sb[l], f32))
        for r in range(2):
            cv_ = cvecs[r]
            cv_.append(_col(np.asarray(ffn1_norm[l], f32)))
            cv_.append(_col(np.asarray(mix_norm[l], f32)))
            cv_.append(_col(np.asarray(ffn2_norm[l], f32)))
            cv_.append(np.concatenate([gf[:, 4 * r:4 * r + 4], gs[:, 4 * r:4 * r + 4], np.zeros((128, 8), f32)], axis=1))
            cv_.append(np.ascontiguousarray(np.broadcast_to(np.asarray(b_forget[l], f32)[None, :], (128, 8))))
    for r in range(2):
        cvecs[r].append(_col(np.asarray(final_norm, f32)))
        cvecs[r] = np.ascontiguousarray(np.concatenate(cvecs[r], axis=1))
    wgu = np.concatenate(wgu, axis=0)
    wdn = np.concatenate(wdn, axis=0)
    wqk = np.concatenate(wqk, axis=0)
    wv = np.concatenate(wv, axis=0)
    wf = np.stack(wf, axis=0)
    wo = np.concatenate(wo, axis=0)
    p = np.arange(128)
    ones = np.ones((128, 128), f32)
    triu = (p[:, None] <= p[None, :]).astype(f32)
    ident = np.eye(128, dtype=f32)
    cf32 = np.ascontiguousarray(np.concatenate([ones, triu, ident], axis=1))
    negtri = -(p[:, None] >= p[None, :]).astype(f32)
    maskF = np.where(p[:, None] <= p[None, :], 0.0, NEG).astype(f32)
    maskS = np.where(p[:, None] < p[None, :], 0.0, NEG).astype(f32)
    sel = np.zeros((128, 8 * 128), f32)
    for hh in range(8):
        sel[3 * hh:3 * hh + 3, hh * 128:(hh + 1) * 128] = 1.0
    cbf = np.ascontiguousarray(np.concatenate([ones, -ones, negtri, ident, maskF, maskS, np.zeros((128, 128), f32), sel], axis=1))
    meta = np.asarray(meta_tokens, f32)
    if _PROG is None:
        _PROG = build_program()
    nc = _PROG
    in_maps = []
    for c in range(8):
        b, r = divmod(c, 2)
        hT = np.ascontiguousarray(np.concatenate([meta, x[b, r * (SEQ // 2):(r + 1) * (SEQ // 2)]], axis=0).T)
        rsel = np.ascontiguousarray(np.broadcast_to(np.array([1.0 - r, float(r)], f32)[None, :], (128, 2)))
        in_maps.append({"h_in": hT, "wgu": wgu, "wdn": wdn, "wqk": wqk, "wv": wv, "wf": wf, "wo": wo,
                        "cvec": cvecs[r], "cf32": cf32, "cbf": cbf, "rsel": rsel})
    res = run_bass_kernel_spmd(nc, in_maps, core_ids=list(range(8)))
    out = np.empty((4, SEQ, D), f32)
    for c in range(8):
        b, r = divmod(c, 2)
        out[b, r * (SEQ // 2):(r + 1) * (SEQ // 2), :] = res.results[c]["outT"].T
    return out
```

```python
import numpy as np
from contextlib import ExitStack
import concourse.bass as bass
import concourse.mybir as mybir
from concourse.bass_utils import run_bass_kernel_spmd

F32 = mybir.dt.float32
BF16 = mybir.dt.bfloat16
AF = mybir.ActivationFunctionType
ALU = mybir.AluOpType

D = 2048
NM = 16
SEQ = 4096
T = SEQ + NM
DFF = 5632
NF = DFF // 128
KC = D // 128
NH = 8
NB = 33
DEPTH = 2
EPS = 1e-6
NEG = -30000.0

GROUPS = [(0, NM)] + [(NM + 512 * i, NM + 512 * (i + 1)) for i in range(8)]
TL = NM + SEQ // 2
NBL = 17
NHL = 4
GROUPS_L = [(0, NM)] + [(NM + 512 * i, NM + 512 * (i + 1)) for i in range(4)]
SGS = [[0, 1, 2], [3, 4]]
PAIRS = [[0, 1], [2, 3], [4, 5], [6, 7]]


def blk(j):
    return (0, NM) if j == 0 else (NM + 128 * (j - 1), NM + 128 * j)


def grp_blocks(g):
    return (0, 0) if g == 0 else (4 * (g - 1) + 1, 4 * g)


class Buf:
    __slots__ = ("name", "w", "r")

    def __init__(self, name):
        self.name = name
        self.w = None
        self.r = {}


class Eng:
    def __init__(self, nc, es, name, e):
        self.name = name
        self.e = e
        self.sem = es.enter_context(nc.semaphore("s_" + name))
        self.key = "E_" + name
        self.cnt = 0
        self.waited = {}

    def wait(self, tok):
        sem, val, key = tok
        if self.waited.get(key, 0) >= val:
            return
        self.e.wait_ge(sem, val)
        self.waited[key] = val


class DSem:
    def __init__(self, nc, es, name):
        self.sem = es.enter_context(nc.semaphore("d_" + name))
        self.key = "D_" + name
        self.cnt = 0


class Ctx:
    def __init__(self, nc, es):
        self.nc = nc
        self.es = es
        self.pe = Eng(nc, es, "pe", nc.tensor)
        self.act = Eng(nc, es, "act", nc.scalar)
        self.dve = Eng(nc, es, "dve", nc.vector)
        self.pool = Eng(nc, es, "pool", nc.gpsimd)
        self.sp = Eng(nc, es, "sp", nc.sync)
        self.engs = [self.pe, self.act, self.dve, self.pool, self.sp]
        self.dsems = []
        self.dcache = {}

    def dsem(self, name):
        if name in self.dcache:
            return self.dcache[name]
        d = DSem(self.nc, self.es, name)
        self.dsems.append(d)
        self.dcache[name] = d
        return d

    def _deps(self, E, reads, writes, dkey=None):
        for b in reads:
            if b.w is not None:
                if b.w[2] == E.key and E is self.pe:
                    continue
                E.wait(b.w)
        for b in writes:
            if b.w is not None and b.w[2] != E.key and b.w[2] != dkey:
                E.wait(b.w)
            for tok in b.r.values():
                if tok[2] != E.key:
                    E.wait(tok)

    def _record(self, tok, reads, writes):
        for b in reads:
            old = b.r.get(tok[2])
            if old is None or old[1] < tok[1]:
                b.r[tok[2]] = tok
        for b in writes:
            b.w = tok
            b.r = {}

    def op(self, E, fn, reads=(), writes=(), sig=True):
        self._deps(E, reads, writes)
        ins = fn()
        if sig:
            E.cnt += 1
            ins.then_inc(E.sem, 1)
            tok = (E.sem, E.cnt, E.key)
        else:
            tok = (E.sem, E.cnt + 1, E.key)
        self._record(tok, reads, writes)
        return ins

    def dma(self, Q, ds, out, in_, reads=(), writes=(), **kw):
        self._deps(Q, reads, writes, dkey=ds.key)
        ins = Q.e.dma_start(out=out, in_=in_, **kw)
        ds.cnt += 16
        ins.then_inc(ds.sem, 16)
        tok = (ds.sem, ds.cnt, ds.key)
        self._record(tok, reads, writes)
        self.last_tok = tok
        return ins

    def barrier(self):
        for E in self.engs:
            for O in self.engs:
                if O is not E and O.cnt > 0:
                    E.wait((O.sem, O.cnt, O.key))
            for d in self.dsems:
                if d.cnt > 0:
                    E.wait((d.sem, d.cnt, d.key))


class Pool:
    def __init__(self, ctx, es, name, n, shape, dtype, psum=False, dma=False):
        nc = ctx.nc
        self.t = []
        self.b = []
        self.d = []
        for i in range(n):
            nm = f"{name}{i}"
            if psum:
                t = es.enter_context(nc.psum_tensor(nm, shape, dtype))
            else:
                t = es.enter_context(nc.sbuf_tensor(nm, shape, dtype))
            self.t.append(t)
            self.b.append(Buf(nm))
        self.n = n
        self.i = -1

    def next(self):
        self.i = (self.i + 1) % self.n
        return self.t[self.i], self.b[self.i], self.i


def build_program(dbg=None):
    nc = bass.Bass("TRN2", target_bir_lowering=False)
    dt = nc.dram_tensor
    h_in = dt("h_in", [D, TL], F32, kind="ExternalInput").ap()
    wgu_d = dt("wgu", [DEPTH * 2 * 2 * NF, 128, D], F32, kind="ExternalInput").ap()
    wdn_d = dt("wdn", [DEPTH * 2 * KC, 128, DFF], F32, kind="ExternalInput").ap()
    wqk_d = dt("wqk", [DEPTH * 32, 128, D], F32, kind="ExternalInput").ap()
    wv_d = dt("wv", [DEPTH * 4, 128, KC * 512], F32, kind="ExternalInput").ap()
    wf_d = dt("wf", [DEPTH, 128, KC * 8], F32, kind="ExternalInput").ap()
    wo_d = dt("wo", [DEPTH * KC, 128, D], F32, kind="ExternalInput").ap()
    NCV = DEPTH * (4 * KC + 8) + KC
    cvec_d = dt("cvec", [128, NCV], F32, kind="ExternalInput").ap()
    cf32_d = dt("cf32", [128, 3 * 128], F32, kind="ExternalInput").ap()
    cbf_d = dt("cbf", [128, 15 * 128], F32, kind="ExternalInput").ap()
    rsel_d = dt("rsel", [128, 2], F32, kind="ExternalInput").ap()
    out_d = dt("outT", [D, SEQ // 2], F32, kind="ExternalOutput").ap()
    H = dt("Hs", [D, TL], F32, kind="Internal").ap()
    VQ = [(0, 528), (528, 1040), (1040, 1552), (1552, 2064)]
    SQK = [dt(f"SQK{c}", [128, TL], BF16, kind="Internal").ap() for c in range(32)]
    RQK = [dt(f"RQK{c}", [256, TL], BF16, kind="Internal").ap() for c in range(32)]
    SV = [[dt(f"SV{v}_{q}", [r1 - r0, 512], BF16, kind="Internal").ap() for q, (r0, r1) in enumerate(VQ)] for v in range(4)]
    RV = [[dt(f"RV{v}_{q}", [2 * (r1 - r0), 512], BF16, kind="Internal").ap() for q, (r0, r1) in enumerate(VQ)] for v in range(4)]
    SLF = dt("SLF", [128, NBL * 8], F32, kind="Internal").ap()
    RLF = dt("RLF", [2 * 128, NBL * 8], F32, kind="Internal").ap()
    SO = [[dt(f"SO{h}_{i}", [64, T], BF16, kind="Internal").ap() for i in range(2)] for h in range(8)]
    RO = [[dt(f"RO{h}_{i}", [128, T], BF16, kind="Internal").ap() for i in range(2)] for h in range(8)]

    with ExitStack() as es:
        ctx = Ctx(nc, es)
        PE, ACT, DVE, POOL, SP = ctx.pe, ctx.act, ctx.dve, ctx.pool, ctx.sp
        sb = lambda name, shape, dtp: es.enter_context(nc.sbuf_tensor(name, shape, dtp))
        _uqc = [0]

        def uq(n):
            _uqc[0] += 1
            return f"{n}_{_uqc[0]}_"

        cvec = sb("cvec_t", [128, NCV], F32)
        cf32 = sb("cf32_t", [128, 3 * 128], F32)
        cbf = sb("cbf_t", [128, 15 * 128], BF16)
        ones3 = sb("ones3_t", [3, 128], BF16)
        lfseq = sb("lfseq_t", [128, NB * NHL], F32)
        rsel = sb("rsel_t", [128, 2], F32)
        b_const = Buf("const")
        b_lf = Buf("lfseq")
        dconst = ctx.dsem("const")
        ctx.dma(SP, dconst, cvec[:], cvec_d[:, :], writes=[b_const])
        ctx.dma(SP, dconst, cf32[:], cf32_d[:, :], writes=[b_const])
        ctx.dma(SP, dconst, rsel[:], rsel_d[:, :], writes=[b_const])
        dconst2 = ctx.dsem("const2")
        ctx.dma(POOL, dconst2, cbf[:], cbf_d[:, :], writes=[b_const])
        ctx.dma(POOL, dconst2, ones3[:], cbf_d[0:3, 0:128], writes=[b_const])
        ctx.barrier()
        ones_f = cf32[:, 0:128]
        triu_f = cf32[:, 128:256]
        ident_f = cf32[:, 256:384]
        ones_b = cbf[:, 0:128]
        negones_b = cbf[:, 128:256]
        negtri_b = cbf[:, 256:384]
        ident_b = cbf[:, 384:512]
        maskF_b = cbf[:, 512:640]
        maskS_b = cbf[:, 640:768]
        zeros_b = cbf[:, 768:896]
        sel24 = cbf[:, 896:1920]
        m0 = rsel[:, 0:1]
        m1 = rsel[:, 1:2]
        ccsem = ctx.dsem("cc")

        def all_gather(src, dst, toks):
            for t in toks:
                POOL.wait(t)
            ins = nc.gpsimd.collective_compute("AllGather", ALU.bypass, replica_groups=PAIRS, ins=[src], outs=[dst])
            ins.then_inc(ccsem.sem)
            ccsem.cnt += 1

        def blend(out, a, b, tmp, tmp_b, reads, writes, eng=None):
            ctx.op(DVE, lambda: nc.vector.tensor_scalar(out=tmp, in0=b, scalar1=m1, scalar2=None, op0=ALU.mult),
                   reads=reads, writes=[tmp_b])
            ctx.op(DVE, lambda: nc.vector.scalar_tensor_tensor(out=out, in0=a, scalar=m0, in1=tmp, op0=ALU.mult, op1=ALU.add),
                   reads=list(reads) + [tmp_b], writes=writes)

        def cv(l, which, k):
            base = l * (4 * KC + 8) + which * KC + k
            return cvec[:, base:base + 1]

        def cv_bf(l):
            base = l * (4 * KC + 8) + 4 * KC
            return cvec[:, base:base + 8]

        def cv_final(k):
            base = DEPTH * (4 * KC + 8) + k
            return cvec[:, base:base + 1]

        def ffn_phase(l, which, src, dst):
            wrow_gu = ((l * 2 + which) * 2) * NF
            wrow_dn = (l * 2 + which) * KC
            with ExitStack() as scope:
                xn = scope.enter_context(nc.sbuf_tensor(uq("f_xn"), [128, KC, 1040], BF16))
                xn_b = Buf("f_xn")
                A = scope.enter_context(nc.sbuf_tensor(uq("f_A"), [128, NF, 1040], BF16))
                A_b = [Buf(f"A{f}") for f in range(NF)]
                wg = Pool(ctx, scope, uq("f_wg"), 2, [128, D], BF16)
                wu = Pool(ctx, scope, uq("f_wu"), 2, [128, D], BF16)
                wgd = [ctx.dsem(f"wg{i}") for i in range(2)]
                wud = [ctx.dsem(f"wu{i}") for i in range(2)]
                wd = Pool(ctx, scope, uq("f_wd"), 2, [128, DFF], BF16)
                wdd = [ctx.dsem(f"wd{i}") for i in range(2)]
                sg = Pool(ctx, scope, uq("f_sg"), 2, [128, 512], F32)
                hr = Pool(ctx, scope, uq("f_hr"), 2, [128, 512], F32)
                hrd = [ctx.dsem(f"hr{i}") for i in range(2)]
                ho = Pool(ctx, scope, uq("f_ho"), 2, [128, 512], F32)
                hod = [ctx.dsem(f"ho{i}") for i in range(2)]
                pg = Pool(ctx, scope, uq("f_pg"), 2, [128, 512], F32, psum=True)
                pu = Pool(ctx, scope, uq("f_pu"), 2, [128, 512], F32, psum=True)
                py = Pool(ctx, scope, uq("f_py"), 2, [128, 512], F32, psum=True)
                pss = Pool(ctx, scope, uq("f_pss"), 2, [128, 512], F32, psum=True)
                for sgl in SGS:
                    norm_phase(scope, src, sgl, lambda k: cv(l, 0 if which == 0 else 2, k),
                               xn=xn, xn_b=xn_b, ps_ss=pss, NP=128)
                    offs = []
                    o = 0
                    for gi in sgl:
                        offs.append(o)
                        o += GROUPS_L[gi][1] - GROUPS_L[gi][0]
                    for f in range(NF):
                        wgt, wgb, wi = wg.next()
                        wut, wub, _ = wu.next()
                        ctx.dma(POOL, wgd[wi], wgt[:], wgu_d[wrow_gu + f], writes=[wgb], max_dma_last_dim=4096)
                        ctx.dma(POOL, wud[wi], wut[:], wgu_d[wrow_gu + NF + f], writes=[wub], max_dma_last_dim=4096)
                        for gi, off in zip(sgl, offs):
                            N = GROUPS_L[gi][1] - GROUPS_L[gi][0]
                            gt, gb, _ = pg.next()
                            ut, ub, _ = pu.next()
                            for k in range(KC):
                                ctx.op(PE, lambda: nc.tensor.matmul(gt[:, :N], wgt[:, k * 128:(k + 1) * 128], xn[:, k, off:off + N],
                                                                    start=(k == 0), stop=(k == KC - 1)),
                                       reads=[wgb, xn_b], writes=[gb], sig=(k == KC - 1))
                            for k in range(KC):
                                ctx.op(PE, lambda: nc.tensor.matmul(ut[:, :N], wut[:, k * 128:(k + 1) * 128], xn[:, k, off:off + N],
                                                                    start=(k == 0), stop=(k == KC - 1)),
                                       reads=[wub, xn_b], writes=[ub], sig=(k == KC - 1))
                            st, sbf, _ = sg.next()
                            ctx.op(ACT, lambda: nc.scalar.activation(out=st[:, :N], in_=gt[:, :N], func=AF.Silu),
                                   reads=[gb], writes=[sbf])
                            ctx.op(DVE, lambda: nc.vector.tensor_tensor(out=A[:, f, off:off + N], in0=st[:, :N], in1=ut[:, :N], op=ALU.mult),
                                   reads=[sbf, ub], writes=[A_b[f]])
                    for dc in range(KC):
                        wdt, wdb, wi = wd.next()
                        ctx.dma(POOL, wdd[wi], wdt[:], wdn_d[wrow_dn + dc], writes=[wdb], max_dma_last_dim=4096)
                        for gi, off in zip(sgl, offs):
                            c0, c1 = GROUPS_L[gi]
                            N = c1 - c0
                            hrt, hrb, hri = hr.next()
                            ctx.dma(SP, hrd[hri], hrt[:, :N], src[dc * 128:(dc + 1) * 128, c0:c1], writes=[hrb])
                            yt, yb, _ = py.next()
                            for f in range(NF):
                                ctx.op(PE, lambda: nc.tensor.matmul(yt[:, :N], wdt[:, f * 128:(f + 1) * 128], A[:, f, off:off + N],
                                                                    start=(f == 0), stop=(f == NF - 1)),
                                       reads=[wdb, A_b[f]], writes=[yb], sig=(f == NF - 1))
                            hot, hob, hoi = ho.next()
                            ctx.op(DVE, lambda: nc.vector.scalar_tensor_tensor(out=hot[:, :N], in0=yt[:, :N], scalar=0.5, in1=hrt[:, :N],
                                                                               op0=ALU.mult, op1=ALU.add),
                                   reads=[yb, hrb], writes=[hob])
                            ctx.dma(SP, hod[hoi], dst[dc * 128:(dc + 1) * 128, c0:c1], hot[:, :N], reads=[hob])
                ctx.barrier()

        _norm_cache = {}

        def norm_phase(scope, src, groups, gain_fn, xn=None, xn_b=None, xoff0=0, out_dst=None, ps_ss=None, NP=256):
            c = getattr(scope, "_norm_tiles", None)
            if c is None:
                c = {}
                c["hst"] = Pool(ctx, scope, uq("hst"), 2, [128, KC, NP], F32)
                c["sq"] = Pool(ctx, scope, uq("nsq"), 2, [128, NP], F32)
                c["lnb"] = Pool(ctx, scope, uq("nln"), 2, [128, NP], F32)
                c["rsb"] = Pool(ctx, scope, uq("nrs"), 2, [128, NP], F32)
                if out_dst is not None:
                    c["ost"] = Pool(ctx, scope, uq("nost"), 3, [128, NP], F32)
                scope._norm_tiles = c
            hst, sq, lnb, rsb = c["hst"], c["sq"], c["lnb"], c["rsb"]
            ost = c.get("ost")
            xoff = xoff0
            for gi in groups:
                c0, c1 = GROUPS_L[gi]
                for p0 in range(c0, c1, NP):
                    p1 = min(c1, p0 + NP)
                    N = p1 - p0
                    ht, hb, hi = hst.next()
                    ctx.dma(SP, g_hst_d[hi], ht[:, :, :N],
                            src[:, p0:p1].rearrange("(k p) n -> p k n", p=128), writes=[hb])
                    st, sbf, _ = ps_ss.next()
                    for k in range(KC):
                        qt, qb, _ = sq.next()
                        ctx.op(ACT, lambda: nc.scalar.activation(out=qt[:, :N], in_=ht[:, k, :N], func=AF.Square),
                               reads=[hb], writes=[qb])
                        ctx.op(PE, lambda: nc.tensor.matmul(st[:, :N], ones_f, qt[:, :N], start=(k == 0), stop=(k == KC - 1)),
                               reads=[qb], writes=[sbf], sig=True)
                    lt, lb, _ = lnb.next()
                    ctx.op(ACT, lambda: nc.scalar.activation(out=lt[:, :N], in_=st[:, :N], func=AF.Ln, bias=EPS, scale=1.0 / D),
                           reads=[sbf], writes=[lb])
                    rt, rb, _ = rsb.next()
                    ctx.op(ACT, lambda: nc.scalar.activation(out=rt[:, :N], in_=lt[:, :N], func=AF.Exp, scale=-0.5),
                           reads=[lb], writes=[rb])
                    for k in range(KC):
                        if out_dst is None:
                            ctx.op(DVE, lambda: nc.vector.scalar_tensor_tensor(
                                out=xn[:, k, xoff:xoff + N], in0=ht[:, k, :N], scalar=gain_fn(k), in1=rt[:, :N],
                                op0=ALU.mult, op1=ALU.mult), reads=[hb, rb], writes=[xn_b])
                        else:
                            ot, ob, oi = ost.next()
                            ctx.op(DVE, lambda: nc.vector.scalar_tensor_tensor(
                                out=ot[:, :N], in0=ht[:, k, :N], scalar=gain_fn(k), in1=rt[:, :N],
                                op0=ALU.mult, op1=ALU.mult), reads=[hb, rb], writes=[ob])
                            ctx.dma(SP, g_ost_d[oi], out_dst[k * 128:(k + 1) * 128, p0 - NM:p1 - NM], ot[:, :N], reads=[ob])
                    xoff += N

        g_hst_d = [ctx.dsem(f"ghst{i}") for i in range(2)]
        g_ost_d = [ctx.dsem(f"gost{i}") for i in range(3)]

        def proj_phase(l):
            with ExitStack() as scope:
                xn = scope.enter_context(nc.sbuf_tensor(uq("p_xn"), [128, KC, TL], BF16))
                xn_b = Buf("p_xn")
                pss = Pool(ctx, scope, uq("p_pss"), 2, [128, 512], F32, psum=True)
                pq = Pool(ctx, scope, uq("p_pq"), 3, [128, 512], F32, psum=True)
                with ExitStack() as nscope:
                    norm_phase(nscope, H, list(range(5)), lambda k: cv(l, 1, k), xn=xn, xn_b=xn_b, ps_ss=pss)
                    ctx.barrier()
                wq = Pool(ctx, scope, uq("p_wq"), 2, [128, D], BF16)
                wqd = [ctx.dsem(f"pwq{i}") for i in range(2)]
                stg = Pool(ctx, scope, uq("p_stg"), 3, [128, 512], BF16)
                stgd = [ctx.dsem(f"pstg{i}") for i in range(3)]
                def load_wq(cc_):
                    wt_, wb_, wi_ = wq.next()
                    ctx.dma(POOL, wqd[wi_], wt_[:], wqk_d[l * 32 + cc_], writes=[wb_], max_dma_last_dim=4096)
                    return wt_, wb_

                wnext = load_wq(0)
                for cc in range(32):
                    wt, wb = wnext
                    is_q = (cc // 8) in (0, 2)
                    qtoks = []
                    for gi in range(5):
                        c0, c1 = GROUPS_L[gi]
                        N = c1 - c0
                        pt, pb, _ = pq.next()
                        for k in range(KC):
                            ctx.op(PE, lambda: nc.tensor.matmul(pt[:, :N], wt[:, k * 128:(k + 1) * 128], xn[:, k, c0:c1],
                                                                start=(k == 0), stop=(k == KC - 1)),
                                   reads=[wb, xn_b], writes=[pb], sig=(k == KC - 1))
                        st, sbf, si = stg.next()
                        scale = (128.0 ** -0.5) if is_q else 1.0
                        if (gi + cc) % 2 == 0:
                            ctx.op(ACT, lambda: nc.scalar.activation(out=st[:, :N], in_=pt[:, :N], func=AF.Copy, scale=scale),
                                   reads=[pb], writes=[sbf])
                        else:
                            ctx.op(DVE, lambda: nc.vector.tensor_scalar(out=st[:, :N], in0=pt[:, :N], scalar1=scale, scalar2=None,
                                                                        op0=ALU.mult), reads=[pb], writes=[sbf])
                        ctx.dma(SP, stgd[si], SQK[cc][:, c0:c1], st[:, :N], reads=[sbf])
                        qtoks.append(ctx.last_tok)
                    if cc + 1 < 32:
                        wnext = load_wq(cc + 1)
                    all_gather(SQK[cc], RQK[cc], qtoks)
                wv = Pool(ctx, scope, uq("p_wv"), 2, [128, KC * 512], BF16)
                wvd = [ctx.dsem(f"pwv{i}") for i in range(2)]
                def load_wv(vg_):
                    wt_, wb_, wi_ = wv.next()
                    ctx.dma(POOL, wvd[wi_], wt_[:], wv_d[l * 4 + vg_], writes=[wb_], max_dma_last_dim=4096)
                    return wt_, wb_

                wvnext = load_wv(0)
                for vg in range(4):
                    wt, wb = wvnext
                    vtoks = [[], [], [], []]
                    for j in range(NBL):
                        t0, t1 = blk(j)
                        kk = t1 - t0
                        pt, pb, _ = pq.next()
                        for k in range(KC):
                            ctx.op(PE, lambda: nc.tensor.matmul(pt[:kk, :512], xn[:, k, t0:t1], wt[:, k * 512:(k + 1) * 512],
                                                                start=(k == 0), stop=(k == KC - 1)),
                                   reads=[wb, xn_b], writes=[pb], sig=(k == KC - 1))
                        st, sbf, si = stg.next()
                        if j % 2 == 0:
                            ctx.op(ACT, lambda: nc.scalar.activation(out=st[:kk, :512], in_=pt[:kk, :512], func=AF.Copy),
                                   reads=[pb], writes=[sbf])
                        else:
                            ctx.op(DVE, lambda: nc.vector.tensor_copy(out=st[:kk, :512], in_=pt[:kk, :512]), reads=[pb], writes=[sbf])
                        q_ = 0 if j <= 4 else (j - 1) // 4
                        ctx.dma(SP, stgd[si], SV[vg][q_][t0 - VQ[q_][0]:t1 - VQ[q_][0], :], st[:kk, :512], reads=[sbf])
                        vtoks[q_].append(ctx.last_tok)
                    if vg + 1 < 4:
                        wvnext = load_wv(vg + 1)
                    for q_ in range(4):
                        all_gather(SV[vg][q_], RV[vg][q_], vtoks[q_])
                wf = scope.enter_context(nc.sbuf_tensor(uq("p_wf"), [128, KC * 8], BF16))
                wfb = Buf("p_wf")
                wfd = ctx.dsem("pwf")
                ctx.dma(POOL, wfd, wf[:], wf_d[l], writes=[wfb])
                ft = Pool(ctx, scope, uq("p_ft"), 2, [128, 8], F32)
                lfl = scope.enter_context(nc.sbuf_tensor(uq("p_lfl"), [128, NBL * 8], F32))
                b_lfl = Buf("lfl")
                ctx.op(DVE, lambda: nc.vector.memset(lfl[:], 0.0), writes=[b_lfl])
                for j in range(NBL):
                    t0, t1 = blk(j)
                    kk = t1 - t0
                    pt, pb, _ = pq.next()
                    for k in range(KC):
                        ctx.op(PE, lambda: nc.tensor.matmul(pt[:kk, :8], xn[:, k, t0:t1], wf[:, k * 8:(k + 1) * 8],
                                                            start=(k == 0), stop=(k == KC - 1)),
                               reads=[wfb, xn_b], writes=[pb], sig=(k == KC - 1))
                    f1, f1b, _ = ft.next()
                    ctx.op(DVE, lambda: nc.vector.tensor_tensor(out=f1[:kk, :], in0=pt[:kk, :8], in1=cv_bf(l)[:kk, :], op=ALU.add),
                           reads=[pb], writes=[f1b])
                    f2, f2b, _ = ft.next()
                    ctx.op(ACT, lambda: nc.scalar.activation(out=f2[:kk, :], in_=f1[:kk, :], func=AF.Exp, scale=-1.0),
                           reads=[f1b], writes=[f2b])
                    f3, f3b, _ = ft.next()
                    ctx.op(ACT, lambda: nc.scalar.activation(out=f3[:kk, :], in_=f2[:kk, :], func=AF.Ln, bias=1.0, scale=1.0),
                           reads=[f2b], writes=[f3b])
                    ctx.op(DVE, lambda: nc.vector.tensor_scalar(out=lfl[:kk, j * 8:(j + 1) * 8], in0=f3[:kk, :], scalar1=-1.0,
                                                                scalar2=None, op0=ALU.mult), reads=[f3b], writes=[b_lfl])
                ctx.dma(SP, ctx.dsem("slf"), SLF[:, :], lfl[:, :], reads=[b_lfl])
                all_gather(SLF, RLF, [ctx.last_tok])
                ctx.barrier()

        def attn_phase(l):
            with ExitStack() as scope:
                bias_t = scope.enter_context(nc.sbuf_tensor(uq("a_bias"), [128, 9 * NB * NHL], F32))
                ch24 = scope.enter_context(nc.sbuf_tensor(uq("a_ch24"), [3 * NHL, T], BF16))
                qT = Pool(ctx, scope, uq("a_qT"), 2, [128, T], BF16)
                kT = Pool(ctx, scope, uq("a_kT"), 2, [128, T], BF16)
                Vh = Pool(ctx, scope, uq("a_Vh"), 2, [128, NB * 128], BF16)
                stgp = Pool(ctx, scope, uq("a_stg"), 6, [128, NB * 128], BF16)
                stgd = [ctx.dsem(f"astg{i}") for i in range(6)]
                for i in range(6):
                    ctx.op(POOL, lambda: nc.gpsimd.memset(stgp.t[i][:, :], 0.0), writes=[stgp.b[i]])

                def load_qk(dst_pool, typ, hl):
                    t, tb, _ = dst_pool.next()
                    cands = []
                    for j in range(2):
                        cc = typ * 8 + 4 * j + hl
                        st, stb, si = stgp.next()
                        ctx.dma(SP, stgd[si], st[:, 0:TL], RQK[cc][0:128, :], writes=[stb])
                        ctx.dma(SP, stgd[si], st[:, TL:T], RQK[cc][128:256, NM:TL], writes=[stb])
                        cands.append((st, stb))

                    def fin():
                        tm, tmb, _ = btmp.next()
                        blend(t[:, 0:T], cands[0][0][:, 0:T], cands[1][0][:, 0:T], tm[:, 0:T], tmb, [cands[0][1], cands[1][1]], [tb])
                    return t, tb, fin

                def load_v(vbase, hl):
                    t, tb, _ = Vh.next()
                    cands = []
                    for j in range(2):
                        col = vbase + (4 * j + hl) * 128
                        st, stb, si = stgp.next()
                        stv = st[:, :].rearrange("p (j c) -> p j c", c=128)
                        vg_, cin = divmod(col, 512)
                        ctx.dma(SP, stgd[si], st[0:NM, 0:128], RV[vg_][0][0:NM, cin:cin + 128], writes=[stb])
                        for s_ in range(2):
                            for q_, (r0, r1) in enumerate(VQ):
                                nr = r1 - r0
                                skip = NM if q_ == 0 else 0
                                gb0 = s_ * 16 + 4 * q_ + 1
                                ctx.dma(SP, stgd[si], stv[:, gb0:gb0 + 4, :],
                                        RV[vg_][q_][s_ * nr + skip:(s_ + 1) * nr, cin:cin + 128].rearrange("(j p) c -> p j c", p=128),
                                        writes=[stb])
                        cands.append((st, stb))

                    def fin():
                        tm, tmb, _ = btmp.next()
                        blend(t[:, :], cands[0][0][:, :], cands[1][0][:, :], tm[:, :], tmb, [cands[0][1], cands[1][1]], [tb])
                    return t, tb, fin

                def load_head(kind, hl):
                    tq = 0 if kind == 0 else 2
                    qt, qb, f1_ = load_qk(qT, tq, hl)
                    kt, kb_, f2_ = load_qk(kT, tq + 1, hl)
                    vt, vb, f3_ = load_v(0 if kind == 0 else 1024, hl)

                    def fin():
                        f1_()
                        f2_()
                        f3_()
                    return (qt, qb, kt, kb_, vt, vb), fin

                heads = [(0, hl) for hl in range(NHL)] + [(1, hl) for hl in range(NHL)]
                nxt, nfin = load_head(*heads[0])
                with ExitStack() as pscope:
                    pm = Pool(ctx, pscope, uq("a_pm"), 2, [128, 512], F32, psum=True)
                    L0 = pscope.enter_context(nc.sbuf_tensor(uq("a_L0"), [128, NBL * 8], F32))
                    L1 = pscope.enter_context(nc.sbuf_tensor(uq("a_L1"), [128, NBL * 8], F32))
                    ltmp = pscope.enter_context(nc.sbuf_tensor(uq("a_ltmp"), [128, NBL * NHL], F32))
                    b_L, b_ltmp = Buf("L01"), Buf("ltmp")
                    dlf = ctx.dsem("rlf")
                    ctx.dma(SP, dlf, L0[:, :], RLF[0:128, :], writes=[b_L])
                    ctx.dma(SP, dlf, L1[:, :], RLF[128:256, :], writes=[b_L])
                    L0v = L0[:, :].rearrange("p (b h) -> p b h", h=8)
                    L1v = L1[:, :].rearrange("p (b h) -> p b h", h=8)
                    lfv = lfseq[:, :].rearrange("p (b h) -> p b h", h=NHL)
                    ltv = ltmp[:, :].rearrange("p (b h) -> p b h", h=NHL)
                    blend(lfv[:, 0:NBL, :], L0v[:, :, 0:NHL], L0v[:, :, NHL:8], ltv[:, 0:NBL, :], b_ltmp, [b_L], [b_lf])
                    blend(lfv[:, NBL:NB, :], L1v[:, 1:NBL, 0:NHL], L1v[:, 1:NBL, NHL:8], ltv[:, 0:NBL - 1, :], b_ltmp, [b_L], [b_lf])
                    ctok = pscope.enter_context(nc.sbuf_tensor(uq("a_ctok"), [128, NB * NHL], F32))
                    tot = pscope.enter_context(nc.sbuf_tensor(uq("a_tot"), [128, NB * NHL], F32))
                    pex = pscope.enter_context(nc.sbuf_tensor(uq("a_pex"), [128, NB * NHL], F32))
                    chm = pscope.enter_context(nc.sbuf_tensor(uq("a_chm"), [NHL, T], F32))
                    r1 = pscope.enter_context(nc.sbuf_tensor(uq("a_r1"), [NHL, T], F32))
                    cb3 = [pscope.enter_context(nc.sbuf_tensor(uq(f"a_cb{i}"), [NHL, T], BF16)) for i in range(3)]
                    rhm = pscope.enter_context(nc.sbuf_tensor(uq("a_rhm"), [NHL, 18], F32))
                    b_ctok, b_tot, b_pex, b_bias, b_chm, b_r1, b_rhm, b_ch3 = (Buf(n) for n in
                                                                               ("ctok", "tot", "pex", "bias", "chm", "r1", "rhm", "ch3"))
                    b_cb3 = [Buf(f"cb{i}") for i in range(3)]
                    wt_, wb_, _ = pm.next()
                    ctx.op(PE, lambda: nc.tensor.matmul(wt_[:, :NB * NHL], triu_f, lfseq[:, :], start=True, stop=True),
                           reads=[b_lf], writes=[wb_])
                    tt_, tb_, _ = pm.next()
                    ctx.op(PE, lambda: nc.tensor.matmul(tt_[:, :NB * NHL], ones_f, lfseq[:, :], start=True, stop=True),
                           reads=[b_lf], writes=[tb_])
                    ctx.op(DVE, lambda: nc.vector.tensor_copy(out=tot[:, :], in_=tt_[:, :NB * NHL]), reads=[tb_], writes=[b_tot])
                    ctx.op(DVE, lambda: nc.vector.memset(pex[:, 0:NHL], 0.0), writes=[b_pex])
                    for j in range(1, NB):
                        ctx.op(DVE, lambda: nc.vector.tensor_tensor(out=pex[:, j * NHL:(j + 1) * NHL], in0=pex[:, (j - 1) * NHL:j * NHL],
                                                                    in1=tot[:, (j - 1) * NHL:j * NHL], op=ALU.add),
                               reads=[b_tot, b_pex], writes=[b_pex])
                    ctx.op(DVE, lambda: nc.vector.tensor_tensor(out=ctok[:, :], in0=wt_[:, :NB * NHL], in1=pex[:, :], op=ALU.add),
                           reads=[wb_, b_pex], writes=[b_ctok])
                    for g in range(9):
                        jb0, jb1 = grp_blocks(g)
                        for kb in range(jb1 + 1):
                            o = (g * NB + kb) * NHL
                            ctx.op(DVE, lambda: nc.vector.tensor_tensor(out=bias_t[:, o:o + NHL], in0=pex[:, jb0 * NHL:(jb0 + 1) * NHL],
                                                                        in1=ctok[:, kb * NHL:(kb + 1) * NHL], op=ALU.subtract),
                                   reads=[b_pex, b_ctok], writes=[b_bias])
                    rt_, rb_, _ = pm.next()
                    for g in range(9):
                        jb0, _ = grp_blocks(g)
                        ctx.op(PE, lambda: nc.tensor.matmul(rt_[0:NHL, 2 * g:2 * g + 2], pex[:, jb0 * NHL:(jb0 + 1) * NHL], ident_f[:, 0:2],
                                                            start=True, stop=True), reads=[b_pex], writes=[rb_])
                    ctx.op(DVE, lambda: nc.vector.tensor_copy(out=rhm[:, 0:18], in_=rt_[0:NHL, 0:18]), reads=[rb_], writes=[b_rhm])
                    for g in range(9):
                        c0, c1 = GROUPS[g]
                        jb0, jb1 = grp_blocks(g)
                        ct_, cbb_, _ = pm.next()
                        for j in range(jb0, jb1 + 1):
                            t0, t1 = blk(j)
                            kk = t1 - t0
                            ctx.op(PE, lambda: nc.tensor.matmul(ct_[0:NHL, t0 - c0:t1 - c0], ctok[:kk, j * NHL:(j + 1) * NHL], ident_f[:kk, :kk],
                                                                start=True, stop=True), reads=[b_ctok], writes=[cbb_])
                        rcol = rhm[:, 2 * g:2 * g + 1]
                        ctx.op(DVE, lambda: nc.vector.tensor_scalar(out=chm[:, c0:c1], in0=ct_[0:NHL, 0:c1 - c0], scalar1=rcol, scalar2=None,
                                                                    op0=ALU.subtract), reads=[cbb_, b_rhm], writes=[b_chm])
                    ctx.op(DVE, lambda: nc.vector.tensor_copy(out=cb3[0][:, :], in_=chm[:, :]), reads=[b_chm], writes=[b_cb3[0]])
                    ctx.op(DVE, lambda: nc.vector.tensor_tensor(out=r1[:, :], in0=chm[:, :], in1=cb3[0][:, :], op=ALU.subtract),
                           reads=[b_chm, b_cb3[0]], writes=[b_r1])
                    ctx.op(DVE, lambda: nc.vector.tensor_copy(out=cb3[1][:, :], in_=r1[:, :]), reads=[b_r1], writes=[b_cb3[1]])
                    ctx.op(DVE, lambda: nc.vector.tensor_tensor(out=chm[:, :], in0=r1[:, :], in1=cb3[1][:, :], op=ALU.subtract),
                           reads=[b_r1, b_cb3[1]], writes=[b_chm])
                    ctx.op(DVE, lambda: nc.vector.tensor_copy(out=cb3[2][:, :], in_=chm[:, :]), reads=[b_chm], writes=[b_cb3[2]])
                    dch = ctx.dsem(f"ch3_{l}")
                    for i in range(3):
                        for h in range(NHL):
                            ctx.dma(SP, dch, ch24[3 * h + i:3 * h + i + 1, :], cb3[i][h:h + 1, :], reads=[b_cb3[i]], writes=[b_ch3])

                    ctx.barrier()
                hd = [ctx.dsem(f"ahd{i}") for i in range(2)]
                ps = Pool(ctx, scope, uq("a_ps"), 3, [128, 512], F32, psum=True)
                po = Pool(ctx, scope, uq("a_po"), 2, [128, 512], F32, psum=True)
                pdn = Pool(ctx, scope, uq("a_pd"), 1, [128, 512], F32, psum=True)
                pT = Pool(ctx, scope, uq("a_pT"), 3, [128, 512], BF16)
                eb = Pool(ctx, scope, uq("a_eb"), 2, [128, 512], F32)
                spb = Pool(ctx, scope, uq("a_sp"), 3, [128, 512], BF16)
                rs = Pool(ctx, scope, uq("a_rs"), 2, [128, 512], BF16)
                usq = Pool(ctx, scope, uq("a_usq"), 2, [128, 512], F32)
                t1p = Pool(ctx, scope, uq("a_t1"), 2, [128, 512], F32)
                t2p = Pool(ctx, scope, uq("a_t2"), 2, [128, 512], F32)
                rrp = Pool(ctx, scope, uq("a_rr"), 2, [128, 512], F32)
                ost = Pool(ctx, scope, uq("a_ost"), 2, [128, 512], BF16)
                ostd = [ctx.dsem(f"aost{i}") for i in range(2)]

                btmp = Pool(ctx, scope, uq("a_btmp"), 1, [128, NB * 128], BF16)
                otoks = [[], []]

                def epilogue(h_feat, g, ot, ob, dt_, db_, fox):
                    c0, c1 = GROUPS[g]
                    N = c1 - c0
                    ut, ub, _ = usq.next()
                    ctx.op(ACT, lambda: nc.scalar.activation(out=ut[:, :N], in_=ot[:, :N], func=AF.Square), reads=[ob], writes=[ub])
                    st, sbf, _ = ps.next()
                    ctx.op(PE, lambda: nc.tensor.matmul(st[:, :N], ones_f, ut[:, :N], start=True, stop=True), reads=[ub], writes=[sbf])
                    lt, lb, _ = t2p.next()
                    if fox:
                        t1t, t1b, _ = t1p.next()
                        ctx.op(ACT, lambda: nc.scalar.activation(out=t1t[:, :N], in_=dt_[:, :N], func=AF.Square, scale=EPS ** 0.5),
                               reads=[db_], writes=[t1b])
                        ctx.op(DVE, lambda: nc.vector.scalar_tensor_tensor(out=lt[:, :N], in0=st[:, :N], scalar=1.0 / 128, in1=t1t[:, :N],
                                                                           op0=ALU.mult, op1=ALU.add), reads=[sbf, t1b], writes=[lb])
                        l2, l2b, _ = t2p.next()
                        ctx.op(ACT, lambda: nc.scalar.activation(out=l2[:, :N], in_=lt[:, :N], func=AF.Ln), reads=[lb], writes=[l2b])
                    else:
                        l2, l2b = lt, lb
                        ctx.op(ACT, lambda: nc.scalar.activation(out=l2[:, :N], in_=st[:, :N], func=AF.Ln, bias=EPS, scale=1.0 / 128),
                               reads=[sbf], writes=[l2b])
                    rt2, rb2, _ = rrp.next()
                    ctx.op(ACT, lambda: nc.scalar.activation(out=rt2[:, :N], in_=l2[:, :N], func=AF.Exp, scale=-0.5), reads=[l2b], writes=[rb2])
                    o_t, o_b, oi = ost.next()
                    ctx.op(DVE, lambda: nc.vector.scalar_tensor_tensor(out=o_t[:, :N], in0=ot[:, :N], scalar=cv(l, 3, h_feat), in1=rt2[:, :N],
                                                                       op0=ALU.mult, op1=ALU.mult), reads=[ob, rb2], writes=[o_b])
                    for i2 in range(2):
                        ctx.dma(SP, ostd[oi], SO[h_feat][i2][:, c0:c1], o_t[i2 * 64:(i2 + 1) * 64, :N], reads=[o_b])
                        otoks[i2].append(ctx.last_tok)
                    if g == 8:
                        for i2 in range(2):
                            all_gather(SO[h_feat][i2], RO[h_feat][i2], otoks[0] + otoks[1])
                        otoks[0].clear()
                        otoks[1].clear()

                def fox_head(h, hd_, mid_cb):
                    qt, qb, kt, kbb, vt, vb = hd_
                    for g in range(9):
                        if g == 5:
                            mid_cb()
                        c0, c1 = GROUPS[g]
                        N = c1 - c0
                        jb0, jb1 = grp_blocks(g)
                        ot, ob, _ = po.next()
                        dt_, db_, _ = pdn.next()
                        tiles = []
                        pend = None
                        for kb in range(jb1 + 2):
                            cur = None
                            if kb <= jb1:
                                ks0, ks1 = blk(kb)
                                kk = ks1 - ks0
                                diag = kb >= jb0
                                n0 = ks0 if diag else c0
                                NA = c1 - n0
                                off = n0 - c0
                                st, sbf, _ = ps.next()
                                ctx.op(PE, lambda: nc.tensor.matmul(st[:kk, :NA], kt[:, ks0:ks1], qt[:, n0:c1], start=True, stop=False),
                                       reads=[kbb, qb], writes=[sbf], sig=False)
                                ctx.op(PE, lambda: nc.tensor.matmul(st[:kk, :NA], sel24[0:3 * NHL, h * 128:h * 128 + kk], ch24[0:3 * NHL, n0:c1],
                                                                    start=False, stop=(not diag)),
                                       reads=[b_ch3], writes=[sbf], sig=(not diag))
                                if diag:
                                    ctx.op(PE, lambda: nc.tensor.matmul(st[:kk, 0:kk], ident_b[:kk, :kk], maskF_b[:kk, :kk],
                                                                        start=False, stop=True), writes=[sbf], sig=True)
                                p_t, p_b, _ = pT.next()
                                bo = (g * NB + kb) * NHL + h
                                ctx.op(ACT, lambda: nc.scalar.activation(out=p_t[:kk, :NA], in_=st[:kk, :NA], func=AF.Exp,
                                                                         bias=bias_t[:kk, bo:bo + 1], scale=1.0),
                                       reads=[sbf, b_bias], writes=[p_b])
                                cur = (kb, kk, NA, off, p_t, p_b)
                            if pend is not None:
                                kb2, kk2, NA2, off2, p2, p2b = pend
                                ctx.op(PE, lambda: nc.tensor.matmul(ot[:, off2:off2 + NA2], vt[:kk2, kb2 * 128:(kb2 + 1) * 128], p2[:kk2, :NA2],
                                                                    start=(kb2 == 0), stop=(kb2 == jb1)),
                                       reads=[vb, p2b], writes=[ob], sig=False)
                                ctx.op(PE, lambda: nc.tensor.matmul(dt_[:, off2:off2 + NA2], ones_b[:kk2, :], p2[:kk2, :NA2],
                                                                    start=(kb2 == 0), stop=(kb2 == jb1)),
                                       reads=[p2b], writes=[db_], sig=True)
                            pend = cur
                        epilogue(h, g, ot, ob, dt_, db_, True)

                def sb_head(h, hd_, mid_cb):
                    qt, qb, kt, kbb, vt, vb = hd_
                    for g in range(9):
                        if g == 5:
                            mid_cb()
                        c0, c1 = GROUPS[g]
                        N = c1 - c0
                        jb0, jb1 = grp_blocks(g)
                        ot, ob, _ = po.next()
                        ctx.op(PE, lambda: nc.tensor.matmul(ot[:, :N], zeros_b, qt[:, c0:c1], start=True, stop=False),
                               reads=[qb], writes=[ob], sig=False)
                        rs0, rs0b, _ = rs.next()
                        rs1, rs1b, _ = rs.next()
                        ctx.op(POOL, lambda: nc.gpsimd.memset(rs0[:, :], 0.0), writes=[rs0b])
                        ctx.op(POOL, lambda: nc.gpsimd.memset(rs1[:, :], 0.0), writes=[rs1b])
                        rcur, rcurb, rnxt, rnxtb = rs0, rs0b, rs1, rs1b
                        order = list(range(jb1, -1, -1))
                        n = len(order)
                        st1 = [None] * n
                        st2 = [None] * n
                        for step in range(n + 2):
                            if step < n:
                                kb = order[step]
                                ks0, ks1 = blk(kb)
                                kk = ks1 - ks0
                                diag = kb >= jb0
                                n0 = ks0 if diag else c0
                                NA = c1 - n0
                                off = n0 - c0
                                zt, zb, _ = ps.next()
                                ctx.op(PE, lambda: nc.tensor.matmul(zt[:kk, :NA], kt[:, ks0:ks1], qt[:, n0:c1], start=True, stop=False),
                                       reads=[kbb, qb], writes=[zb], sig=(not diag))
                                if diag:
                                    ctx.op(PE, lambda: nc.tensor.matmul(zt[:kk, 0:kk], ident_b[:kk, :kk], maskS_b[:kk, :kk],
                                                                        start=False, stop=False), writes=[zb], sig=True)
                                et, ebb, _ = eb.next()
                                ctx.op(ACT, lambda: nc.scalar.activation(out=et[:kk, :NA], in_=zt[:kk, :NA], func=AF.Exp),
                                       reads=[zb], writes=[ebb])
                                s_t, s_b, _ = spb.next()
                                ctx.op(ACT, lambda: nc.scalar.activation(out=s_t[:kk, :NA], in_=et[:kk, :NA], func=AF.Ln, bias=1.0, scale=1.0),
                                       reads=[ebb], writes=[s_b])
                                st1[step] = (kb, kk, NA, off, zt, zb, s_t, s_b)
                            if 1 <= step <= n:
                                i = step - 1
                                kb, kk, NA, off, zt, zb, s_t, s_b = st1[i]
                                first = (i == 0)
                                ctx.op(PE, lambda: nc.tensor.matmul(zt[:kk, :NA], negtri_b[:kk, :kk], s_t[:kk, :NA], start=False, stop=first),
                                       reads=[s_b], writes=[zb], sig=first)
                                if not first:
                                    ctx.op(PE, lambda: nc.tensor.matmul(zt[:kk, :NA], negones_b[:, :kk], rcur[:, off:off + NA],
                                                                        start=False, stop=True), reads=[rcurb], writes=[zb], sig=True)
                                a_t, a_b, _ = pT.next()
                                ctx.op(ACT, lambda: nc.scalar.activation(out=a_t[:kk, :NA], in_=zt[:kk, :NA], func=AF.Exp),
                                       reads=[zb], writes=[a_b])
                                if kb > 0:
                                    ctx.op(POOL, lambda: nc.gpsimd.tensor_tensor(out=rnxt[:, off:off + NA], in0=rcur[:, off:off + NA],
                                                                                 in1=s_t[:, :NA], op=ALU.add),
                                           reads=[rcurb, s_b], writes=[rnxtb])
                                    rcur, rcurb, rnxt, rnxtb = rnxt, rnxtb, rcur, rcurb
                                st2[i] = (kb, kk, NA, off, a_t, a_b)
                            if step >= 2:
                                i = step - 2
                                kb, kk, NA, off, a_t, a_b = st2[i]
                                ctx.op(PE, lambda: nc.tensor.matmul(ot[:, off:off + NA], vt[:kk, kb * 128:(kb + 1) * 128], a_t[:kk, :NA],
                                                                    start=False, stop=(i == n - 1)),
                                       reads=[vb, a_b], writes=[ob], sig=(i == n - 1))
                        epilogue(NHL + h, g, ot, ob, None, None, False)

                nfin()
                for i, (kind, hl) in enumerate(heads):
                    cur = nxt
                    if i + 1 < len(heads):
                        nxt, nfin = load_head(*heads[i + 1])
                    else:
                        nfin = lambda: None
                    if kind == 0:
                        fox_head(hl, cur, nfin)
                    else:
                        sb_head(hl, cur, nfin)
                ctx.barrier()

        def out_phase(l):
            with ExitStack() as scope:
                on = scope.enter_context(nc.sbuf_tensor(uq("o_on"), [128, KC, TL], BF16))
                on_b = Buf("o_on")
                ostg = Pool(ctx, scope, uq("o_stg"), 2, [128, T], BF16)
                ostgd = [ctx.dsem(f"ostg{i}") for i in range(2)]
                otmp = Pool(ctx, scope, uq("o_tmp"), 2, [128, SEQ // 2], BF16)
                for k in range(KC):
                    s_, hc = divmod(k, 8)
                    st, stb, si = ostg.next()
                    for i2 in range(2):
                        ctx.dma(SP, ostgd[si], st[i2 * 64:(i2 + 1) * 64, :], RO[hc][i2][s_ * 64:(s_ + 1) * 64, :], writes=[stb])
                    ctx.op(DVE, lambda: nc.vector.tensor_copy(out=on[:, k, 0:NM], in_=st[:, 0:NM]), reads=[stb], writes=[on_b])
                    tm, tmb, _ = otmp.next()
                    blend(on[:, k, NM:TL], st[:, NM:TL], st[:, TL:T], tm[:, :], tmb, [stb], [on_b])
                wo = Pool(ctx, scope, uq("o_wo"), 2, [128, D], BF16)
                wod = [ctx.dsem(f"owo{i}") for i in range(2)]
                hr = Pool(ctx, scope, uq("o_hr"), 3, [128, 512], F32)
                hrd = [ctx.dsem(f"ohr{i}") for i in range(3)]
                ho = Pool(ctx, scope, uq("o_ho"), 3, [128, 512], F32)
                hod = [ctx.dsem(f"oho{i}") for i in range(3)]
                py = Pool(ctx, scope, uq("o_py"), 3, [128, 512], F32, psum=True)
                for dc in range(KC):
                    wt, wb, wi = wo.next()
                    ctx.dma(POOL, wod[wi], wt[:], wo_d[l * KC + dc], writes=[wb], max_dma_last_dim=4096)
                    for gi in range(5):
                        c0, c1 = GROUPS_L[gi]
                        N = c1 - c0
                        hrt, hrb, hri = hr.next()
                        ctx.dma(SP, hrd[hri], hrt[:, :N], H[dc * 128:(dc + 1) * 128, c0:c1], writes=[hrb])
                        yt, yb, _ = py.next()
                        for k in range(KC):
                            ctx.op(PE, lambda: nc.tensor.matmul(yt[:, :N], wt[:, k * 128:(k + 1) * 128], on[:, k, c0:c1],
                                                                start=(k == 0), stop=(k == KC - 1)),
                                   reads=[wb, on_b], writes=[yb], sig=(k == KC - 1))
                        hot, hob, hoi = ho.next()
                        ctx.op(DVE, lambda: nc.vector.tensor_tensor(out=hot[:, :N], in0=yt[:, :N], in1=hrt[:, :N], op=ALU.add),
                               reads=[yb, hrb], writes=[hob])
                        ctx.dma(SP, hod[hoi], H[dc * 128:(dc + 1) * 128, c0:c1], hot[:, :N], reads=[hob])
                ctx.barrier()

        def final_phase():
            with ExitStack() as scope:
                pss = Pool(ctx, scope, uq("fn_pss"), 2, [128, 512], F32, psum=True)
                norm_phase(scope, H, list(range(1, 5)), cv_final, out_dst=out_d, ps_ss=pss)
                ctx.barrier()

        for l in range(DEPTH):
            ffn_phase(l, 0, h_in if l == 0 else H, H)
            proj_phase(l)
            attn_phase(l)
            out_phase(l)
            ffn_phase(l, 1, H, H)
        final_phase()
    return nc


def _lhsT_tiles(W, ncol_chunks):
    kc = W.shape[0] // 128
    return np.ascontiguousarray(W.reshape(kc, 128, ncol_chunks, 128).transpose(2, 1, 0, 3).reshape(ncol_chunks, 128, kc * 128))


def _col(v):
    return np.ascontiguousarray(v.reshape(-1, 128).T)


_PROG = None


def kernel(x, meta_tokens, ffn1_norm, ffn1_w_gate, ffn1_w_up, ffn1_w_down, mix_norm, w_in, b_forget, g_fox, g_sb,
           w_out, ffn2_norm, ffn2_w_gate, ffn2_w_up, ffn2_w_down, final_norm):
    global _PROG
    f32 = np.float32
    x = np.asarray(x, f32)
    wgu = []
    wdn = []
    wqk = []
    wv = []
    wf = []
    wo = []
    cvecs = [[], []]
    for l in range(DEPTH):
        for (g_, u_, d_) in ((ffn1_w_gate, ffn1_w_up, ffn1_w_down), (ffn2_w_gate, ffn2_w_up, ffn2_w_down)):
            wgu.append(_lhsT_tiles(np.asarray(g_[l], f32), NF))
            wgu.append(_lhsT_tiles(np.asarray(u_[l], f32), NF))
            wdn.append(_lhsT_tiles(np.asarray(d_[l], f32), KC))
        wi = np.asarray(w_in[l], f32)
        qk_cols = np.concatenate([wi[:, 0:1024], wi[:, 1024:2048], wi[:, 3072:4096], wi[:, 4096:5120]], axis=1)
        wqk.append(_lhsT_tiles(qk_cols, 32))
        v_cols = np.concatenate([wi[:, 2048:3072], wi[:, 5120:6144]], axis=1)
        wv.append(np.ascontiguousarray(v_cols.reshape(KC, 128, 4, 512).transpose(2, 1, 0, 3).reshape(4, 128, KC * 512)))
        wf.append(np.ascontiguousarray(wi[:, 6144:6152].reshape(KC, 128, 8).transpose(1, 0, 2).reshape(128, KC * 8)))
        wperm = [0, 1, 2, 3, 8, 9, 10, 11, 4, 5, 6, 7, 12, 13, 14, 15]
        wo.append(_lhsT_tiles(np.ascontiguousarray(np.asarray(w_out[l], f32).reshape(KC, 128, D)[wperm].reshape(D, D)), KC))
        gf = _col(np.asarray(g_fox[l], f32))
        gs = _col(np.asarray(g_sb[l], f32))
        for r in range(2):
            cv_ = cvecs[r]
            cv_.append(_col(np.asarray(ffn1_norm[l], f32)))
            cv_.append(_col(np.asarray(mix_norm[l], f32)))
            cv_.append(_col(np.asarray(ffn2_norm[l], f32)))
            cv_.append(np.concatenate([gf[:, 4 * r:4 * r + 4], gs[:, 4 * r:4 * r + 4], np.zeros((128, 8), f32)], axis=1))
            cv_.append(np.ascontiguousarray(np.broadcast_to(np.asarray(b_forget[l], f32)[None, :], (128, 8))))
    for r in range(2):
        cvecs[r].append(_col(np.asarray(final_norm, f32)))
        cvecs[r] = np.ascontiguousarray(np.concatenate(cvecs[r], axis=1))
    wgu = np.concatenate(wgu, axis=0)
    wdn = np.concatenate(wdn, axis=0)
    wqk = np.concatenate(wqk, axis=0)
    wv = np.concatenate(wv, axis=0)
    wf = np.stack(wf, axis=0)
    wo = np.concatenate(wo, axis=0)
    p = np.arange(128)
    ones = np.ones((128, 128), f32)
    triu = (p[:, None] <= p[None, :]).astype(f32)
    ident = np.eye(128, dtype=f32)
    cf32 = np.ascontiguousarray(np.concatenate([ones, triu, ident], axis=1))
    negtri = -(p[:, None] >= p[None, :]).astype(f32)
    maskF = np.where(p[:, None] <= p[None, :], 0.0, NEG).astype(f32)
    maskS = np.where(p[:, None] < p[None, :], 0.0, NEG).astype(f32)
    sel = np.zeros((128, 8 * 128), f32)
    for hh in range(8):
        sel[3 * hh:3 * hh + 3, hh * 128:(hh + 1) * 128] = 1.0
    cbf = np.ascontiguousarray(np.concatenate([ones, -ones, negtri, ident, maskF, maskS, np.zeros((128, 128), f32), sel], axis=1))
    meta = np.asarray(meta_tokens, f32)
    if _PROG is None:
        _PROG = build_program()
    nc = _PROG
    in_maps = []
    for c in range(8):
        b, r = divmod(c, 2)
        hT = np.ascontiguousarray(np.concatenate([meta, x[b, r * (SEQ // 2):(r + 1) * (SEQ // 2)]], axis=0).T)
        rsel = np.ascontiguousarray(np.broadcast_to(np.array([1.0 - r, float(r)], f32)[None, :], (128, 2)))
        in_maps.append({"h_in": hT, "wgu": wgu, "wdn": wdn, "wqk": wqk, "wv": wv, "wf": wf, "wo": wo,
                        "cvec": cvecs[r], "cf32": cf32, "cbf": cbf, "rsel": rsel})
    res = run_bass_kernel_spmd(nc, in_maps, core_ids=list(range(8)))
    out = np.empty((4, SEQ, D), f32)
    for c in range(8):
        b, r = divmod(c, 2)
        out[b, r * (SEQ // 2):(r + 1) * (SEQ // 2), :] = res.results[c]["outT"].T
    return out
```
